# Optimizing a Trainium2 kernel written in Bass

```python
import jax, jax.numpy as jnp
from jax import lax
import numpy as np

D_MODEL = 2048
BATCH = 4
SEQ = 2048
DEPTH = 1
DEC_BATCH = 32
DEC_SEQ = 8
PAST_LEN = 16384
PAGE_SIZE = 128

HEAD_DIM = 64
ATTN_WIDTH = D_MODEL // 2
N_HEADS = ATTN_WIDTH // HEAD_DIM
CONV_CH = D_MODEL - ATTN_WIDTH
MIX_WIDTH = ATTN_WIDTH + CONV_CH
IN_WIDTH = 3 * ATTN_WIDTH + 2 * CONV_CH
CONV_WIDTH = 31
CONV_STATE = CONV_WIDTH - 1
DILATED = ((128, 1), (512, 4), (2048, 16))
MAX_WINDOW = 2048
Q_BLOCK = 128
D_FF = 5632
ROPE_THETA = 10000.0
EPS = 1e-6
FFN_RES = 0.5

kernel_name = "hymba_dilated_attn_conformer_conv_macaron"


def rmsnorm(x, g):
    xf = x.astype(jnp.float32)
    y = xf * lax.rsqrt(jnp.mean(xf * xf, axis=-1, keepdims=True) + EPS)
    return (y * g.astype(jnp.float32)).astype(x.dtype)


def layernorm(x, g, b):
    xf = x.astype(jnp.float32)
    mu = jnp.mean(xf, axis=-1, keepdims=True)
    xc = xf - mu
    y = xc * lax.rsqrt(jnp.mean(xc * xc, axis=-1, keepdims=True) + EPS)
    return (y * g.astype(jnp.float32) + b.astype(jnp.float32)).astype(x.dtype)


def rope(x, pos):
    half = HEAD_DIM // 2
    inv = ROPE_THETA ** (-jnp.arange(half, dtype=jnp.float32) / half)
    ang = pos.astype(jnp.float32)[:, None] * inv[None, :]
    cos = jnp.cos(ang)[:, None, :]
    sin = jnp.sin(ang)[:, None, :]
    x1 = x[..., :half].astype(jnp.float32)
    x2 = x[..., half:].astype(jnp.float32)
    out = jnp.concatenate([x1 * cos - x2 * sin, x2 * cos + x1 * sin], axis=-1)
    return out.astype(x.dtype)


def ffn_half(x, g, wg, wu, wd):
    h = rmsnorm(x, g)
    return x + FFN_RES * ((jax.nn.silu(h @ wg) * (h @ wu)) @ wd)


def mixer_inputs(h, w_in, g_q, g_k, pos):
    z = h @ w_in
    q, k, v, c = jnp.split(z, [ATTN_WIDTH, 2 * ATTN_WIDTH, 3 * ATTN_WIDTH], axis=-1)
    shp = h.shape[:-1] + (N_HEADS, HEAD_DIM)
    q = rope(rmsnorm(q.reshape(shp), g_q), pos)
    k = rope(rmsnorm(k.reshape(shp), g_k), pos)
    v = v.reshape(shp)
    a, b = jnp.split(c, 2, axis=-1)
    u = a * jax.nn.sigmoid(b)
    return q, k, v, u


def dilated_prompt(q, k, v, d, span):
    B, S, H, Dh = q.shape
    L = S // d
    nb = -(-L // Q_BLOCK)
    Lp = nb * Q_BLOCK

    def by_residue(x):
        return x.reshape(B, L, d, H, Dh).transpose(0, 2, 1, 3, 4)

    qr, kr, vr = by_residue(q), by_residue(k), by_residue(v)
    kpad = ((0, 0), (0, 0), (Q_BLOCK, Lp - L), (0, 0), (0, 0))
    kb = jnp.pad(kr, kpad).reshape(B, d, nb + 1, Q_BLOCK, H, Dh)
    vb = jnp.pad(vr, kpad).reshape(B, d, nb + 1, Q_BLOCK, H, Dh)
    qb = jnp.pad(qr, ((0, 0), (0, 0), (0, Lp - L), (0, 0), (0, 0))).reshape(B, d, nb, Q_BLOCK, H, Dh)
    kband = jnp.concatenate([kb[:, :, :-1], kb[:, :, 1:]], axis=3)
    vband = jnp.concatenate([vb[:, :, :-1], vb[:, :, 1:]], axis=3)
    s = jnp.einsum('brnqhd,brnkhd->brnhqk', qb, kband,
                   preferred_element_type=jnp.float32) * (HEAD_DIM ** -0.5)
    qi = jnp.arange(Q_BLOCK)[:, None]
    ki = jnp.arange(2 * Q_BLOCK)[None, :]
    rel = qi + Q_BLOCK - ki
    key_m = jnp.arange(nb)[:, None, None] * Q_BLOCK - Q_BLOCK + ki
    mask = (rel >= 0) & (rel <= span) & (key_m >= 0)
    s = jnp.where(mask[:, None], s, -jnp.inf)
    lse = jax.nn.logsumexp(s, axis=-1)
    p = jnp.exp(s - lse[..., None])
    o = jnp.einsum('brnhqk,brnkhd->brnqhd', p, vband.astype(jnp.float32))
    o = o.reshape(B, d, Lp, H, Dh)[:, :, :L].transpose(0, 2, 1, 3, 4).reshape(B, S, H, Dh)
    lse = lse.transpose(0, 1, 2, 4, 3).reshape(B, d, Lp, H)[:, :, :L]
    lse = lse.transpose(0, 2, 1, 3).reshape(B, S, H)
    return o, lse


def dilated_sample(q, k_all, v_all, buf_len, d, span):
    T = q.shape[1]
    j = jnp.arange(span + 1)
    idx = buf_len + jnp.arange(T)[:, None] - d * j[None, :]
    valid = idx >= 0
    idxc = jnp.maximum(idx, 0)
    kg = k_all[:, idxc]
    vg = v_all[:, idxc]
    s = jnp.einsum('bthd,btjhd->bthj', q, kg,
                   preferred_element_type=jnp.float32) * (HEAD_DIM ** -0.5)
    s = jnp.where(valid[:, None, :], s, -jnp.inf)
    lse = jax.nn.logsumexp(s, axis=-1)
    p = jnp.exp(s - lse[..., None])
    o = jnp.einsum('bthj,btjhd->bthd', p, vg.astype(jnp.float32))
    return o, lse


def dilated_mixture(parts):
    o = jnp.stack([p[0] for p in parts])
    lse = jnp.stack([p[1] for p in parts])
    w = jax.nn.softmax(lse, axis=0)
    return jnp.einsum('pbsh,pbshd->bshd', w, o)


def conv_branch(u_ext, dw_w, dw_b, ln_g, ln_b):
    y = lax.conv_general_dilated(u_ext, dw_w.astype(u_ext.dtype)[:, None, :],
                                 window_strides=(1,), padding='VALID',
                                 dimension_numbers=('NWC', 'WIO', 'NWC'),
                                 feature_group_count=CONV_CH)
    return jax.nn.silu(layernorm(y + dw_b, ln_g, ln_b))


def mix_out(attn, conv, w_out):
    B, S = conv.shape[:2]
    cat = jnp.concatenate([attn.astype(conv.dtype).reshape(B, S, ATTN_WIDTH), conv], axis=-1)
    return cat @ w_out


def setup_inputs(seed: int = 0) -> dict:
    key = jax.random.key(seed)
    ks = jax.random.split(key, 24)
    f32 = jnp.float32
    kv_buf = min(MAX_WINDOW, PAST_LEN)
    nrm = lambda k, shp, sc: jax.random.normal(k, shp, f32) * sc
    gain = lambda k, shp: 1.0 + 0.02 * jax.random.normal(k, shp, f32)
    return {
        "x_prompt": nrm(ks[0], (BATCH, SEQ, D_MODEL), 1.0),
        "x_sample": nrm(ks[1], (DEC_BATCH, DEC_SEQ, D_MODEL), 1.0),
        "cache_k": nrm(ks[2], (DEPTH, DEC_BATCH, kv_buf, N_HEADS, HEAD_DIM), 1.0),
        "cache_v": nrm(ks[3], (DEPTH, DEC_BATCH, kv_buf, N_HEADS, HEAD_DIM), 1.0),
        "state_conv": nrm(ks[4], (DEPTH, DEC_BATCH, CONV_STATE, CONV_CH), 0.5),
        "ln_ffn1": gain(ks[5], (DEPTH, D_MODEL)),
        "ffn1_w_gate": nrm(ks[6], (DEPTH, D_MODEL, D_FF), D_MODEL ** -0.5),
        "ffn1_w_up": nrm(ks[7], (DEPTH, D_MODEL, D_FF), D_MODEL ** -0.5),
        "ffn1_w_down": nrm(ks[8], (DEPTH, D_FF, D_MODEL), D_FF ** -0.5),
        "ln_mix": gain(ks[9], (DEPTH, D_MODEL)),
        "w_in": nrm(ks[10], (DEPTH, D_MODEL, IN_WIDTH), D_MODEL ** -0.5),
        "q_norm": gain(ks[11], (DEPTH, HEAD_DIM)),
        "k_norm": gain(ks[12], (DEPTH, HEAD_DIM)),
        "conv_dw_w": nrm(ks[13], (DEPTH, CONV_WIDTH, CONV_CH), CONV_WIDTH ** -0.5),
        "conv_dw_b": nrm(ks[14], (DEPTH, CONV_CH), 0.02),
        "conv_ln_g": gain(ks[15], (DEPTH, CONV_CH)),
        "conv_ln_b": nrm(ks[16], (DEPTH, CONV_CH), 0.02),
        "w_out": nrm(ks[17], (DEPTH, MIX_WIDTH, D_MODEL), MIX_WIDTH ** -0.5),
        "ln_ffn2": gain(ks[18], (DEPTH, D_MODEL)),
        "ffn2_w_gate": nrm(ks[19], (DEPTH, D_MODEL, D_FF), D_MODEL ** -0.5),
        "ffn2_w_up": nrm(ks[20], (DEPTH, D_MODEL, D_FF), D_MODEL ** -0.5),
        "ffn2_w_down": nrm(ks[21], (DEPTH, D_FF, D_MODEL), D_FF ** -0.5),
    }


def reference(x_prompt, x_sample, cache_k, cache_v, state_conv, ln_ffn1, ffn1_w_gate,
              ffn1_w_up, ffn1_w_down, ln_mix, w_in, q_norm, k_norm, conv_dw_w, conv_dw_b,
              conv_ln_g, conv_ln_b, w_out, ln_ffn2, ffn2_w_gate, ffn2_w_up, ffn2_w_down):
    S = x_prompt.shape[1]
    T = x_sample.shape[1]
    buf_len = cache_k.shape[2]
    pos_p = jnp.arange(S)
    pos_s = PAST_LEN + jnp.arange(T)
    hp, hs = x_prompt, x_sample
    kp_l, vp_l, cp_l, ks_l, vs_l, cs_l = [], [], [], [], [], []
    for l in range(DEPTH):
        hp = ffn_half(hp, ln_ffn1[l], ffn1_w_gate[l], ffn1_w_up[l], ffn1_w_down[l])
        hs = ffn_half(hs, ln_ffn1[l], ffn1_w_gate[l], ffn1_w_up[l], ffn1_w_down[l])

        q, k, v, u = mixer_inputs(rmsnorm(hp, ln_mix[l]), w_in[l], q_norm[l], k_norm[l], pos_p)
        attn = dilated_mixture([dilated_prompt(q, k, v, d, w // d) for (w, d) in DILATED])
        u_ext = jnp.pad(u, ((0, 0), (CONV_STATE, 0), (0, 0)))
        conv = conv_branch(u_ext, conv_dw_w[l], conv_dw_b[l], conv_ln_g[l], conv_ln_b[l])
        hp = hp + mix_out(attn, conv, w_out[l])
        kp_l.append(k[:, -min(MAX_WINDOW, S):])
        vp_l.append(v[:, -min(MAX_WINDOW, S):])
        cp_l.append(u_ext[:, -CONV_STATE:])

        q, k, v, u = mixer_inputs(rmsnorm(hs, ln_mix[l]), w_in[l], q_norm[l], k_norm[l], pos_s)
        k_all = jnp.concatenate([cache_k[l].astype(k.dtype), k], axis=1)
        v_all = jnp.concatenate([cache_v[l].astype(v.dtype), v], axis=1)
        attn = dilated_mixture([dilated_sample(q, k_all, v_all, buf_len, d, w // d)
                                for (w, d) in DILATED])
        u_ext = jnp.concatenate([state_conv[l].astype(u.dtype), u], axis=1)
        conv = conv_branch(u_ext, conv_dw_w[l], conv_dw_b[l], conv_ln_g[l], conv_ln_b[l])
        hs = hs + mix_out(attn, conv, w_out[l])
        ks_l.append(k)
        vs_l.append(v)
        cs_l.append(u_ext[:, -CONV_STATE:])

        hp = ffn_half(hp, ln_ffn2[l], ffn2_w_gate[l], ffn2_w_up[l], ffn2_w_down[l])
        hs = ffn_half(hs, ln_ffn2[l], ffn2_w_gate[l], ffn2_w_up[l], ffn2_w_down[l])
    return (hp, hs, jnp.stack(kp_l), jnp.stack(vp_l), jnp.stack(cp_l),
            jnp.stack(ks_l), jnp.stack(vs_l), jnp.stack(cs_l))
```

```python
import contextlib
import numpy as np
import ml_dtypes
import concourse.bass as bass
import concourse.mybir as mybir
from concourse.bass_utils import run_bass_kernel_spmd

F32 = mybir.dt.float32
BF16 = mybir.dt.bfloat16
ALU = mybir.AluOpType
AF = mybir.ActivationFunctionType
AX = mybir.AxisListType

ENGS = ("pe", "act", "dve", "pool", "sp")

D = 2048
DFF = 5632
NT = 9
NTOK = 1056
EPS = 1e-6
TILES = [(t * 128, 128) for t in range(8)] + [(1024, 32)]
GROUPS = [(0, 512, (0, 1, 2, 3)), (512, 512, (4, 5, 6, 7)), (1024, 32, (8,))]
NSLOT = 4
SLOT_BYTES = 8192


class Op:
    __slots__ = ("eng", "fn", "reads", "writes", "dsem", "pos", "deps", "sig", "cnt", "waits",
                 "dval", "dinc", "name")


class Prog:
    def __init__(self, nc):
        self.nc = nc
        self.ops = []
        self.last_w = {}
        self.readers = {}
        self.dsem_last = {}
        self.dsem_val = {}
        self.last_eng = {}
        self.fence_op = None

    def op(self, eng, fn, reads=(), writes=(), dsem=None, dinc=16, nofence=False, name=""):
        o = Op()
        o.eng, o.fn, o.reads, o.writes, o.dsem, o.name = eng, fn, tuple(reads), tuple(writes), dsem, name
        o.sig, o.cnt, o.waits, o.dval, o.dinc = False, 0, [], 0, dinc
        deps = set()
        for k in o.reads:
            w = self.last_w.get(k)
            if w is not None:
                deps.add(w)
        for k in o.writes:
            w = self.last_w.get(k)
            if w is not None:
                deps.add(w)
            for r in self.readers.get(k, ()):
                deps.add(r)
        if self.fence_op is not None and not nofence:
            deps.add(self.fence_op)
        if dsem is not None:
            p = self.dsem_last.get(dsem)
            if p is not None:
                deps.add(p)
            self.dsem_last[dsem] = o
            self.dsem_val[dsem] = self.dsem_val.get(dsem, 0) + dinc
            o.dval = self.dsem_val[dsem]
        deps.discard(o)
        o.deps = [d for d in deps if not (eng == "pe" and d.eng == "pe" and d.dsem is None)]
        for k in o.writes:
            self.last_w[k] = o
            self.readers[k] = []
        for k in o.reads:
            if k not in o.writes:
                self.readers.setdefault(k, []).append(o)
        self.ops.append(o)
        if dsem is None:
            self.last_eng[eng] = o
        return o

    def fence(self, fn):
        o = self.op("dve", fn, name="fence")
        deps = set(o.deps)
        for e, last in self.last_eng.items():
            if last is not o:
                deps.add(last)
        for k, last in self.dsem_last.items():
            if not (isinstance(k, str) and k.startswith("cc_")):
                deps.add(last)
        deps.discard(o)
        o.deps = list(deps)
        self.fence_op = o
        return o

    def finalize(self):
        per = {e: [] for e in ENGS}
        for o in self.ops:
            o.pos = len(per[o.eng])
            per[o.eng].append(o)
        for e in ENGS:
            known = {x: -1 for x in ENGS}
            kd = {}
            for o in per[e]:
                need = {}
                needd = {}
                for d in o.deps:
                    if d.dsem is not None:
                        if kd.get(d.dsem, 0) < d.dval:
                            needd[d.dsem] = max(needd.get(d.dsem, 0), d.dval)
                    else:
                        if known[d.eng] < d.pos:
                            if d.eng not in need or need[d.eng].pos < d.pos:
                                need[d.eng] = d
                o.waits = []
                for x, d in need.items():
                    d.sig = True
                    known[x] = d.pos
                    o.waits.append(d)
                for s, v in needd.items():
                    kd[s] = v
                    o.waits.append((s, v))
        for e in ENGS:
            c = 0
            for o in per[e]:
                if o.dsem is None and o.sig:
                    c += 1
                    o.cnt = c
        self.per = per

    def emit(self):
        nc = self.nc
        per = self.per
        with contextlib.ExitStack() as st:
            esem = {e: st.enter_context(nc.semaphore("s_" + e)) for e in ENGS}
            dsems = {}
            for k in self.dsem_val:
                dsems[k] = st.enter_context(nc.semaphore("d%d" % len(dsems)))
            block = st.enter_context(nc.Block())

            def run(e, eng):
                for o in per[e]:
                    for w in o.waits:
                        if isinstance(w, tuple):
                            eng.wait_ge(dsems[w[0]], w[1])
                        else:
                            eng.wait_ge(esem[w.eng], w.cnt)
                    ins = o.fn(eng)
                    if o.dsem is not None:
                        ins.then_inc(dsems[o.dsem], o.dinc)
                    elif o.sig:
                        ins.then_inc(esem[e], 1)
                for k, last in self.dsem_last.items():
                    if last.eng == e:
                        eng.wait_ge(dsems[k], last.dval)

            block.tensor(lambda eng: run("pe", eng))
            block.scalar(lambda eng: run("act", eng))
            block.vector(lambda eng: run("dve", eng))
            block.gpsimd(lambda eng: run("pool", eng))
            block.sync(lambda eng: run("sp", eng))


class Builder:
    def __init__(self, stage="full", ncores=8, cc_inc=1):
        self.stage = stage
        self.ncores = ncores
        self.cc_inc = cc_inc
        nc = bass.Bass("TRN2", target_bir_lowering=False)
        self.nc = nc
        self.P = Prog(nc)
        self.sb_off = 16384
        self.cnt = {}
        self.declare_io()
        self.alloc_common()

    def dram_in(self, name, shape, dt=F32):
        return self.nc.dram_tensor(name, list(shape), dt, kind="ExternalInput").ap()

    def dram_out(self, name, shape, dt=F32):
        return self.nc.dram_tensor(name, list(shape), dt, kind="ExternalOutput").ap()

    def sb_at(self, name, shape, dt, off):
        return self.nc.alloc_sbuf_tensor_at(name, list(shape), dt, offset=off)

    def sb(self, name, shape, dt):
        nbytes = int(np.prod(shape[1:])) * (4 if dt == F32 else 2)
        t = self.sb_at(name, shape, dt, self.sb_off)
        self.sb_off += (nbytes + 63) // 64 * 64
        assert self.sb_off <= 224 * 1024, (name, self.sb_off)
        return t

    def nxt(self, key, mod):
        v = self.cnt.get(key, 0)
        self.cnt[key] = v + 1
        return v % mod

    def declare_io(self):
        di = self.dram_in
        st = self.stage
        self.x_tok = di("x_tok", [NTOK, D])
        self.wts = {}
        ffns = {"norm1": ("ffn1",), "ffn1": ("ffn1",), "full": ("ffn1", "ffn2"), "mix": ()}[st]
        lns = {"norm1": ("ln_ffn1",), "ffn1": ("ln_ffn1",), "full": ("ln_ffn1", "ln_mix", "ln_ffn2"), "mix": ("ln_mix",)}[st]
        for f in ffns:
            self.wts[f + "_g"] = di(f + "_w_gate", [D, DFF])
            self.wts[f + "_u"] = di(f + "_w_up", [D, DFF])
            self.wts[f + "_d"] = di(f + "_w_down", [DFF, D])
        self.ln = {k: di(k, [1, D]) for k in lns}
        self.ident_bf_d = di("ident_bf", [128, 128], BF16)
        self.y_tok = self.dram_out("y_tok", [NTOK, D])
        if st in ("norm1", "ffn1"):
            return
        self.w_in = di("w_in", [D, 5120])
        self.w_out = di("w_out", [D, D])
        self.q_norm = di("q_norm", [1, 64])
        self.k_norm = di("k_norm", [1, 64])
        self.conv_dw_w = di("conv_dw_w", [31, 1024])
        self.cpar = di("cpar", [24, 128])
        self.cache_k = di("cache_k", [4, 2048, 1024])
        self.cache_v = di("cache_v", [4, 2048, 1024])
        self.state_conv = di("state_conv", [120, 1024])
        self.ident_f_d = di("ident_f", [128, 128])
        self.cos_d = di("cos_t", [128, NT * 32])
        self.sin_d = di("sin_t", [128, NT * 32])
        self.maskg_d = di("maskg", [128, 19 * 128], BF16)
        self.maskc_d = di("maskc", [128, 19 * 128], BF16)
        self.masks_d = di("masks", [128, 48 * 32], BF16)
        self.maskn_d = di("maskn", [32, 32], BF16)
        self.flag_d = di("flag", [128, 1])
        self.newk = self.dram_out("newk", [NTOK, 1024])
        self.newv = self.dram_out("newv", [NTOK, 1024])
        self.conv_p = self.dram_out("conv_p", [30, 1024])
        self.conv_s = self.dram_out("conv_s", [120, 1024])
        nc = self.nc
        self.hsp = nc.dram_tensor("hsp", [NTOK, D], F32, kind="Internal").ap()
        self.xin_v = nc.dram_tensor("xin_v", [1024, 1024], BF16, kind="Internal").ap()
        self.xin_k = nc.dram_tensor("xin_k", [1024, 1024], BF16, kind="Internal").ap()
        self.xin_t = nc.dram_tensor("xin_t", [32, 1024], BF16, kind="Internal").ap()
        self.xout_v = nc.dram_tensor("xout_v", [2048, 1024], BF16, kind="Internal").ap()
        self.xout_k = nc.dram_tensor("xout_k", [2048, 1024], BF16, kind="Internal").ap()
        self.xout_t = nc.dram_tensor("xout_t", [64, 1024], BF16, kind="Internal").ap()

    def alloc_common(self):
        sb = self.sb
        self.CAT = sb("CAT", [128, 16, NTOK], BF16)
        self.RINGALL = sb("RINGALL", [128, NSLOT * (SLOT_BYTES // 2)], BF16)
        self.RING = [self.RINGALL[:, i * (SLOT_BYTES // 2):(i + 1) * (SLOT_BYTES // 2)] for i in range(NSLOT)]
        self.N0 = self.sb_off
        self.GB = sb("GB", [128, D], F32)
        self.XN = [sb("XN%d" % i, [128, D], BF16) for i in range(2)]
        self.SQ = sb("SQ", [128, D], BF16)
        self.IDB = sb("IDB", [128, 128], BF16)
        self.ONB = sb("ONB", [128, 128], BF16)
        self.EPSB = sb("EPSB", [128, 1], F32)
        self.SS = sb("SS", [128, 16], F32)
        self.RS = sb("RS", [128, 16], F32)
        self.FDUM = sb("FDUM", [128, 2], F32)
        self.R0 = self.sb_off
        off = self.R0
        self.H = []
        for t in range(NT):
            self.H.append(self.sb_at("H%d" % t, [128, D], F32, off))
            off += D * 4
        self.AT = self.sb_at("AT", [128, 8, NTOK], BF16, off)
        off += 8 * NTOK * 2
        self.SIL = []
        for i in range(2):
            self.SIL.append(self.sb_at("SIL%d" % i, [128, 512], F32, off))
            off += 2048
        assert off <= 224 * 1024, off
        self.R_end_ffn = off
        self.PP = [self.nc.alloc_psum_tensor("PP%d" % i, [128, 1024], F32) for i in range(4)]
        self.PS = []
        for i in range(4):
            self.PS.append(self.PP[i][:, 0:512])
            self.PS.append(self.PP[i][:, 512:1024])
        ppb = self.PP[3].bitcast(BF16)
        self.PT_bf = [ppb[:, 0:512], ppb[:, 1024:1536]]

        P = self.P
        P.op("sp", lambda e: e.dma_start(out=self.IDB[:], in_=self.ident_bf_d[:, :]), writes=["IDB"], dsem="c0")
        P.op("dve", lambda e: e.memset(self.ONB[:], 1.0), writes=["ONB"])
        P.op("dve", lambda e: e.memset(self.EPSB[:], EPS), writes=["EPSB"])
        self.w_list = []
        self.w_issued = 0
        self.w_next = 0
        self.w_done = 0

    def w_plan(self, aps):
        self.w_list.extend(aps)

    def w_pump(self):
        while self.w_issued < min(len(self.w_list), self.w_done + NSLOT):
            j = self.w_issued
            ap, shape = self.w_list[j]
            slot = self.RING[j % NSLOT]
            n = int(np.prod(shape[1:]))
            if len(shape) == 3:
                dst = slot[:, 0:n].rearrange("p (a b) -> p a b", b=shape[2])
            else:
                dst = slot[:, 0:n]
            self.P.op("pool", lambda e, dst=dst, ap=ap: e.dma_start(out=dst, in_=ap),
                      writes=[("RING", j % NSLOT)], dsem=("ring", j % NSLOT), nofence=True)
            self.w_issued += 1

    def w_get(self):
        i = self.w_next
        self.w_next += 1
        self.w_pump()
        assert i < self.w_issued, (i, self.w_issued, self.w_done)
        return self.RING[i % NSLOT], ("RING", i % NSLOT)

    def w_get_pair(self):
        i = self.w_next
        assert i % 2 == 0
        _, k0 = self.w_get()
        _, k1 = self.w_get()
        base = (i % NSLOT) * (SLOT_BYTES // 2)
        aps = [bass.AP(self.RINGALL, base + kc * 256, [[NSLOT * (SLOT_BYTES // 2), 128], [SLOT_BYTES // 2, 2], [1, 256]])
               for kc in range(16)]
        return aps, [k0, k1]

    def w_release(self, n=1):
        self.w_done += n
        self.w_pump()

    def norm_begin(self, gname):
        self.P.op("sp", lambda e: e.dma_start(out=self.GB[:], in_=self.ln[gname].partition_broadcast(128)),
                  writes=["GB"], dsem="gb")

    def norm_tile(self, t, load_x=False):
        P = self.P
        r0, npt = TILES[t]
        H = self.H[t]
        if load_x:
            P.op("sp", lambda e: e.dma_start(out=H[0:npt, :], in_=self.x_tok[r0:r0 + npt, :]),
                 writes=[("H", t)], dsem=("xl", t % 4))
        P.op("act", lambda e: e.activation(out=self.SQ[0:npt, :], in_=H[0:npt, :], func=AF.Square,
                                           accum_out=self.SS[0:npt, t:t + 1]),
             reads=[("H", t)], writes=["SQ", ("SS", t)])
        P.op("act", lambda e: e.activation(out=self.RS[0:npt, t:t + 1], in_=self.SS[0:npt, t:t + 1], func=AF.Sqrt,
                                           scale=1.0 / D, bias=self.EPSB[0:npt, :]),
             reads=[("SS", t), "EPSB"], writes=[("RS", t)])
        P.op("dve", lambda e: e.reciprocal(out=self.RS[0:npt, t:t + 1], in_=self.RS[0:npt, t:t + 1]),
             reads=[("RS", t)], writes=[("RS", t)])
        xi = self.nxt("XN", 2)
        xn = self.XN[xi]
        P.op("dve", lambda e: e.scalar_tensor_tensor(
            out=xn[0:npt, :], in0=H[0:npt, :], scalar=self.RS[0:npt, t:t + 1], in1=self.GB[0:npt, :],
            op0=ALU.mult, op1=ALU.mult),
            reads=[("H", t), ("RS", t), "GB"], writes=[("XN", xi)])

        def stage2():
            for kq in range(4):
                half = self.nxt("T", 2)
                pt = self.PT_bf[half]
                for i in range(4):
                    kc = kq * 4 + i
                    P.op("pe", lambda e, pt=pt, kc=kc, i=i: e.transpose(
                        out=pt[:, i * 128:i * 128 + npt], in_=xn[0:npt, kc * 128:(kc + 1) * 128],
                        identity=self.IDB[0:npt, 0:npt]),
                        reads=[("XN", xi), "IDB"], writes=[("PS", 6 + half)])
                src = pt.rearrange("p (a b) -> p a b", b=128)[:, :, 0:npt]
                dst = self.CAT[:, kq * 4:kq * 4 + 4, r0:r0 + npt]
                if kq % 2 == 0:
                    P.op("act", lambda e, src=src, dst=dst: e.activation(out=dst, in_=src, func=AF.Copy),
                         writes=[("PS", 6 + half), ("CAT", kq, t)])
                else:
                    P.op("dve", lambda e, src=src, dst=dst: e.tensor_copy(out=dst, in_=src),
                         writes=[("PS", 6 + half), ("CAT", kq, t)])
        return stage2

    def rmsnorm_to_cat(self, gname, load_x=False):
        self.norm_begin(gname)
        for t in range(NT):
            self.norm_tile(t, load_x)()

    def ffn_plan(self, f):
        wg = self.wts[f + "_g"].rearrange("(kc p) c -> p kc c", p=128)
        wu = self.wts[f + "_u"].rearrange("(kc p) c -> p kc c", p=128)
        wd = self.wts[f + "_d"].rearrange("(j p) c -> p j c", p=128)
        lst = []
        j0 = 0
        for g in range(6):
            J = 8 if g < 5 else 4
            for jp in range(J // 2):
                c0 = (j0 + 2 * jp) * 128
                lst.append((wg[:, :, c0:c0 + 256], [128, 16, 256]))
                lst.append((wu[:, :, c0:c0 + 256], [128, 16, 256]))
            for c in range(4):
                lst.append((wd[:, j0:j0 + J, c * 512:(c + 1) * 512], [128, J, 512]))
            j0 += J
        self.w_plan(lst)

    def ffn(self, f, out_dram=None, tile_hook=None):
        P = self.P
        for g in range(6):
            J = 8 if g < 5 else 4
            for jp in range(J // 2):
                sg, kg = self.w_get()
                su, ku = self.w_get()
                vg = sg[:, :].rearrange("p (a b) -> p a b", b=256)
                vu = su[:, :].rearrange("p (a b) -> p a b", b=256)
                for s in range(2):
                    j = 2 * jp + s
                    for (n0, N, tiles) in GROUPS:
                        pb = self.nxt("GU", 2)
                        psg, psu = self.PS[2 * pb], self.PS[2 * pb + 1]
                        for (ps, v, kw, bank) in ((psg, vg, kg, 2 * pb), (psu, vu, ku, 2 * pb + 1)):
                            for kc in range(16):
                                P.op("pe", lambda e, ps=ps, v=v, kc=kc, s=s, n0=n0, N=N: e.matmul(
                                    ps[:, 0:N], lhsT=v[:, kc, s * 128:(s + 1) * 128], rhs=self.CAT[:, kc, n0:n0 + N],
                                    start=(kc == 0), stop=(kc == 15)),
                                    reads=[kw] + [("CAT", kc // 4, t) for t in tiles], writes=[("PS", bank)])
                        sb_ = self.nxt("SIL", 2)
                        sil = self.SIL[sb_]
                        P.op("act", lambda e, sil=sil, psg=psg, N=N: e.activation(out=sil[:, 0:N], in_=psg[:, 0:N], func=AF.Silu),
                             writes=[("PS", 2 * pb), ("SIL", sb_)])
                        P.op("dve", lambda e, sil=sil, psu=psu, j=j, n0=n0, N=N: e.tensor_tensor(
                            out=self.AT[:, j, n0:n0 + N], in0=sil[:, 0:N], in1=psu[:, 0:N], op=ALU.mult),
                            reads=[("SIL", sb_)], writes=[("PS", 2 * pb + 1)] + [("AT", j, t) for t in tiles])
                self.w_release(2)
            last_g = (g == 5)
            if last_g and tile_hook is not None:
                blks = [self.w_get() for c in range(4)]
                for t, (r0, npt) in enumerate(TILES):
                    H = self.H[t]
                    for c in range(4):
                        sd, kd = blks[c]
                        vd = sd[:, 0:J * 512].rearrange("p (a b) -> p a b", b=512)
                        db = 4 + self.nxt("D", 2)
                        psd = self.PS[db]
                        for j in range(J):
                            P.op("pe", lambda e, psd=psd, vd=vd, j=j, r0=r0, npt=npt, st=(j == 0), sp=(j == J - 1): e.matmul(
                                psd[0:npt, :], lhsT=self.AT[:, j, r0:r0 + npt], rhs=vd[:, j, :],
                                start=st, stop=sp),
                                reads=[kd, ("AT", j, t)], writes=[("PS", db)])
                        P.op("dve", lambda e, psd=psd, H=H, c=c, npt=npt: e.scalar_tensor_tensor(
                            out=H[0:npt, c * 512:(c + 1) * 512], in0=psd[0:npt, :], scalar=0.5,
                            in1=H[0:npt, c * 512:(c + 1) * 512], op0=ALU.mult, op1=ALU.add),
                            writes=[("PS", db), ("H", t)])
                    tile_hook(t)
                self.w_release(4)
                continue
            for c in range(4):
                sd, kd = self.w_get()
                vd = sd[:, 0:J * 512].rearrange("p (a b) -> p a b", b=512)
                for t, (r0, npt) in enumerate(TILES):
                    db = 4 + self.nxt("D", 2)
                    psd = self.PS[db]
                    for j in range(J):
                        P.op("pe", lambda e, psd=psd, vd=vd, j=j, r0=r0, npt=npt, st=(j == 0), sp=(j == J - 1): e.matmul(
                            psd[0:npt, :], lhsT=self.AT[:, j, r0:r0 + npt], rhs=vd[:, j, :],
                            start=st, stop=sp),
                            reads=[kd, ("AT", j, t)], writes=[("PS", db)])
                    H = self.H[t]
                    P.op("dve", lambda e, psd=psd, H=H, c=c, npt=npt: e.scalar_tensor_tensor(
                        out=H[0:npt, c * 512:(c + 1) * 512], in0=psd[0:npt, :], scalar=0.5,
                        in1=H[0:npt, c * 512:(c + 1) * 512], op0=ALU.mult, op1=ALU.add),
                        writes=[("PS", db), ("H", t)])
                    if out_dram is not None and g == 5 and c == 3:
                        P.op("sp", lambda e, H=H, r0=r0, npt=npt: e.dma_start(out=out_dram[r0:r0 + npt, :], in_=H[0:npt, :]),
                             reads=[("H", t)], dsem=("yo", t % 4))
                self.w_release(1)

    def mixer_plan(self):
        w_in = self.w_in.rearrange("(kc p) c -> p kc c", p=128)
        w_out = self.w_out.rearrange("(kc p) c -> p kc c", p=128)
        lst = []
        for cb in range(12):
            lst.append((w_in[:, :, cb * 256:(cb + 1) * 256], [128, 16, 256]))
        for i in range(4):
            lst.append((w_in[:, :, 3072 + i * 256:3072 + (i + 1) * 256], [128, 16, 256]))
            lst.append((w_in[:, :, 4096 + i * 256:4096 + (i + 1) * 256], [128, 16, 256]))
        for cb in range(8):
            lst.append((w_out[:, :, cb * 256:(cb + 1) * 256], [128, 16, 256]))
        self.w_plan(lst)

    def alloc_mixer(self):
        A = [self.R0]

        def al(name, shape, dt, ptr=A):
            nbytes = int(np.prod(shape[1:])) * (4 if dt == F32 else 2)
            t = self.sb_at(name, shape, dt, ptr[0])
            ptr[0] += (nbytes + 63) // 64 * 64
            assert ptr[0] <= 224 * 1024, (name, ptr[0])
            return t
        self.QT = al("QT", [128, 8, NTOK], BF16)
        self.KT = al("KT", [128, 8, NTOK], BF16)
        self.VS = al("VS", [128, 1024], BF16)
        self.IDF = al("IDF", [128, 128], F32)
        self.ONF = al("ONF", [128, 128], F32)
        self.DWT = al("DWT", [128, 8, 31], F32)
        self.CPT = al("CPT", [128, 24], F32)
        self.FLAG = al("FLAG", [128, 1], F32)
        self.UALL_off = A[0]
        self.UALL = al("UALL", [128, 8, 1208], BF16)
        self.UTP = al("UTP", [128, 8, 30], F32)
        self.UTS = al("UTS", [128, 8, 4, 38], F32)
        x0 = A[0]
        M = [x0]
        m = lambda n, sh, dt: al(n, sh, dt, M)
        self.COS = m("COS", [128, NT, 32], F32)
        self.SIN = m("SIN", [128, NT, 32], F32)
        self.GQK = m("GQK", [128, 2, 64], F32)
        self.ROPE = [[m("RA%d" % i, [128, NT, 64], F32), m("RB%d" % i, [128, NT, 64], F32)] for i in range(2)]
        self.SQF = m("SQF", [128, 512], F32)
        self.SS4 = m("SS4", [128, 8], F32)
        self.RS4 = m("RS4", [128, 8], F32)
        self.XS = m("XS", [128, 8, 64], F32)
        self.T2 = m("T2", [128, 8, 64], F32)
        self.OUTF = [m("OUTF%d" % i, [128, 8, 64], F32) for i in range(3)]
        self.XB16 = [m("XB16%d" % i, [128, 512], BF16) for i in range(3)]
        self.VF = [m("VF%d" % i, [128, 512], F32) for i in range(2)]
        self.VB = [m("VB%d" % i, [128, 512], BF16) for i in range(2)]
        self.CPL = m("CPL", [128, 128], F32)
        N = [self.N0]
        n = lambda nm, sh, dt: al(nm, sh, dt, N)
        self.SIG = [n("SIG%d" % i, [128, 512], F32) for i in range(2)]
        self.TMP2 = [n("TMP2%d" % i, [128, 512], F32) for i in range(2)]
        self.SCT = n("SCT", [128, 1024], F32)
        self.DWL = n("DWL", [128, 1024], F32)
        self.OST = n("OST", [128, 1024], F32)
        assert N[0] <= self.N0 + 20480, N[0]
        C = [x0]
        c = lambda nm, sh, dt: al(nm, sh, dt, C)
        self.Y = c("Y", [128, 8, 1176], F32)
        self.HALO = c("HALO", [128, 8, 32], BF16)
        self.YSQ = [c("YSQ%d" % i, [128, 256], F32) for i in range(2)]
        self.MUS = [c("MU%d" % i, [128, 256], F32) for i in range(2)]
        self.VARS = [c("VAR%d" % i, [128, 256], F32) for i in range(2)]
        self.RSTDS = [c("RSTD%d" % i, [128, 256], F32) for i in range(2)]
        self.TN = [c("TN%d" % i, [128, 256], F32) for i in range(2)]
        N2 = [self.N0]
        self.DG = [al("DG%d" % i, [128, 31, 128], BF16, N2) for i in range(2)]
        assert N2[0] <= self.N0 + 20480, N2[0]
        T = [self.UALL_off]
        a = lambda nm, sh, dt: al(nm, sh, dt, T)
        self.KCTX = [a("KCTX%d" % i, [128, 1024], BF16) for i in range(2)]
        self.VSTA = [a("VSTA%d" % i, [128, 16, 128], BF16) for i in range(2)]
        self.VSTB = [a("VSTB%d" % i, [128, 16, 128], BF16) for i in range(2)]
        self.MASKG = a("MASKG", [128, 19, 128], BF16)
        self.MASKC = a("MASKC", [128, 19, 128], BF16)
        self.EX = [a("EX%d" % i, [128, 1024], BF16) for i in range(2)]
        self.PTB2 = [a("PTB%d" % i, [128, 2, 512], BF16) for i in range(2)]
        self.RDEN = [a("RDEN%d" % i, [128, 512], F32) for i in range(2)]
        self.CK = [a("CK%d" % i, [128, 1024], F32) for i in range(2)]
        self.CV = [a("CV%d" % i, [128, 1024], F32) for i in range(2)]
        self.CKB = [a("CKB%d" % i, [128, 1024], BF16) for i in range(2)]
        self.CVB = [a("CVB%d" % i, [128, 1024], BF16) for i in range(2)]
        self.KTS = [a("KTS%d" % i, [128, 8, 128], BF16) for i in range(2)]
        self.EXS = [a("EXS%d" % i, [128, 512], BF16) for i in range(2)]
        self.PTS = [a("PTS%d" % i, [128, 16, 32], BF16) for i in range(2)]
        self.MASKS = a("MASKS", [128, 48, 32], BF16)
        self.MASKN = a("MASKN", [128, 32], BF16)
        self.RDS = a("RDS", [128, 512], F32)
        self.QBD = a("QBD", [128, 8, 64], BF16)

    def fence(self):
        self.P.fence(lambda e: e.memset(self.FDUM[:], 0.0))

    def spill_h(self):
        P = self.P
        for t, (r0, npt) in enumerate(TILES):
            H = self.H[t]
            P.op("sp", lambda e, H=H, r0=r0, npt=npt: e.dma_start(out=self.hsp[r0:r0 + npt, :], in_=H[0:npt, :]),
                 reads=[("H", t)], writes=["hsp"], dsem=("hs", t % 4))

    def reload_h(self):
        P = self.P
        for t, (r0, npt) in enumerate(TILES):
            H = self.H[t]
            P.op("sp", lambda e, H=H, r0=r0, npt=npt: e.dma_start(out=H[0:npt, :], in_=self.hsp[r0:r0 + npt, :]),
                 reads=["hsp"], writes=[("H", t)], dsem=("hs", t % 4))

    def pe_transpose_f32(self, bank, src_ap, rows, cols, key):
        ps = self.PS[bank]
        self.P.op("pe", lambda e, ps=ps, src_ap=src_ap, rows=rows, cols=cols: e.transpose(
            out=ps[0:cols, 0:rows], in_=src_ap, identity=self.IDF[0:rows, 0:rows]),
            reads=["IDF", key], writes=[("PS", bank)])

    def mixer_setup(self):
        P = self.P
        ld = lambda dst, src, key, ds: P.op("sp", lambda e: e.dma_start(out=dst, in_=src), writes=[key], dsem=ds)
        ld(self.IDF[:], self.ident_f_d[:, :], "IDF", "m0")
        ld(self.COS[:].rearrange("p a b -> p (a b)"), self.cos_d[:, :], "COS", "m1")
        ld(self.SIN[:].rearrange("p a b -> p (a b)"), self.sin_d[:, :], "SIN", "m2")
        ld(self.GQK[:, 0, :], self.q_norm.partition_broadcast(128), "GQ", "m3")
        ld(self.GQK[:, 1, :], self.k_norm.partition_broadcast(128), "GK", "m0")
        ld(self.FLAG[:], self.flag_d[:, :], "FLAG", "m1")
        ld(self.SCT[0:120, :], self.state_conv[:, :], "SCT", "m2")
        ld(self.DWL[0:31, :], self.conv_dw_w[:, :], "DWL", "m3")
        ld(self.CPL[0:24, :], self.cpar[:, :], "CPL", "m0")
        P.op("dve", lambda e: e.memset(self.ONF[:], 1.0), writes=["ONF"])
        for i, gk in enumerate(("GQ", "GK")):
            RA, RB = self.ROPE[i]
            g1 = self.GQK[:, i, 0:32].unsqueeze(1).to_broadcast([128, NT, 32])
            g2 = self.GQK[:, i, 32:64].unsqueeze(1).to_broadcast([128, NT, 32])
            P.op("dve", lambda e, RA=RA, g1=g1: e.tensor_tensor(out=RA[:, :, 0:32], in0=self.COS[:], in1=g1, op=ALU.mult),
                 reads=["COS", gk], writes=[("ROPE", i, 0)])
            P.op("dve", lambda e, RA=RA, g2=g2: e.tensor_tensor(out=RA[:, :, 32:64], in0=self.COS[:], in1=g2, op=ALU.mult),
                 reads=["COS", gk], writes=[("ROPE", i, 1)])
            P.op("dve", lambda e, RB=RB, g2=g2: e.scalar_tensor_tensor(out=RB[:, :, 0:32], in0=self.SIN[:], scalar=-1.0, in1=g2,
                                                                       op0=ALU.mult, op1=ALU.mult),
                 reads=["SIN", gk], writes=[("ROPE", i, 2)])
            P.op("dve", lambda e, RB=RB, g1=g1: e.tensor_tensor(out=RB[:, :, 32:64], in0=self.SIN[:], in1=g1, op=ALU.mult),
                 reads=["SIN", gk], writes=[("ROPE", i, 3)])
        bank = 4
        self.pe_transpose_f32(bank, self.CPL[0:24, :], 24, 128, "CPL")
        P.op("dve", lambda e: e.tensor_copy(out=self.CPT[:, :], in_=self.PS[4][:, 0:24]),
             writes=[("PS", 4), "CPT"])
        for c in range(8):
            bank = 4 + (c % 2)
            P.op("pe", lambda e, c=c, bank=bank: e.transpose(out=self.PS[bank][:, 0:31], in_=self.DWL[0:31, c * 128:(c + 1) * 128],
                                                             identity=self.IDF[0:31, 0:31]),
                 reads=["IDF", "DWL"], writes=[("PS", bank)])
            P.op("dve", lambda e, c=c, bank=bank: e.tensor_copy(out=self.DWT[:, c, :], in_=self.PS[bank][:, 0:31]),
                 writes=[("PS", bank), ("DWT", c)])
        for c in range(8):
            bank = 4 + (c % 2)
            P.op("pe", lambda e, c=c, bank=bank: e.transpose(out=self.PS[bank][:, 0:120], in_=self.SCT[0:120, c * 128:(c + 1) * 128],
                                                             identity=self.IDF[0:120, 0:120]),
                 reads=["IDF", "SCT"], writes=[("PS", bank)])
            src = self.PS[bank][:, 0:120].rearrange("p (b j) -> p b j", j=30)
            dst1 = self.UALL[:, c, 1054:1206].rearrange("p (b j) -> p b j", j=38)[:, :, 0:30]
            P.op("act", lambda e, src=src, dst1=dst1: e.activation(out=dst1, in_=src, func=AF.Copy),
                 writes=[("PS", bank), ("UALL", c, 2)])
            P.op("dve", lambda e, src=src, c=c: e.tensor_copy(out=self.UTS[:, c, :, 0:30], in_=src),
                 writes=[("PS", bank), ("UTS", c)])

    def proj_qk(self):
        P = self.P
        self.pend_tr = []
        for cbp in range(4):
            isk = cbp >= 2
            qi = 1 if isk else 0
            cloc = (cbp % 2) * 512
            dstT = self.KT if isk else self.QT
            rhs_aps, wks = self.w_get_pair()
            RA, RB = self.ROPE[qi]
            for t, (r0, npt) in enumerate(TILES):
                bank = self.nxt("PQ", 6)
                ps = self.PS[bank]
                for kc in range(16):
                    P.op("pe", lambda e, ps=ps, rhs=rhs_aps[kc], kc=kc, r0=r0, npt=npt: e.matmul(
                        ps[0:npt, :].rearrange("p (a b) -> p a b", b=256), lhsT=self.CAT[:, kc, r0:r0 + npt], rhs=rhs,
                        start=(kc == 0), stop=(kc == 15)),
                        reads=wks + [("CAT", kc // 4, t)], writes=[("PS", bank)])
                ps3 = ps[0:npt, :].rearrange("p (h d) -> p h d", d=64)
                P.op("act", lambda e, ps=ps, npt=npt: e.activation(out=self.SQF[0:npt, :], in_=ps[0:npt, :], func=AF.Square),
                     writes=[("PS", bank), "SQF"])
                P.op("dve", lambda e, npt=npt: e.tensor_reduce(out=self.SS4[0:npt, :], in_=self.SQF[0:npt, :].rearrange("p (h d) -> p h d", d=64),
                                                                axis=AX.X, op=ALU.add),
                     reads=["SQF"], writes=["SS4"])
                P.op("act", lambda e, npt=npt: e.activation(out=self.RS4[0:npt, :], in_=self.SS4[0:npt, :], func=AF.Sqrt,
                                                            scale=1.0 / 64, bias=self.EPSB[0:npt, :]),
                     reads=["SS4", "EPSB"], writes=["RS4"])
                P.op("dve", lambda e, npt=npt: e.reciprocal(out=self.RS4[0:npt, :], in_=self.RS4[0:npt, :]),
                     writes=["RS4"])
                P.op("dve", lambda e, ps3=ps3, npt=npt: e.tensor_tensor(
                    out=self.XS[0:npt], in0=ps3, in1=self.RS4[0:npt, :].unsqueeze(2).to_broadcast([npt, 8, 64]), op=ALU.mult),
                    reads=["RS4"], writes=[("PS", bank), "XS"])
                ob = self.nxt("OUTF", 3)
                OF = self.OUTF[ob]
                P.op("dve", lambda e, OF=OF, RA=RA, npt=npt, t=t: e.tensor_tensor(
                    out=OF[0:npt], in0=self.XS[0:npt], in1=RA[0:npt, t, :].unsqueeze(1).to_broadcast([npt, 8, 64]), op=ALU.mult),
                    reads=["XS", ("ROPE", qi, 0), ("ROPE", qi, 1)], writes=[("OUTF", ob)])
                P.op("dve", lambda e, RB=RB, npt=npt, t=t: e.tensor_tensor(
                    out=self.T2[0:npt, :, 0:32], in0=self.XS[0:npt, :, 32:64],
                    in1=RB[0:npt, t, 0:32].unsqueeze(1).to_broadcast([npt, 8, 32]), op=ALU.mult),
                    reads=["XS", ("ROPE", qi, 2)], writes=[("T2", 0)])
                P.op("dve", lambda e, RB=RB, npt=npt, t=t: e.tensor_tensor(
                    out=self.T2[0:npt, :, 32:64], in0=self.XS[0:npt, :, 0:32],
                    in1=RB[0:npt, t, 32:64].unsqueeze(1).to_broadcast([npt, 8, 32]), op=ALU.mult),
                    reads=["XS", ("ROPE", qi, 3)], writes=[("T2", 1)])
                P.op("dve", lambda e, OF=OF, npt=npt: e.tensor_tensor(out=OF[0:npt], in0=OF[0:npt], in1=self.T2[0:npt], op=ALU.add),
                     reads=[("T2", 0), ("T2", 1)], writes=[("OUTF", ob)])
                if isk:
                    P.op("sp", lambda e, OF=OF, r0=r0, npt=npt, cloc=cloc: e.dma_start(
                        out=self.newk[r0:r0 + npt, cloc:cloc + 512], in_=OF[0:npt].rearrange("p h d -> p (h d)")),
                        reads=[("OUTF", ob)], dsem=("ko", self.nxt("ko", 4)))
                xb = self.nxt("XB", 3)
                XB = self.XB16[xb]
                P.op("act", lambda e, XB=XB, OF=OF, npt=npt: e.activation(out=XB[0:npt, :], in_=OF[0:npt].rearrange("p h d -> p (h d)"),
                                                                          func=AF.Copy),
                     reads=[("OUTF", ob)], writes=[("XB", xb)])

                def stage2(XB=XB, xb=xb, npt=npt, r0=r0, t=t, cbp=cbp, isk=isk, dstT=dstT):
                    half = self.nxt("T", 2)
                    pt = self.PT_bf[half]
                    for i in range(4):
                        P.op("pe", lambda e, pt=pt, XB=XB, i=i, npt=npt: e.transpose(
                            out=pt[:, i * 128:i * 128 + npt], in_=XB[0:npt, i * 128:(i + 1) * 128], identity=self.IDB[0:npt, 0:npt]),
                            reads=[("XB", xb), "IDB"], writes=[("PS", 6 + half)])
                    src = pt.rearrange("p (a b) -> p a b", b=128)[:, :, 0:npt]
                    c0 = 4 * (cbp % 2)
                    dst = dstT[:, c0:c0 + 4, r0:r0 + npt]
                    kks = [("KT" if isk else "QT", c0 // 2, t), ("KT" if isk else "QT", c0 // 2 + 1, t)]
                    if self.nxt("evq", 2) == 0:
                        P.op("act", lambda e, src=src, dst=dst: e.activation(out=dst, in_=src, func=AF.Copy),
                             writes=[("PS", 6 + half)] + kks)
                    else:
                        P.op("dve", lambda e, src=src, dst=dst: e.tensor_copy(out=dst, in_=src),
                             writes=[("PS", 6 + half)] + kks)
                self.pend_tr.append(stage2)
                if len(self.pend_tr) > 2:
                    self.pend_tr.pop(0)()
            self.w_release(2)
        while self.pend_tr:
            self.pend_tr.pop(0)()

    def proj_v(self):
        P = self.P
        for cbp in range(2):
            cloc = cbp * 512
            rhs_aps, wks = self.w_get_pair()
            for t, (r0, npt) in enumerate(TILES):
                bank = self.nxt("PQ", 6)
                ps = self.PS[bank]
                for kc in range(16):
                    P.op("pe", lambda e, ps=ps, rhs=rhs_aps[kc], kc=kc, r0=r0, npt=npt: e.matmul(
                        ps[0:npt, :].rearrange("p (a b) -> p a b", b=256), lhsT=self.CAT[:, kc, r0:r0 + npt], rhs=rhs,
                        start=(kc == 0), stop=(kc == 15)),
                        reads=wks + [("CAT", kc // 4, t)], writes=[("PS", bank)])
                vb = self.nxt("VF", 2)
                VF = self.VF[vb]
                P.op("act", lambda e, VF=VF, ps=ps, npt=npt: e.activation(out=VF[0:npt, :], in_=ps[0:npt, :], func=AF.Copy),
                     writes=[("PS", bank), ("VF", vb)])
                P.op("sp", lambda e, VF=VF, r0=r0, npt=npt, cloc=cloc: e.dma_start(out=self.newv[r0:r0 + npt, cloc:cloc + 512], in_=VF[0:npt, :]),
                     reads=[("VF", vb)], dsem=("vo", self.nxt("vo", 4)))
                if t < 8:
                    v2 = self.nxt("VB", 2)
                    VB = self.VB[v2]
                    P.op("dve", lambda e, VB=VB, VF=VF: e.tensor_copy(out=VB[:, :], in_=VF[:, :]),
                         reads=[("VF", vb)], writes=[("VB", v2)])
                    P.op("sp", lambda e, VB=VB, r0=r0, cloc=cloc: e.dma_start(out=self.xin_v[r0:r0 + 128, cloc:cloc + 512], in_=VB[:, :]),
                         reads=[("VB", v2)], writes=["xin_v"], dsem=("xv", self.nxt("xv", 4)))
                else:
                    P.op("dve", lambda e, VF=VF, cloc=cloc: e.tensor_copy(out=self.VS[0:32, cloc:cloc + 512], in_=VF[0:32, :]),
                         reads=[("VF", vb)], writes=[("VS", cbp)])
            self.w_release(2)

    def proj_glu(self):
        P = self.P
        for i in range(4):
            sa, ka = self.w_get()
            sb_, kb = self.w_get()
            va = sa[:, :].rearrange("p (a b) -> p a b", b=256)
            vb = sb_[:, :].rearrange("p (a b) -> p a b", b=256)
            for s in range(2):
                c = 2 * i + s
                for gi, (n0, N, tiles) in enumerate(GROUPS):
                    pb = self.nxt("GU", 2)
                    psa, psb = self.PS[2 * pb], self.PS[2 * pb + 1]
                    for (ps, v, kw, bank) in ((psa, va, ka, 2 * pb), (psb, vb, kb, 2 * pb + 1)):
                        for kc in range(16):
                            P.op("pe", lambda e, ps=ps, v=v, kc=kc, s=s, n0=n0, N=N: e.matmul(
                                ps[:, 0:N], lhsT=v[:, kc, s * 128:(s + 1) * 128], rhs=self.CAT[:, kc, n0:n0 + N],
                                start=(kc == 0), stop=(kc == 15)),
                                reads=[kw] + [("CAT", kc // 4, t) for t in tiles], writes=[("PS", bank)])
                    sg = self.nxt("SIG", 2)
                    SIG, TMP = self.SIG[sg], self.TMP2[sg]
                    P.op("act", lambda e, SIG=SIG, psb=psb, N=N: e.activation(out=SIG[:, 0:N], in_=psb[:, 0:N], func=AF.Tanh, scale=0.5),
                         writes=[("PS", 2 * pb + 1), ("SIG", sg)])
                    P.op("dve", lambda e, SIG=SIG, TMP=TMP, psa=psa, N=N: e.scalar_tensor_tensor(
                        out=TMP[:, 0:N], in0=SIG[:, 0:N], scalar=1.0, in1=psa[:, 0:N], op0=ALU.add, op1=ALU.mult),
                        reads=[("SIG", sg)], writes=[("PS", 2 * pb), ("TMP2", sg)])
                    if gi < 2:
                        dst = self.UALL[:, c, 30 + n0:30 + n0 + N]
                        src = TMP[:, 0:N]
                    else:
                        dst = self.UALL[:, c, 1054:1206].rearrange("p (b j) -> p b j", j=38)[:, :, 30:38]
                        src = TMP[:, 0:32].rearrange("p (b j) -> p b j", j=8)
                    P.op("act", lambda e, dst=dst, src=src: e.activation(out=dst, in_=src, func=AF.Copy, scale=0.5),
                         reads=[("TMP2", sg)], writes=[("UALL", c, gi)])
                    if gi == 1:
                        P.op("dve", lambda e, TMP=TMP, c=c: e.tensor_scalar(out=self.UTP[:, c, :], in0=TMP[:, 482:512], scalar1=0.5, scalar2=None,
                                                                             op0=ALU.mult),
                             reads=[("TMP2", sg)], writes=[("UTP", c)])
                    if gi == 2:
                        P.op("dve", lambda e, src=src, c=c: e.tensor_scalar(out=self.UTS[:, c, :, 30:38], in0=src, scalar1=0.5, scalar2=None,
                                                                             op0=ALU.mult),
                             reads=[("TMP2", sg)], writes=[("UTS", c)])
            self.w_release(2)

    def conv_state_outputs(self):
        P = self.P
        jobs = [(self.conv_p[:, :], [self.UTP[:, c, :] for c in range(8)], [("UTP", c) for c in range(8)])]
        for b in range(4):
            jobs.append((self.conv_s[b * 30:(b + 1) * 30, :], [self.UTS[:, c, b, 8:38] for c in range(8)], [("UTS", c) for c in range(8)]))
        for (dst, srcs, keys) in jobs:
            for hf in range(2):
                bank = 4 + hf
                for i in range(4):
                    c = hf * 4 + i
                    P.op("pe", lambda e, bank=bank, i=i, src=srcs[c]: e.transpose(
                        out=self.PS[bank][0:30, i * 128:(i + 1) * 128], in_=src, identity=self.IDF[:, :]),
                        reads=["IDF", keys[c]], writes=[("PS", bank)])
                if hf == 0:
                    P.op("act", lambda e, bank=bank: e.activation(out=self.OST[0:30, 0:512], in_=self.PS[bank][0:30, :], func=AF.Copy),
                         writes=[("PS", bank), ("OST", 0)])
                else:
                    P.op("dve", lambda e, bank=bank: e.tensor_copy(out=self.OST[0:30, 512:1024], in_=self.PS[bank][0:30, :]),
                         writes=[("PS", bank), ("OST", 1)])
            P.op("sp", lambda e, dst=dst: e.dma_start(out=dst, in_=self.OST[0:30, :]),
                 reads=[("OST", 0), ("OST", 1)], dsem=("co", self.nxt("co", 2)))

    def _allgather(self, nm, src, dst):
        groups = [[2 * i, 2 * i + 1] for i in range(self.ncores // 2)]
        self.P.op("pool", lambda e: e.collective_compute("AllGather", ALU.bypass, replica_groups=groups, ins=[src], outs=[dst]),
                  reads=["xin_" + nm], writes=["xout_" + nm], dsem="cc_" + nm, dinc=self.cc_inc, nofence=True)

    def exchange_k(self):
        P = self.P
        for c in range(8):
            P.op("sp", lambda e, c=c: e.dma_start(out=self.xin_k[c * 128:(c + 1) * 128, :], in_=self.KT[:, c, 0:1024]),
                 reads=[("KT", c // 2, t) for t in range(9)], writes=["xin_k"], dsem=("xk", c % 4))
        self._allgather("k", self.xin_k, self.xout_k)

    def exchange_v(self):
        self._allgather("v", self.xin_v, self.xout_v)

    def exchange_t(self):
        P = self.P
        tail = self.xin_t.rearrange("r (q j) -> (r q) j", j=32).rearrange("(c p) j -> p c j", p=128)
        P.op("sp", lambda e: e.dma_start(out=tail[:, :, :], in_=self.UALL[:, :, 1024:1056]),
             reads=[("UALL", c, 1) for c in range(8)] + [("UALL", c, 2) for c in range(8)], writes=["xin_t"], dsem="xt")
        self._allgather("t", self.xin_t, self.xout_t)

    def conv_ln(self):
        P = self.P
        hsrc = self.xout_t[0:32, :].rearrange("r (q j) -> (r q) j", j=32).rearrange("(c p) j -> p c j", p=128)
        P.op("sp", lambda e: e.dma_start(out=self.HALO[:, :, :], in_=hsrc), reads=["xout_t"], writes=["HALO"], dsem="hl")
        P.op("dve", lambda e: e.tensor_scalar(out=self.UALL[:, :, 0:30], in0=self.HALO[:, :, 0:30], scalar1=self.FLAG[:, 0:1], scalar2=None,
                                              op0=ALU.mult),
             reads=["HALO", "FLAG"], writes=[("UALL", c, 3) for c in range(8)])
        pieces = [(0, 512), (512, 512), (1054, 122)]
        ycol = [0, 512, 1024]
        for c in range(8):
            DG = self.DG[c % 2]
            P.op("dve", lambda e, DG=DG, c=c: e.tensor_tensor(
                out=DG[:, :, :], in0=self.IDB[:, :].unsqueeze(1).to_broadcast([128, 31, 128]),
                in1=self.DWT[:, c, :].unsqueeze(2).to_broadcast([128, 31, 128]), op=ALU.mult),
                reads=["IDB", ("DWT", c)], writes=[("DG", c % 2)])
            for pi, (a0, N) in enumerate(pieces):
                bank = self.nxt("CV", 4)
                ps = self.PS[bank]
                for k in range(31):
                    P.op("pe", lambda e, ps=ps, DG=DG, k=k, c=c, a0=a0, N=N: e.matmul(
                        ps[:, 0:N], lhsT=DG[:, k, :], rhs=self.UALL[:, c, a0 + k:a0 + k + N], start=(k == 0), stop=(k == 30)),
                        reads=[("DG", c % 2)] + [("UALL", c, g) for g in range(4)], writes=[("PS", bank)])
                P.op("act", lambda e, ps=ps, c=c, y0=ycol[pi], N=N: e.activation(
                    out=self.Y[:, c, y0:y0 + N], in_=ps[:, 0:N], func=AF.Identity, bias=self.CPT[:, c:c + 1]),
                    reads=["CPT"], writes=[("PS", bank), ("Y", c, pi)])
        lnp = [(0, 256, 0), (256, 256, 0), (512, 256, 1), (768, 256, 1), (1024, 122, 2)]

        def stats(li):
            y0, N, pi = lnp[li]
            MU, VAR, RSTD = self.MUS[li % 2], self.VARS[li % 2], self.RSTDS[li % 2]
            kx = li % 2
            for c in range(8):
                yb = self.nxt("YSQ", 2)
                YSQ = self.YSQ[yb]
                P.op("act", lambda e, YSQ=YSQ, c=c, y0=y0, N=N: e.activation(out=YSQ[:, 0:N], in_=self.Y[:, c, y0:y0 + N], func=AF.Square),
                     reads=[("Y", c, pi)], writes=[("YSQ", yb)])
                P.op("pe", lambda e, c=c, y0=y0, N=N: e.matmul(self.PS[4][:, 0:N], lhsT=self.ONF[:, :], rhs=self.Y[:, c, y0:y0 + N],
                                                               start=(c == 0), stop=(c == 7)),
                     reads=["ONF", ("Y", c, pi)], writes=[("PS", 4)])
                P.op("pe", lambda e, YSQ=YSQ, c=c, N=N: e.matmul(self.PS[5][:, 0:N], lhsT=self.ONF[:, :], rhs=YSQ[:, 0:N],
                                                                 start=(c == 0), stop=(c == 7)),
                     reads=["ONF", ("YSQ", yb)], writes=[("PS", 5)])
            P.op("dve", lambda e, N=N: e.tensor_scalar(out=MU[:, 0:N], in0=self.PS[4][:, 0:N], scalar1=1.0 / 1024, scalar2=None, op0=ALU.mult),
                 writes=[("PS", 4), ("MU", kx)])
            P.op("dve", lambda e, N=N: e.tensor_tensor(out=VAR[:, 0:N], in0=MU[:, 0:N], in1=MU[:, 0:N], op=ALU.mult),
                 reads=[("MU", kx)], writes=[("VAR", kx)])
            P.op("dve", lambda e, N=N: e.scalar_tensor_tensor(out=VAR[:, 0:N], in0=self.PS[5][:, 0:N], scalar=1.0 / 1024, in1=VAR[:, 0:N],
                                                              op0=ALU.mult, op1=ALU.subtract),
                 writes=[("PS", 5), ("VAR", kx)])
            P.op("act", lambda e, N=N: e.activation(out=RSTD[:, 0:N], in_=VAR[:, 0:N], func=AF.Sqrt, bias=self.EPSB[:, :]),
                 reads=[("VAR", kx), "EPSB"], writes=[("RSTD", kx)])
            P.op("dve", lambda e, N=N: e.reciprocal(out=RSTD[:, 0:N], in_=RSTD[:, 0:N]), writes=[("RSTD", kx)])

        def normalize(li):
            y0, N, pi = lnp[li]
            MU, RSTD = self.MUS[li % 2], self.RSTDS[li % 2]
            kx = li % 2
            for c in range(8):
                tb = self.nxt("TN", 2)
                TN = self.TN[tb]
                P.op("dve", lambda e, TN=TN, c=c, y0=y0, N=N: e.tensor_tensor(out=TN[:, 0:N], in0=self.Y[:, c, y0:y0 + N], in1=MU[:, 0:N],
                                                                             op=ALU.subtract),
                     reads=[("Y", c, pi), ("MU", kx)], writes=[("TN", tb)])
                P.op("dve", lambda e, TN=TN, N=N: e.tensor_tensor(out=TN[:, 0:N], in0=TN[:, 0:N], in1=RSTD[:, 0:N], op=ALU.mult),
                     reads=[("RSTD", kx)], writes=[("TN", tb)])
                if pi < 2:
                    dst = self.CAT[:, 8 + c, y0:y0 + N]
                    src = TN[:, 0:N]
                    tiles = (y0 // 128, y0 // 128 + 1)
                else:
                    dst = self.CAT[:, 8 + c, 1024:1056].rearrange("p (b j) -> p b j", j=8)
                    src = TN[:, 0:152].rearrange("p (b j) -> p b j", j=38)[:, :, 0:8]
                    tiles = (8,)
                P.op("act", lambda e, dst=dst, src=src, c=c: e.activation(out=dst, in_=src, func=AF.Silu,
                                                                          scale=self.CPT[:, 8 + c:9 + c], bias=self.CPT[:, 16 + c:17 + c]),
                     reads=[("TN", tb), "CPT"], writes=[("CAT", (8 + c) // 4, t) for t in tiles])

        stats(0)
        for li in range(len(lnp)):
            if li + 1 < len(lnp):
                stats(li + 1)
            normalize(li)

    def attention(self):
        P = self.P
        ld = lambda dst, src, key, ds: P.op("sp", lambda e: e.dma_start(out=dst, in_=src), writes=[key], dsem=ds)
        ld(self.MASKG[:].rearrange("p a b -> p (a b)"), self.maskg_d[:, :], "MASKG", "m0")
        ld(self.MASKC[:].rearrange("p a b -> p (a b)"), self.maskc_d[:, :], "MASKC", "m1")
        ld(self.MASKS[:].rearrange("p a b -> p (a b)"), self.masks_d[:, :], "MASKS", "m2")
        ld(self.MASKN[0:32, :], self.maskn_d[:, :], "MASKN", "m3")
        for kb in range(2):
            P.op("pool", lambda e, kb=kb: e.memset(self.VSTA[kb][:, :, 64:128], 1.0), writes=[("VONE", 0, kb)])
            P.op("pool", lambda e, kb=kb: e.memset(self.VSTB[kb][:, :, 0:64], 1.0), writes=[("VONE", 1, kb)])

        def loads(c):
            kb = c % 2
            KC = self.KCTX[kb]
            P.op("sp", lambda e, KC=KC, c=c: e.dma_start(out=KC[:, :], in_=self.xout_k[c * 128:(c + 1) * 128, :]),
                 reads=["xout_k"], writes=[("KCTX", kb)], dsem=("kc", kb))
            for e_, VT in ((0, self.VSTA[kb]), (1, self.VSTB[kb])):
                f0 = c * 128 + e_ * 64
                P.op("sp", lambda e, VT=VT, f0=f0, e_=e_: e.dma_start(
                    out=VT[:, 0:8, e_ * 64:(e_ + 1) * 64], in_=self.xout_v[0:1024, f0:f0 + 64].rearrange("(t p) f -> p t f", p=128)),
                    reads=["xout_v"], writes=[("VST", e_, kb, 0)], dsem=("vc", e_, kb))
                P.op("sp", lambda e, VT=VT, f0=f0, e_=e_: e.dma_start(
                    out=VT[:, 8:16, e_ * 64:(e_ + 1) * 64], in_=self.xin_v[0:1024, f0:f0 + 64].rearrange("(t p) f -> p t f", p=128)),
                    reads=["xin_v"], writes=[("VST", e_, kb, 1)], dsem=("vo2", e_, kb))

        steps = []
        for c in range(8):
            for e_ in range(2):
                for qg in range(2):
                    blocks = [(0, j) for j in range(8)] + [(1, j) for j in range(4 * qg + 4)]
                    nd = self.nxt("ND", 2)
                    npair = len(blocks) // 2
                    for bp in range(npair):
                        steps.append(dict(c=c, e_=e_, qg=qg, bp=bp, npair=npair, blk=blocks[2 * bp:2 * bp + 2], nd=nd,
                                          pp=self.nxt("SPP", 2), ex=self.nxt("EX", 2),
                                          pb=[self.nxt("PTB", 2)],
                                          rd=self.nxt("RDEN", 2) if bp == npair - 1 else None,
                                          newc=(e_ == 0 and qg == 0 and bp == 0)))

        def stageA(s):
            c, e_, qg = s["c"], s["e_"], s["qg"]
            kb = c % 2
            KC = self.KCTX[kb]
            r0, r1 = e_ * 64, (e_ + 1) * 64
            q0 = qg * 512
            PPt = self.PP[s["pp"]]
            for h2 in range(2):
                own, j = s["blk"][h2]
                if own:
                    lhs = self.KT[r0:r1, c, j * 128:(j + 1) * 128]
                    rk = [("KT", c // 2, j)]
                else:
                    lhs = KC[r0:r1, j * 128:(j + 1) * 128]
                    rk = [("KCTX", kb)]
                P.op("pe", lambda e, PPt=PPt, h2=h2, lhs=lhs, c=c, r0=r0, r1=r1, q0=q0: e.matmul(
                    PPt[:, h2 * 512:(h2 + 1) * 512], lhsT=lhs, rhs=self.QT[r0:r1, c, q0:q0 + 512], start=True, stop=True),
                    reads=rk + [("QT", c // 2, t) for t in range(4 * qg, 4 * qg + 4)], writes=[("PS", 2 * s["pp"] + h2)])

        def stageB(s):
            PPt = self.PP[s["pp"]]
            EX = self.EX[s["ex"]]
            P.op("act", lambda e, EX=EX, PPt=PPt: e.activation(out=EX[:, :], in_=PPt[:, :], func=AF.Exp, scale=0.125),
                 writes=[("PS", 2 * s["pp"]), ("PS", 2 * s["pp"] + 1), ("EX", s["ex"])])

        def stageC(s):
            c, e_, qg, bp, npair, nd = s["c"], s["e_"], s["qg"], s["bp"], s["npair"], s["nd"]
            kb = c % 2
            VT = (self.VSTA if e_ == 0 else self.VSTB)[kb]
            r0, r1 = e_ * 64, (e_ + 1) * 64
            o0, o1 = (1 - e_) * 64, (2 - e_) * 64
            q0 = qg * 512
            bn = 4 + nd
            EX = self.EX[s["ex"]]
            own, j = s["blk"][0]
            assert s["blk"][1] == (own, j + 1)
            mt, mkey = (self.MASKG, "MASKG") if own else (self.MASKC, "MASKC")
            d0 = (4 * qg - j) if own else (8 + 4 * qg - j)
            mk = bass.AP(mt, (d0 + 3) * 128, [[19 * 128, 128], [-128, 2], [1, 512]])
            pb = s["pb"][0]
            PT = self.PTB2[pb % 2]
            P.op("dve", lambda e, PT=PT, EX=EX, mk=mk: e.tensor_tensor(
                out=PT[:, :, :], in0=EX[:, :].rearrange("p (a b) -> p a b", b=512), in1=mk, op=ALU.mult),
                reads=[("EX", s["ex"]), mkey], writes=[("PTB", pb % 2)])
            for h2 in range(2):
                own, j = s["blk"][h2]
                vt = VT[:, (8 + j) if own else j, :]
                vk = ("VST", e_, kb, 1 if own else 0)
                first = (bp == 0 and h2 == 0)
                last = (bp == npair - 1 and h2 == 1)
                P.op("pe", lambda e, PT=PT, vt=vt, bn=bn, first=first, last=last, h2=h2: e.matmul(
                    self.PS[bn][:, :], lhsT=vt, rhs=PT[:, h2, :], start=first, stop=last),
                    reads=[("PTB", pb % 2), vk, ("VONE", e_, kb)], writes=[("PS", bn)])
            if bp == npair - 1:
                rd = s["rd"]
                RD = self.RDEN[rd]

                def norm(RD=RD, rd=rd, bn=bn, r0=r0, r1=r1, o0=o0, o1=o1, c=c, q0=q0, qg=qg):
                    P.op("dve", lambda e: e.reciprocal(out=RD[r0:r1, :], in_=self.PS[bn][o0:o1, :]),
                         writes=[("PS", bn), ("RDEN", rd)])
                    P.op("dve", lambda e: e.tensor_tensor(
                        out=self.CAT[r0:r1, c, q0:q0 + 512], in0=self.PS[bn][r0:r1, :], in1=RD[r0:r1, :], op=ALU.mult),
                        reads=[("RDEN", rd)], writes=[("PS", bn)] + [("CAT", c // 4, t) for t in range(4 * qg, 4 * qg + 4)])
                pend_norm.append([2, norm])
            for pn in pend_norm[:]:
                if pn[0] == 0:
                    pn[1]()
                    pend_norm.remove(pn)
                else:
                    pn[0] -= 1

        pend_norm = []
        loads(0)
        LOOK = 2
        n = len(steps)
        for i in range(min(LOOK, n)):
            stageA(steps[i])
        for i in range(n):
            s_ = steps[i]
            if s_["newc"] and s_["c"] + 1 < 8:
                loads(s_["c"] + 1)
            stageB(s_)
            if i + LOOK < n:
                stageA(steps[i + LOOK])
            stageC(s_)
        for pn in pend_norm:
            pn[1]()

    def sample_attention(self):
        P = self.P
        NUM, DEN = 2, 3
        P.op("dve", lambda e: e.memset(self.QBD[:], 0.0), writes=["QBD"])
        for e_ in range(2):
            P.op("dve", lambda e, e_=e_: e.tensor_copy(out=self.QBD[e_ * 64:(e_ + 1) * 64, :, e_ * 32:(e_ + 1) * 32],
                                                      in_=self.QT[e_ * 64:(e_ + 1) * 64, :, 1024:1056]),
                 reads=[("QT", cp, 8) for cp in range(4)], writes=["QBD"])
        steps = []
        for (b, r) in [(b, r) for b in range(4) for r in range(12)] + [(-1, -1)]:
            steps.append(dict(b=b, r=r, new=(b < 0), sb=self.nxt("SS", 2), cb=self.nxt("CK", 2) if b >= 0 else None,
                              ex=self.nxt("EXS", 2)))
        nst = len(steps)

        def stageA(s):
            b, r, new, sb_ = s["b"], s["r"], s["new"], s["sb"]
            ps = self.PS[sb_]
            if not new:
                cb = s["cb"]
                CK, CV, CKB, CVB, KTS = self.CK[cb], self.CV[cb], self.CKB[cb], self.CVB[cb], self.KTS[cb]
                if r < 8:
                    ksrc = self.cache_k[b].rearrange("(m r) f -> r m f", r=16)[r]
                    vsrc = self.cache_v[b].rearrange("(m r) f -> r m f", r=16)[r]
                else:
                    ksrc = self.cache_k[b, (r + 4) * 128:(r + 5) * 128, :]
                    vsrc = self.cache_v[b, (r + 4) * 128:(r + 5) * 128, :]
                P.op("sp", lambda e, CK=CK, ksrc=ksrc: e.dma_start(out=CK[:, :], in_=ksrc),
                     writes=[("CK", cb)], dsem=("ck", cb))
                P.op("sp", lambda e, CV=CV, vsrc=vsrc: e.dma_start(out=CV[:, :], in_=vsrc),
                     writes=[("CV", cb)], dsem=("cv", cb))
                P.op("act", lambda e, CK=CK, CKB=CKB: e.activation(out=CKB[:, :], in_=CK[:, :], func=AF.Copy),
                     reads=[("CK", cb)], writes=[("CKB", cb)])
                P.op("dve", lambda e, CV=CV, CVB=CVB: e.tensor_copy(out=CVB[:, :], in_=CV[:, :]),
                     reads=[("CV", cb)], writes=[("CVB", cb)])
                for hq in range(2):
                    half = self.nxt("T", 2)
                    pt = self.PT_bf[half]
                    for i in range(4):
                        c = hq * 4 + i
                        P.op("pe", lambda e, pt=pt, CKB=CKB, c=c, i=i: e.transpose(
                            out=pt[:, i * 128:(i + 1) * 128], in_=CKB[:, c * 128:(c + 1) * 128], identity=self.IDB[:, :]),
                            reads=[("CKB", cb), "IDB"], writes=[("PS", 6 + half)])
                    src = pt.rearrange("p (a b) -> p a b", b=128)
                    dst = KTS[:, hq * 4:hq * 4 + 4, :]
                    if hq == 0:
                        P.op("act", lambda e, src=src, dst=dst: e.activation(out=dst, in_=src, func=AF.Copy),
                             writes=[("PS", 6 + half), ("KTS", cb, hq)])
                    else:
                        P.op("dve", lambda e, src=src, dst=dst: e.tensor_copy(out=dst, in_=src),
                             writes=[("PS", 6 + half), ("KTS", cb, hq)])
                npos = 128
            else:
                npos = 32
            for c in range(8):
                if new:
                    lhs = self.KT[:, c, 1024:1056]
                    rk = [("KT", c // 2, 8)]
                else:
                    lhs = KTS[:, c, :]
                    rk = [("KTS", s["cb"], c // 4)]
                P.op("pe", lambda e, ps=ps, lhs=lhs, c=c, npos=npos: e.matmul(
                    ps[0:npos, c * 64:(c + 1) * 64], lhsT=lhs, rhs=self.QBD[:, c, :], start=True, stop=True),
                    reads=rk + ["QBD"], writes=[("PS", sb_)])

        def stageB(s):
            b, r, new, sb_, ex = s["b"], s["r"], s["new"], s["sb"], s["ex"]
            ps = self.PS[sb_]
            npos = 32 if new else 128
            EXS, PTS = self.EXS[ex], self.PTS[ex]
            P.op("act", lambda e, EXS=EXS, ps=ps, npos=npos: e.activation(out=EXS[0:npos, :], in_=ps[0:npos, :], func=AF.Exp, scale=0.125),
                 writes=[("PS", sb_), ("EXS", ex)])
            if new:
                mk = self.MASKN[0:32, :].unsqueeze(1).to_broadcast([32, 16, 32])
                mkey = "MASKN"
            else:
                mk = self.MASKS[:, b * 12 + r, :].unsqueeze(1).to_broadcast([128, 16, 32])
                mkey = "MASKS"
            P.op("dve", lambda e, PTS=PTS, EXS=EXS, mk=mk, npos=npos: e.tensor_tensor(
                out=PTS[0:npos], in0=EXS[0:npos, :].rearrange("p (h q) -> p h q", q=32), in1=mk, op=ALU.mult),
                reads=[("EXS", ex), mkey], writes=[("PTS", ex)])

        def stageC(s, si):
            new, ex = s["new"], s["ex"]
            npos = 32 if new else 128
            PTS = self.PTS[ex]
            first, last = si == 0, si == nst - 1
            for c in range(8):
                if new:
                    lhs = self.VS[0:32, c * 128:(c + 1) * 128]
                    rk = [("VS", c // 4)]
                else:
                    lhs = self.CVB[s["cb"]][:, c * 128:(c + 1) * 128]
                    rk = [("CVB", s["cb"])]
                P.op("pe", lambda e, lhs=lhs, PTS=PTS, c=c, npos=npos, st=(first and c == 0), sp=(last and c == 7): e.matmul(
                    self.PS[NUM][:, c * 64:(c + 1) * 64], lhsT=lhs, rhs=PTS[0:npos, 2 * c:2 * c + 2, :], start=st, stop=sp,
                    skip_group_check=True),
                    reads=rk + [("PTS", ex)], writes=[("PS", NUM)])
            P.op("pe", lambda e, PTS=PTS, npos=npos, first=first, last=last: e.matmul(
                self.PS[DEN][:, :], lhsT=self.ONB[0:npos, :], rhs=PTS[0:npos].rearrange("p h q -> p (h q)"), start=first, stop=last),
                reads=["ONB", ("PTS", ex)], writes=[("PS", DEN)])

        stageA(steps[0])
        for si in range(nst):
            stageB(steps[si])
            if si + 1 < nst:
                stageA(steps[si + 1])
            stageC(steps[si], si)
        P.op("dve", lambda e: e.reciprocal(out=self.RDS[:, :], in_=self.PS[DEN][:, :]), writes=[("PS", DEN), "RDS"])
        for e_ in range(2):
            r0, r1 = e_ * 64, (e_ + 1) * 64
            num = self.PS[NUM][r0:r1, :].rearrange("p (c e q) -> p c e q", e=2, q=32)[:, :, e_, :]
            rds = self.RDS[r0:r1, :].rearrange("p (c e q) -> p c e q", e=2, q=32)[:, :, e_, :]
            P.op("dve", lambda e, num=num, rds=rds, r0=r0, r1=r1: e.tensor_tensor(out=self.CAT[r0:r1, 0:8, 1024:1056], in0=num, in1=rds, op=ALU.mult),
                 reads=["RDS"], writes=[("PS", NUM), ("CAT", 0, 8), ("CAT", 1, 8)])

    def out_proj(self, tile_hook=None):
        P = self.P
        for cbp in range(4):
            rhs_aps, wks = self.w_get_pair()
            for t, (r0, npt) in enumerate(TILES):
                bank = self.nxt("PQ", 6)
                ps = self.PS[bank]
                for kc in range(16):
                    P.op("pe", lambda e, ps=ps, rhs=rhs_aps[kc], kc=kc, r0=r0, npt=npt: e.matmul(
                        ps[0:npt, :].rearrange("p (a b) -> p a b", b=256), lhsT=self.CAT[:, kc, r0:r0 + npt], rhs=rhs,
                        start=(kc == 0), stop=(kc == 15)),
                        reads=wks + [("CAT", kc // 4, t)], writes=[("PS", bank)])
                H = self.H[t]
                P.op("dve", lambda e, ps=ps, H=H, cbp=cbp, npt=npt: e.tensor_tensor(
                    out=H[0:npt, cbp * 512:(cbp + 1) * 512], in0=ps[0:npt, :], in1=H[0:npt, cbp * 512:(cbp + 1) * 512], op=ALU.add),
                    writes=[("PS", bank), ("H", t)])
                if cbp == 3 and tile_hook is not None:
                    tile_hook(t)
            self.w_release(2)

    def make_norm_hook(self, extra=None, delay=1):
        pend = []

        def hook(t):
            pend.append(self.norm_tile(t))
            if extra is not None:
                extra(t)
            if len(pend) > delay:
                pend.pop(0)()

        def flush():
            while pend:
                pend.pop(0)()
        return hook, flush

    def spill_tile(self, t):
        r0, npt = TILES[t]
        H = self.H[t]
        self.P.op("sp", lambda e: e.dma_start(out=self.hsp[r0:r0 + npt, :], in_=H[0:npt, :]),
                  reads=[("H", t)], writes=["hsp"], dsem=("hs", t % 4))

    def mixer(self, pre_normed=False):
        upto = getattr(self, "upto", None)
        steps = [("norm", (lambda: self.fence()) if pre_normed else
                  (lambda: (self.rmsnorm_to_cat("ln_mix"), self.spill_h(), self.fence()))),
                 ("setup", self.mixer_setup), ("qk", lambda: (self.proj_qk(), self.exchange_k())),
                 ("v", lambda: (self.proj_v(), self.exchange_v())), ("glu", lambda: (self.proj_glu(), self.exchange_t())),
                 ("cso", self.conv_state_outputs), ("xchg", self.fence),
                 ("conv", lambda: (self.conv_ln(), self.fence())), ("attn", self.attention),
                 ("sattn", lambda: (self.sample_attention(), self.fence())),
                 ("out", lambda: (self.reload_h(), getattr(self, "pre_out", lambda: None)(), self.out_proj(getattr(self, "out_hook", None))))]
        self.alloc_mixer()
        for name, fn in steps:
            fn()
            if upto == name:
                self.w_list = self.w_list[:self.w_issued]
                return

    def build(self):
        if self.stage != "mix":
            self.ffn_plan("ffn1")
        if self.stage == "norm1":
            self.w_list = []
            for kk in ("ffn1_g", "ffn1_u", "ffn1_d"):
                pass
            self.rmsnorm_to_cat("ln_ffn1", load_x=True)
            dbg = self.dram_out("dbg", [128, 16 * NTOK], BF16)
            self.P.op("sp", lambda e: e.dma_start(out=dbg[:, :], in_=self.CAT[:, :, :].rearrange("p a b -> p (a b)")),
                      reads=[("CAT", kq, t) for kq in range(4) for t in range(NT)], dsem="dbg")
        if self.stage == "ffn1":
            self.rmsnorm_to_cat("ln_ffn1", load_x=True)
            self.ffn("ffn1", out_dram=self.y_tok)
        if self.stage == "full":
            self.mixer_plan()
            self.ffn_plan("ffn2")
            self.rmsnorm_to_cat("ln_ffn1", load_x=True)
            self.norm_begin("ln_mix")
            hook, flush = self.make_norm_hook(extra=self.spill_tile)
            self.ffn("ffn1", tile_hook=hook)
            flush()
            hook2, flush2 = self.make_norm_hook()
            started = []

            def out_hook(t):
                if not started:
                    started.append(1)
                hook2(t)
            self.out_hook = out_hook
            self.pre_out = lambda: self.norm_begin("ln_ffn2")
            self.mixer(pre_normed=True)
            flush2()

            def y_hook(t):
                r0, npt = TILES[t]
                H = self.H[t]
                self.P.op("sp", lambda e: e.dma_start(out=self.y_tok[r0:r0 + npt, :], in_=H[0:npt, :]),
                          reads=[("H", t)], dsem=("yo", t % 4))
            self.ffn("ffn2", tile_hook=y_hook)
        if self.stage == "mix":
            self.w_list = []
            self.mixer_plan()
            for t, (r0, npt) in enumerate(TILES):
                H = self.H[t]
                self.P.op("sp", lambda e, H=H, r0=r0, npt=npt: e.dma_start(out=H[0:npt, :], in_=self.x_tok[r0:r0 + npt, :]),
                          writes=[("H", t)], dsem=("xl", t % 4))
            self.mixer()
            for t, (r0, npt) in enumerate(TILES):
                H = self.H[t]
                self.P.op("sp", lambda e, H=H, r0=r0, npt=npt: e.dma_start(out=self.y_tok[r0:r0 + npt, :], in_=H[0:npt, :]),
                          reads=[("H", t)], dsem=("yo", t % 4))
        self.P.finalize()
        self.P.emit()
        return self.nc


def _ident_bf():
    return np.eye(128, dtype=np.float32).astype(ml_dtypes.bfloat16)


def _mult(delta):
    d = np.asarray(delta)
    c = ((d >= 0) & (d <= 128)).astype(np.float32)
    c += ((d >= 0) & (d <= 512) & (d % 4 == 0))
    c += ((d >= 0) & (d <= 2048) & (d % 16 == 0))
    return c


def _const_tables(half):
    bf = ml_dtypes.bfloat16
    t = {}
    t["ident_bf"] = _ident_bf()
    t["ident_f"] = np.eye(128, dtype=np.float32)
    pos = np.zeros((128, NT), np.float32)
    for tt in range(8):
        pos[:, tt] = half * 1024 + tt * 128 + np.arange(128)
    pos[:32, 8] = 16384 + (np.arange(32) % 8)
    inv = (10000.0 ** (-np.arange(32, dtype=np.float32) / 32)).astype(np.float32)
    ang = (pos[:, :, None] * inv[None, None, :]).astype(np.float32)
    t["cos_t"] = np.cos(ang.astype(np.float64)).astype(np.float32).reshape(128, NT * 32)
    t["sin_t"] = np.sin(ang.astype(np.float64)).astype(np.float32).reshape(128, NT * 32)
    k = np.arange(128)[:, None, None]
    q = np.arange(128)[None, None, :]
    d = (np.arange(19) - 3)[None, :, None]
    mg = _mult(d * 128 + q - k)
    t["maskg"] = mg.astype(bf).reshape(128, 19 * 128)
    t["maskc"] = (mg * float(half)).astype(bf).reshape(128, 19 * 128)
    p = np.arange(128)[:, None, None, None, None]
    b = np.arange(4)[None, :, None, None, None]
    sidx = np.arange(12)[None, None, :, None, None]
    b2 = np.arange(4)[None, None, None, :, None]
    tq = np.arange(8)[None, None, None, None, :]
    m_p3 = ((tq == sidx) & (sidx < 8)).astype(np.float32) * np.ones_like(p, dtype=np.float32)
    dlt = 2048 + tq - (128 * (sidx + 4) + p)
    m_rc = (((dlt >= 0) & (dlt <= 128)).astype(np.float32) + ((dlt >= 0) & (dlt <= 512) & (dlt % 4 == 0))) * (sidx >= 8)
    ms = (m_p3 + m_rc) * (b2 == b)
    t["masks"] = ms.astype(bf).reshape(128, 48 * 32)
    kb = (np.arange(32) // 8)[:, None]
    kt = (np.arange(32) % 8)[:, None]
    qb = (np.arange(32) // 8)[None, :]
    qt = (np.arange(32) % 8)[None, :]
    t["maskn"] = (_mult(qt - kt) * (kb == qb)).astype(bf)
    t["flag"] = np.full((128, 1), float(half), np.float32)
    return t


def make_in_maps(inp, ncores=8, stage="full"):
    f32 = lambda a: np.ascontiguousarray(np.asarray(a, dtype=np.float32))
    maps = []
    shared = {}
    if stage == "full":
        for f in ("ffn1", "ffn2"):
            for n in ("gate", "up", "down"):
                shared["%s_w_%s" % (f, n)] = f32(inp["%s_w_%s" % (f, n)][0])
        for k in ("ln_ffn1", "ln_mix", "ln_ffn2"):
            shared[k] = f32(inp[k][0]).reshape(1, D)
    else:
        shared["ln_mix"] = f32(inp["ln_mix"][0]).reshape(1, D)
    shared["w_in"] = f32(inp["w_in"][0])
    shared["w_out"] = f32(inp["w_out"][0])
    shared["q_norm"] = f32(inp["q_norm"][0]).reshape(1, 64)
    shared["k_norm"] = f32(inp["k_norm"][0]).reshape(1, 64)
    shared["conv_dw_w"] = f32(inp["conv_dw_w"][0])
    shared["cpar"] = np.concatenate([f32(inp["conv_dw_b"][0]).reshape(8, 128), f32(inp["conv_ln_g"][0]).reshape(8, 128),
                                     f32(inp["conv_ln_b"][0]).reshape(8, 128)], axis=0)
    tabs = [_const_tables(0), _const_tables(1)]
    for c in range(ncores):
        b, half = c // 2, c % 2
        m = dict(shared)
        m.update(tabs[half])
        xs = f32(inp["x_sample"][4 * c:4 * c + 4]).reshape(32, D)
        m["x_tok"] = np.concatenate([f32(inp["x_prompt"][b, half * 1024:(half + 1) * 1024]), xs], axis=0)
        m["cache_k"] = f32(inp["cache_k"][0, 4 * c:4 * c + 4]).reshape(4, 2048, 1024)
        m["cache_v"] = f32(inp["cache_v"][0, 4 * c:4 * c + 4]).reshape(4, 2048, 1024)
        m["state_conv"] = f32(inp["state_conv"][0, 4 * c:4 * c + 4]).reshape(120, 1024)
        maps.append(m)
    return maps


def assemble(results, ncores=8):
    nb = ncores // 2
    y_p = np.zeros((nb, 2048, D), np.float32)
    y_s = np.zeros((4 * ncores, 8, D), np.float32)
    k_p = np.zeros((1, nb, 2048, 16, 64), np.float32)
    v_p = np.zeros((1, nb, 2048, 16, 64), np.float32)
    c_p = np.zeros((1, nb, 30, 1024), np.float32)
    k_s = np.zeros((1, 4 * ncores, 8, 16, 64), np.float32)
    v_s = np.zeros((1, 4 * ncores, 8, 16, 64), np.float32)
    c_s = np.zeros((1, 4 * ncores, 30, 1024), np.float32)
    for c in range(ncores):
        r = results[c]
        b, half = c // 2, c % 2
        sl = slice(half * 1024, (half + 1) * 1024)
        y = np.asarray(r["y_tok"])
        y_p[b, sl] = y[:1024]
        y_s[4 * c:4 * c + 4] = y[1024:].reshape(4, 8, D)
        nk = np.asarray(r["newk"])
        nv = np.asarray(r["newv"])
        k_p[0, b, sl] = nk[:1024].reshape(1024, 16, 64)
        v_p[0, b, sl] = nv[:1024].reshape(1024, 16, 64)
        k_s[0, 4 * c:4 * c + 4] = nk[1024:].reshape(4, 8, 16, 64)
        v_s[0, 4 * c:4 * c + 4] = nv[1024:].reshape(4, 8, 16, 64)
        if half == 1:
            c_p[0, b] = np.asarray(r["conv_p"])
        c_s[0, 4 * c:4 * c + 4] = np.asarray(r["conv_s"]).reshape(4, 30, 1024)
    return (y_p, y_s, k_p, v_p, c_p, k_s, v_s, c_s)


def kernel(**inputs):
    nc = Builder(stage="full", ncores=8).build()
    maps = make_in_maps(inputs, 8, "full")
    res = run_bass_kernel_spmd(nc, maps, core_ids=list(range(8)))
    return assemble(res.results, 8)
```

```python
import contextlib
import numpy as np
import ml_dtypes
import concourse.bass as bass
import concourse.mybir as mybir
from concourse.bass_utils import run_bass_kernel_spmd

F32 = mybir.dt.float32
BF16 = mybir.dt.bfloat16
ALU = mybir.AluOpType
AF = mybir.ActivationFunctionType
AX = mybir.AxisListType

ENGS = ("pe", "act", "dve", "pool", "sp")

D = 2048
DFF = 5632
NT = 9
NTOK = 1056
EPS = 1e-6
TILES = [(t * 128, 128) for t in range(8)] + [(1024, 32)]
GROUPS = [(0, 512, (0, 1, 2, 3)), (512, 512, (4, 5, 6, 7)), (1024, 32, (8,))]
NSLOT = 4
SLOT_BYTES = 8192


class Op:
    __slots__ = ("eng", "fn", "reads", "writes", "dsem", "pos", "deps", "sig", "cnt", "waits",
                 "dval", "dinc", "name")


class Prog:
    def __init__(self, nc):
        self.nc = nc
        self.ops = []
        self.last_w = {}
        self.readers = {}
        self.dsem_last = {}
        self.dsem_val = {}
        self.last_eng = {}
        self.fence_op = None

    def op(self, eng, fn, reads=(), writes=(), dsem=None, dinc=16, nofence=False, name=""):
        o = Op()
        o.eng, o.fn, o.reads, o.writes, o.dsem, o.name = eng, fn, tuple(reads), tuple(writes), dsem, name
        o.sig, o.cnt, o.waits, o.dval, o.dinc = False, 0, [], 0, dinc
        deps = set()
        for k in o.reads:
            w = self.last_w.get(k)
            if w is not None:
                deps.add(w)
        for k in o.writes:
            w = self.last_w.get(k)
            if w is not None:
                deps.add(w)
            for r in self.readers.get(k, ()):
                deps.add(r)
        if self.fence_op is not None and not nofence:
            deps.add(self.fence_op)
        if dsem is not None:
            p = self.dsem_last.get(dsem)
            if p is not None:
                deps.add(p)
            self.dsem_last[dsem] = o
            self.dsem_val[dsem] = self.dsem_val.get(dsem, 0) + dinc
            o.dval = self.dsem_val[dsem]
        deps.discard(o)
        o.deps = [d for d in deps if not (eng == "pe" and d.eng == "pe" and d.dsem is None)]
        for k in o.writes:
            self.last_w[k] = o
            self.readers[k] = []
        for k in o.reads:
            if k not in o.writes:
                self.readers.setdefault(k, []).append(o)
        self.ops.append(o)
        if dsem is None:
            self.last_eng[eng] = o
        return o

    def fence(self, fn):
        o = self.op("dve", fn, name="fence")
        deps = set(o.deps)
        for e, last in self.last_eng.items():
            if last is not o:
                deps.add(last)
        for k, last in self.dsem_last.items():
            if not (isinstance(k, str) and k.startswith("cc_")):
                deps.add(last)
        deps.discard(o)
        o.deps = list(deps)
        self.fence_op = o
        return o

    def finalize(self):
        per = {e: [] for e in ENGS}
        for o in self.ops:
            o.pos = len(per[o.eng])
            per[o.eng].append(o)
        for e in ENGS:
            known = {x: -1 for x in ENGS}
            kd = {}
            for o in per[e]:
                need = {}
                needd = {}
                for d in o.deps:
                    if d.dsem is not None:
                        if kd.get(d.dsem, 0) < d.dval:
                            needd[d.dsem] = max(needd.get(d.dsem, 0), d.dval)
                    else:
                        if known[d.eng] < d.pos:
                            if d.eng not in need or need[d.eng].pos < d.pos:
                                need[d.eng] = d
                o.waits = []
                for x, d in need.items():
                    d.sig = True
                    known[x] = d.pos
                    o.waits.append(d)
                for s, v in needd.items():
                    kd[s] = v
                    o.waits.append((s, v))
        for e in ENGS:
            c = 0
            for o in per[e]:
                if o.dsem is None and o.sig:
                    c += 1
                    o.cnt = c
        self.per = per

    def emit(self):
        nc = self.nc
        per = self.per
        with contextlib.ExitStack() as st:
            esem = {e: st.enter_context(nc.semaphore("s_" + e)) for e in ENGS}
            dsems = {}
            for k in self.dsem_val:
                dsems[k] = st.enter_context(nc.semaphore("d%d" % len(dsems)))
            block = st.enter_context(nc.Block())

            def run(e, eng):
                for o in per[e]:
                    for w in o.waits:
                        if isinstance(w, tuple):
                            eng.wait_ge(dsems[w[0]], w[1])
                        else:
                            eng.wait_ge(esem[w.eng], w.cnt)
                    ins = o.fn(eng)
                    if o.dsem is not None:
                        ins.then_inc(dsems[o.dsem], o.dinc)
                    elif o.sig:
                        ins.then_inc(esem[e], 1)
                for k, last in self.dsem_last.items():
                    if last.eng == e:
                        eng.wait_ge(dsems[k], last.dval)

            block.tensor(lambda eng: run("pe", eng))
            block.scalar(lambda eng: run("act", eng))
            block.vector(lambda eng: run("dve", eng))
            block.gpsimd(lambda eng: run("pool", eng))
            block.sync(lambda eng: run("sp", eng))


class Builder:
    def __init__(self, stage="full", ncores=8, cc_inc=1):
        self.stage = stage
        self.ncores = ncores
        self.cc_inc = cc_inc
        nc = bass.Bass("TRN2", target_bir_lowering=False)
        self.nc = nc
        self.P = Prog(nc)
        self.sb_off = 16384
        self.cnt = {}
        self.declare_io()
        self.alloc_common()

    def dram_in(self, name, shape, dt=F32):
        return self.nc.dram_tensor(name, list(shape), dt, kind="ExternalInput").ap()

    def dram_out(self, name, shape, dt=F32):
        return self.nc.dram_tensor(name, list(shape), dt, kind="ExternalOutput").ap()

    def sb_at(self, name, shape, dt, off):
        return self.nc.alloc_sbuf_tensor_at(name, list(shape), dt, offset=off)

    def sb(self, name, shape, dt):
        nbytes = int(np.prod(shape[1:])) * (4 if dt == F32 else 2)
        t = self.sb_at(name, shape, dt, self.sb_off)
        self.sb_off += (nbytes + 63) // 64 * 64
        assert self.sb_off <= 224 * 1024, (name, self.sb_off)
        return t

    def nxt(self, key, mod):
        v = self.cnt.get(key, 0)
        self.cnt[key] = v + 1
        return v % mod

    def declare_io(self):
        di = self.dram_in
        st = self.stage
        self.x_tok = di("x_tok", [NTOK, D])
        self.wts = {}
        ffns = {"norm1": ("ffn1",), "ffn1": ("ffn1",), "full": ("ffn1", "ffn2"), "mix": ()}[st]
        lns = {"norm1": ("ln_ffn1",), "ffn1": ("ln_ffn1",), "full": ("ln_ffn1", "ln_mix", "ln_ffn2"), "mix": ("ln_mix",)}[st]
        for f in ffns:
            self.wts[f + "_g"] = di(f + "_w_gate", [D, DFF])
            self.wts[f + "_u"] = di(f + "_w_up", [D, DFF])
            self.wts[f + "_d"] = di(f + "_w_down", [DFF, D])
        self.ln = {k: di(k, [1, D]) for k in lns}
        self.ident_bf_d = di("ident_bf", [128, 128], BF16)
        self.y_tok = self.dram_out("y_tok", [NTOK, D])
        if st in ("norm1", "ffn1"):
            return
        self.w_in = di("w_in", [D, 5120])
        self.w_out = di("w_out", [D, D])
        self.q_norm = di("q_norm", [1, 64])
        self.k_norm = di("k_norm", [1, 64])
        self.conv_dw_w = di("conv_dw_w", [31, 1024])
        self.cpar = di("cpar", [24, 128])
        self.cache_k = di("cache_k", [4, 2048, 1024])
        self.cache_v = di("cache_v", [4, 2048, 1024])
        self.state_conv = di("state_conv", [120, 1024])
        self.ident_f_d = di("ident_f", [128, 128])
        self.cos_d = di("cos_t", [128, NT * 32])
        self.sin_d = di("sin_t", [128, NT * 32])
        self.maskg_d = di("maskg", [128, 19 * 128], BF16)
        self.maskc_d = di("maskc", [128, 19 * 128], BF16)
        self.masks_d = di("masks", [128, 48 * 32], BF16)
        self.maskn_d = di("maskn", [32, 32], BF16)
        self.flag_d = di("flag", [128, 1])
        self.newk = self.dram_out("newk", [NTOK, 1024])
        self.newv = self.dram_out("newv", [NTOK, 1024])
        self.conv_p = self.dram_out("conv_p", [30, 1024])
        self.conv_s = self.dram_out("conv_s", [120, 1024])
        nc = self.nc
        self.hsp = nc.dram_tensor("hsp", [NTOK, D], F32, kind="Internal").ap()
        self.xin_v = nc.dram_tensor("xin_v", [1024, 1024], BF16, kind="Internal").ap()
        self.xin_k = nc.dram_tensor("xin_k", [1024, 1024], BF16, kind="Internal").ap()
        self.xin_t = nc.dram_tensor("xin_t", [32, 1024], BF16, kind="Internal").ap()
        self.xout_v = nc.dram_tensor("xout_v", [2048, 1024], BF16, kind="Internal").ap()
        self.xout_k = nc.dram_tensor("xout_k", [2048, 1024], BF16, kind="Internal").ap()
        self.xout_t = nc.dram_tensor("xout_t", [64, 1024], BF16, kind="Internal").ap()

    def alloc_common(self):
        sb = self.sb
        self.CAT = sb("CAT", [128, 16, NTOK], BF16)
        self.RINGALL = sb("RINGALL", [128, NSLOT * (SLOT_BYTES // 2)], BF16)
        self.RING = [self.RINGALL[:, i * (SLOT_BYTES // 2):(i + 1) * (SLOT_BYTES // 2)] for i in range(NSLOT)]
        self.N0 = self.sb_off
        self.GB = sb("GB", [128, D], F32)
        self.XN = [sb("XN%d" % i, [128, D], BF16) for i in range(2)]
        self.SQ = sb("SQ", [128, D], BF16)
        self.IDB = sb("IDB", [128, 128], BF16)
        self.ONB = sb("ONB", [128, 128], BF16)
        self.EPSB = sb("EPSB", [128, 1], F32)
        self.SS = sb("SS", [128, 16], F32)
        self.RS = sb("RS", [128, 16], F32)
        self.FDUM = sb("FDUM", [128, 2], F32)
        self.R0 = self.sb_off
        off = self.R0
        self.H = []
        for t in range(NT):
            self.H.append(self.sb_at("H%d" % t, [128, D], F32, off))
            off += D * 4
        self.AT = self.sb_at("AT", [128, 8, NTOK], BF16, off)
        off += 8 * NTOK * 2
        self.SIL = []
        for i in range(2):
            self.SIL.append(self.sb_at("SIL%d" % i, [128, 512], F32, off))
            off += 2048
        assert off <= 224 * 1024, off
        self.R_end_ffn = off
        self.PP = [self.nc.alloc_psum_tensor("PP%d" % i, [128, 1024], F32) for i in range(4)]
        self.PS = []
        for i in range(4):
            self.PS.append(self.PP[i][:, 0:512])
            self.PS.append(self.PP[i][:, 512:1024])
        ppb = self.PP[3].bitcast(BF16)
        self.PT_bf = [ppb[:, 0:512], ppb[:, 1024:1536]]

        P = self.P
        P.op("sp", lambda e: e.dma_start(out=self.IDB[:], in_=self.ident_bf_d[:, :]), writes=["IDB"], dsem="c0")
        P.op("dve", lambda e: e.memset(self.ONB[:], 1.0), writes=["ONB"])
        P.op("dve", lambda e: e.memset(self.EPSB[:], EPS), writes=["EPSB"])
        self.w_list = []
        self.w_issued = 0
        self.w_next = 0
        self.w_done = 0

    def w_plan(self, aps):
        self.w_list.extend(aps)

    def w_pump(self):
        while self.w_issued < min(len(self.w_list), self.w_done + NSLOT):
            j = self.w_issued
            ap, shape = self.w_list[j]
            slot = self.RING[j % NSLOT]
            n = int(np.prod(shape[1:]))
            if len(shape) == 3:
                dst = slot[:, 0:n].rearrange("p (a b) -> p a b", b=shape[2])
            else:
                dst = slot[:, 0:n]
            self.P.op("pool", lambda e, dst=dst, ap=ap: e.dma_start(out=dst, in_=ap),
                      writes=[("RING", j % NSLOT)], dsem=("ring", j % NSLOT), nofence=True)
            self.w_issued += 1

    def w_get(self):
        i = self.w_next
        self.w_next += 1
        self.w_pump()
        assert i < self.w_issued, (i, self.w_issued, self.w_done)
        return self.RING[i % NSLOT], ("RING", i % NSLOT)

    def w_get_pair(self):
        i = self.w_next
        assert i % 2 == 0
        _, k0 = self.w_get()
        _, k1 = self.w_get()
        base = (i % NSLOT) * (SLOT_BYTES // 2)
        aps = [bass.AP(self.RINGALL, base + kc * 256, [[NSLOT * (SLOT_BYTES // 2), 128], [SLOT_BYTES // 2, 2], [1, 256]])
               for kc in range(16)]
        return aps, [k0, k1]

    def w_release(self, n=1):
        self.w_done += n
        self.w_pump()

    def rmsnorm_to_cat(self, gname, load_x=False):
        P = self.P
        P.op("sp", lambda e: e.dma_start(out=self.GB[:], in_=self.ln[gname].partition_broadcast(128)),
             writes=["GB"], dsem="gb")
        for t, (r0, npt) in enumerate(TILES):
            H = self.H[t]
            if load_x:
                P.op("sp", lambda e, H=H, r0=r0, npt=npt: e.dma_start(out=H[0:npt, :], in_=self.x_tok[r0:r0 + npt, :]),
                     writes=[("H", t)], dsem=("xl", t % 4))
            P.op("act", lambda e, H=H, npt=npt, t=t: e.activation(out=self.SQ[0:npt, :], in_=H[0:npt, :], func=AF.Square,
                                                                accum_out=self.SS[0:npt, t:t + 1]),
                 reads=[("H", t)], writes=["SQ", ("SS", t)])
            P.op("act", lambda e, npt=npt, t=t: e.activation(out=self.RS[0:npt, t:t + 1], in_=self.SS[0:npt, t:t + 1], func=AF.Sqrt,
                                                             scale=1.0 / D, bias=self.EPSB[0:npt, :]),
                 reads=[("SS", t), "EPSB"], writes=[("RS", t)])
            P.op("dve", lambda e, npt=npt, t=t: e.reciprocal(out=self.RS[0:npt, t:t + 1], in_=self.RS[0:npt, t:t + 1]),
                 reads=[("RS", t)], writes=[("RS", t)])
            xn = self.XN[t % 2]
            P.op("dve", lambda e, H=H, xn=xn, npt=npt, t=t: e.scalar_tensor_tensor(
                out=xn[0:npt, :], in0=H[0:npt, :], scalar=self.RS[0:npt, t:t + 1], in1=self.GB[0:npt, :],
                op0=ALU.mult, op1=ALU.mult),
                reads=[("H", t), ("RS", t), "GB"], writes=[("XN", t % 2)])
            for kq in range(4):
                half = self.nxt("T", 2)
                pt = self.PT_bf[half]
                for i in range(4):
                    kc = kq * 4 + i
                    P.op("pe", lambda e, pt=pt, xn=xn, kc=kc, i=i, npt=npt: e.transpose(
                        out=pt[:, i * 128:i * 128 + npt], in_=xn[0:npt, kc * 128:(kc + 1) * 128],
                        identity=self.IDB[0:npt, 0:npt]),
                        reads=[("XN", t % 2), "IDB"], writes=[("PS", 6 + half)])
                src = pt.rearrange("p (a b) -> p a b", b=128)[:, :, 0:npt]
                dst = self.CAT[:, kq * 4:kq * 4 + 4, r0:r0 + npt]
                if kq % 2 == 0:
                    P.op("act", lambda e, src=src, dst=dst: e.activation(out=dst, in_=src, func=AF.Copy),
                         writes=[("PS", 6 + half), ("CAT", kq, t)])
                else:
                    P.op("dve", lambda e, src=src, dst=dst: e.tensor_copy(out=dst, in_=src),
                         writes=[("PS", 6 + half), ("CAT", kq, t)])

    def ffn_plan(self, f):
        wg = self.wts[f + "_g"].rearrange("(kc p) c -> p kc c", p=128)
        wu = self.wts[f + "_u"].rearrange("(kc p) c -> p kc c", p=128)
        wd = self.wts[f + "_d"].rearrange("(j p) c -> p j c", p=128)
        lst = []
        j0 = 0
        for g in range(6):
            J = 8 if g < 5 else 4
            for jp in range(J // 2):
                c0 = (j0 + 2 * jp) * 128
                lst.append((wg[:, :, c0:c0 + 256], [128, 16, 256]))
                lst.append((wu[:, :, c0:c0 + 256], [128, 16, 256]))
            for c in range(4):
                lst.append((wd[:, j0:j0 + J, c * 512:(c + 1) * 512], [128, J, 512]))
            j0 += J
        self.w_plan(lst)

    def ffn(self, f, out_dram=None):
        P = self.P
        for g in range(6):
            J = 8 if g < 5 else 4
            for jp in range(J // 2):
                sg, kg = self.w_get()
                su, ku = self.w_get()
                vg = sg[:, :].rearrange("p (a b) -> p a b", b=256)
                vu = su[:, :].rearrange("p (a b) -> p a b", b=256)
                for s in range(2):
                    j = 2 * jp + s
                    for (n0, N, tiles) in GROUPS:
                        pb = self.nxt("GU", 2)
                        psg, psu = self.PS[2 * pb], self.PS[2 * pb + 1]
                        for (ps, v, kw, bank) in ((psg, vg, kg, 2 * pb), (psu, vu, ku, 2 * pb + 1)):
                            for kc in range(16):
                                P.op("pe", lambda e, ps=ps, v=v, kc=kc, s=s, n0=n0, N=N: e.matmul(
                                    ps[:, 0:N], lhsT=v[:, kc, s * 128:(s + 1) * 128], rhs=self.CAT[:, kc, n0:n0 + N],
                                    start=(kc == 0), stop=(kc == 15)),
                                    reads=[kw] + [("CAT", kc // 4, t) for t in tiles], writes=[("PS", bank)])
                        sb_ = self.nxt("SIL", 2)
                        sil = self.SIL[sb_]
                        P.op("act", lambda e, sil=sil, psg=psg, N=N: e.activation(out=sil[:, 0:N], in_=psg[:, 0:N], func=AF.Silu),
                             writes=[("PS", 2 * pb), ("SIL", sb_)])
                        P.op("dve", lambda e, sil=sil, psu=psu, j=j, n0=n0, N=N: e.tensor_tensor(
                            out=self.AT[:, j, n0:n0 + N], in0=sil[:, 0:N], in1=psu[:, 0:N], op=ALU.mult),
                            reads=[("SIL", sb_)], writes=[("PS", 2 * pb + 1)] + [("AT", j, t) for t in tiles])
                self.w_release(2)
            for c in range(4):
                sd, kd = self.w_get()
                vd = sd[:, 0:J * 512].rearrange("p (a b) -> p a b", b=512)
                for t, (r0, npt) in enumerate(TILES):
                    db = 4 + self.nxt("D", 2)
                    psd = self.PS[db]
                    for j in range(J):
                        P.op("pe", lambda e, psd=psd, vd=vd, j=j, r0=r0, npt=npt, st=(j == 0), sp=(j == J - 1): e.matmul(
                            psd[0:npt, :], lhsT=self.AT[:, j, r0:r0 + npt], rhs=vd[:, j, :],
                            start=st, stop=sp),
                            reads=[kd, ("AT", j, t)], writes=[("PS", db)])
                    H = self.H[t]
                    P.op("dve", lambda e, psd=psd, H=H, c=c, npt=npt: e.scalar_tensor_tensor(
                        out=H[0:npt, c * 512:(c + 1) * 512], in0=psd[0:npt, :], scalar=0.5,
                        in1=H[0:npt, c * 512:(c + 1) * 512], op0=ALU.mult, op1=ALU.add),
                        writes=[("PS", db), ("H", t)])
                    if out_dram is not None and g == 5 and c == 3:
                        P.op("sp", lambda e, H=H, r0=r0, npt=npt: e.dma_start(out=out_dram[r0:r0 + npt, :], in_=H[0:npt, :]),
                             reads=[("H", t)], dsem=("yo", t % 4))
                self.w_release(1)

    def mixer_plan(self):
        w_in = self.w_in.rearrange("(kc p) c -> p kc c", p=128)
        w_out = self.w_out.rearrange("(kc p) c -> p kc c", p=128)
        lst = []
        for cb in range(12):
            lst.append((w_in[:, :, cb * 256:(cb + 1) * 256], [128, 16, 256]))
        for i in range(4):
            lst.append((w_in[:, :, 3072 + i * 256:3072 + (i + 1) * 256], [128, 16, 256]))
            lst.append((w_in[:, :, 4096 + i * 256:4096 + (i + 1) * 256], [128, 16, 256]))
        for cb in range(8):
            lst.append((w_out[:, :, cb * 256:(cb + 1) * 256], [128, 16, 256]))
        self.w_plan(lst)

    def alloc_mixer(self):
        A = [self.R0]

        def al(name, shape, dt, ptr=A):
            nbytes = int(np.prod(shape[1:])) * (4 if dt == F32 else 2)
            t = self.sb_at(name, shape, dt, ptr[0])
            ptr[0] += (nbytes + 63) // 64 * 64
            assert ptr[0] <= 224 * 1024, (name, ptr[0])
            return t
        self.QT = al("QT", [128, 8, NTOK], BF16)
        self.KT = al("KT", [128, 8, NTOK], BF16)
        self.VS = al("VS", [128, 1024], BF16)
        self.IDF = al("IDF", [128, 128], F32)
        self.ONF = al("ONF", [128, 128], F32)
        self.DWT = al("DWT", [128, 8, 31], F32)
        self.CPT = al("CPT", [128, 24], F32)
        self.FLAG = al("FLAG", [128, 1], F32)
        self.UALL_off = A[0]
        self.UALL = al("UALL", [128, 8, 1208], BF16)
        self.UTP = al("UTP", [128, 8, 30], F32)
        self.UTS = al("UTS", [128, 8, 4, 38], F32)
        x0 = A[0]
        M = [x0]
        m = lambda n, sh, dt: al(n, sh, dt, M)
        self.COS = m("COS", [128, NT, 32], F32)
        self.SIN = m("SIN", [128, NT, 32], F32)
        self.GQK = m("GQK", [128, 2, 64], F32)
        self.ROPE = [[m("RA%d" % i, [128, NT, 64], F32), m("RB%d" % i, [128, NT, 64], F32)] for i in range(2)]
        self.SQF = m("SQF", [128, 512], F32)
        self.SS4 = m("SS4", [128, 8], F32)
        self.RS4 = m("RS4", [128, 8], F32)
        self.XS = m("XS", [128, 8, 64], F32)
        self.T2 = m("T2", [128, 8, 64], F32)
        self.OUTF = [m("OUTF%d" % i, [128, 8, 64], F32) for i in range(3)]
        self.XB16 = [m("XB16%d" % i, [128, 512], BF16) for i in range(3)]
        self.VF = [m("VF%d" % i, [128, 512], F32) for i in range(2)]
        self.VB = [m("VB%d" % i, [128, 512], BF16) for i in range(2)]
        self.CPL = m("CPL", [128, 128], F32)
        N = [self.N0]
        n = lambda nm, sh, dt: al(nm, sh, dt, N)
        self.SIG = [n("SIG%d" % i, [128, 512], F32) for i in range(2)]
        self.TMP2 = [n("TMP2%d" % i, [128, 512], F32) for i in range(2)]
        self.SCT = n("SCT", [128, 1024], F32)
        self.DWL = n("DWL", [128, 1024], F32)
        self.OST = n("OST", [128, 1024], F32)
        assert N[0] <= self.N0 + 20480, N[0]
        C = [x0]
        c = lambda nm, sh, dt: al(nm, sh, dt, C)
        self.Y = c("Y", [128, 8, 1176], F32)
        self.HALO = c("HALO", [128, 8, 32], BF16)
        self.YSQ = [c("YSQ%d" % i, [128, 256], F32) for i in range(2)]
        self.MUS = [c("MU%d" % i, [128, 256], F32) for i in range(2)]
        self.VARS = [c("VAR%d" % i, [128, 256], F32) for i in range(2)]
        self.RSTDS = [c("RSTD%d" % i, [128, 256], F32) for i in range(2)]
        self.TN = [c("TN%d" % i, [128, 256], F32) for i in range(2)]
        N2 = [self.N0]
        self.DG = [al("DG%d" % i, [128, 31, 128], BF16, N2) for i in range(2)]
        assert N2[0] <= self.N0 + 20480, N2[0]
        T = [self.UALL_off]
        a = lambda nm, sh, dt: al(nm, sh, dt, T)
        self.KCTX = [a("KCTX%d" % i, [128, 1024], BF16) for i in range(2)]
        self.VSTA = [a("VSTA%d" % i, [128, 16, 128], BF16) for i in range(2)]
        self.VSTB = [a("VSTB%d" % i, [128, 16, 128], BF16) for i in range(2)]
        self.MASKG = a("MASKG", [128, 19, 128], BF16)
        self.MASKC = a("MASKC", [128, 19, 128], BF16)
        self.EX = [a("EX%d" % i, [128, 1024], BF16) for i in range(2)]
        self.PTB2 = [a("PTB%d" % i, [128, 2, 512], BF16) for i in range(2)]
        self.RDEN = [a("RDEN%d" % i, [128, 512], F32) for i in range(2)]
        self.CKB = [a("CKB%d" % i, [128, 1024], BF16) for i in range(3)]
        self.CVB = [a("CVB%d" % i, [128, 1024], BF16) for i in range(3)]
        self.KTS = [a("KTS%d" % i, [128, 8, 128], BF16) for i in range(3)]
        self.EXS = [a("EXS%d" % i, [128, 512], BF16) for i in range(2)]
        self.PTS = [a("PTS%d" % i, [128, 16, 32], BF16) for i in range(2)]
        self.MASKS = a("MASKS", [128, 48, 32], BF16)
        self.MASKN = a("MASKN", [128, 32], BF16)
        self.RDS = a("RDS", [128, 512], F32)
        self.QBD = a("QBD", [128, 8, 64], BF16)

    def fence(self):
        self.P.fence(lambda e: e.memset(self.FDUM[:], 0.0))

    def spill_h(self):
        P = self.P
        for t, (r0, npt) in enumerate(TILES):
            H = self.H[t]
            P.op("sp", lambda e, H=H, r0=r0, npt=npt: e.dma_start(out=self.hsp[r0:r0 + npt, :], in_=H[0:npt, :]),
                 reads=[("H", t)], writes=["hsp"], dsem=("hs", t % 4))

    def reload_h(self):
        P = self.P
        for t, (r0, npt) in enumerate(TILES):
            H = self.H[t]
            P.op("sp", lambda e, H=H, r0=r0, npt=npt: e.dma_start(out=H[0:npt, :], in_=self.hsp[r0:r0 + npt, :]),
                 reads=["hsp"], writes=[("H", t)], dsem=("hs", t % 4))

    def pe_transpose_f32(self, bank, src_ap, rows, cols, key):
        ps = self.PS[bank]
        self.P.op("pe", lambda e, ps=ps, src_ap=src_ap, rows=rows, cols=cols: e.transpose(
            out=ps[0:cols, 0:rows], in_=src_ap, identity=self.IDF[0:rows, 0:rows]),
            reads=["IDF", key], writes=[("PS", bank)])

    def mixer_setup(self):
        P = self.P
        ld = lambda dst, src, key, ds: P.op("sp", lambda e: e.dma_start(out=dst, in_=src), writes=[key], dsem=ds)
        ld(self.IDF[:], self.ident_f_d[:, :], "IDF", "m0")
        ld(self.COS[:].rearrange("p a b -> p (a b)"), self.cos_d[:, :], "COS", "m1")
        ld(self.SIN[:].rearrange("p a b -> p (a b)"), self.sin_d[:, :], "SIN", "m2")
        ld(self.GQK[:, 0, :], self.q_norm.partition_broadcast(128), "GQ", "m3")
        ld(self.GQK[:, 1, :], self.k_norm.partition_broadcast(128), "GK", "m0")
        ld(self.FLAG[:], self.flag_d[:, :], "FLAG", "m1")
        ld(self.SCT[0:120, :], self.state_conv[:, :], "SCT", "m2")
        ld(self.DWL[0:31, :], self.conv_dw_w[:, :], "DWL", "m3")
        ld(self.CPL[0:24, :], self.cpar[:, :], "CPL", "m0")
        P.op("dve", lambda e: e.memset(self.ONF[:], 1.0), writes=["ONF"])
        for i, gk in enumerate(("GQ", "GK")):
            RA, RB = self.ROPE[i]
            g1 = self.GQK[:, i, 0:32].unsqueeze(1).to_broadcast([128, NT, 32])
            g2 = self.GQK[:, i, 32:64].unsqueeze(1).to_broadcast([128, NT, 32])
            P.op("dve", lambda e, RA=RA, g1=g1: e.tensor_tensor(out=RA[:, :, 0:32], in0=self.COS[:], in1=g1, op=ALU.mult),
                 reads=["COS", gk], writes=[("ROPE", i, 0)])
            P.op("dve", lambda e, RA=RA, g2=g2: e.tensor_tensor(out=RA[:, :, 32:64], in0=self.COS[:], in1=g2, op=ALU.mult),
                 reads=["COS", gk], writes=[("ROPE", i, 1)])
            P.op("dve", lambda e, RB=RB, g2=g2: e.scalar_tensor_tensor(out=RB[:, :, 0:32], in0=self.SIN[:], scalar=-1.0, in1=g2,
                                                                       op0=ALU.mult, op1=ALU.mult),
                 reads=["SIN", gk], writes=[("ROPE", i, 2)])
            P.op("dve", lambda e, RB=RB, g1=g1: e.tensor_tensor(out=RB[:, :, 32:64], in0=self.SIN[:], in1=g1, op=ALU.mult),
                 reads=["SIN", gk], writes=[("ROPE", i, 3)])
        bank = 4
        self.pe_transpose_f32(bank, self.CPL[0:24, :], 24, 128, "CPL")
        P.op("dve", lambda e: e.tensor_copy(out=self.CPT[:, :], in_=self.PS[4][:, 0:24]),
             writes=[("PS", 4), "CPT"])
        for c in range(8):
            bank = 4 + (c % 2)
            P.op("pe", lambda e, c=c, bank=bank: e.transpose(out=self.PS[bank][:, 0:31], in_=self.DWL[0:31, c * 128:(c + 1) * 128],
                                                             identity=self.IDF[0:31, 0:31]),
                 reads=["IDF", "DWL"], writes=[("PS", bank)])
            P.op("dve", lambda e, c=c, bank=bank: e.tensor_copy(out=self.DWT[:, c, :], in_=self.PS[bank][:, 0:31]),
                 writes=[("PS", bank), ("DWT", c)])
        for c in range(8):
            bank = 4 + (c % 2)
            P.op("pe", lambda e, c=c, bank=bank: e.transpose(out=self.PS[bank][:, 0:120], in_=self.SCT[0:120, c * 128:(c + 1) * 128],
                                                             identity=self.IDF[0:120, 0:120]),
                 reads=["IDF", "SCT"], writes=[("PS", bank)])
            src = self.PS[bank][:, 0:120].rearrange("p (b j) -> p b j", j=30)
            dst1 = self.UALL[:, c, 1054:1206].rearrange("p (b j) -> p b j", j=38)[:, :, 0:30]
            P.op("act", lambda e, src=src, dst1=dst1: e.activation(out=dst1, in_=src, func=AF.Copy),
                 writes=[("PS", bank), ("UALL", c, 2)])
            P.op("dve", lambda e, src=src, c=c: e.tensor_copy(out=self.UTS[:, c, :, 0:30], in_=src),
                 writes=[("PS", bank), ("UTS", c)])

    def proj_qk(self):
        P = self.P
        self.pend_tr = []
        for cbp in range(4):
            isk = cbp >= 2
            qi = 1 if isk else 0
            cloc = (cbp % 2) * 512
            dstT = self.KT if isk else self.QT
            rhs_aps, wks = self.w_get_pair()
            RA, RB = self.ROPE[qi]
            for t, (r0, npt) in enumerate(TILES):
                bank = self.nxt("PQ", 6)
                ps = self.PS[bank]
                for kc in range(16):
                    P.op("pe", lambda e, ps=ps, rhs=rhs_aps[kc], kc=kc, r0=r0, npt=npt: e.matmul(
                        ps[0:npt, :].rearrange("p (a b) -> p a b", b=256), lhsT=self.CAT[:, kc, r0:r0 + npt], rhs=rhs,
                        start=(kc == 0), stop=(kc == 15)),
                        reads=wks + [("CAT", kc // 4, t)], writes=[("PS", bank)])
                ps3 = ps[0:npt, :].rearrange("p (h d) -> p h d", d=64)
                P.op("act", lambda e, ps=ps, npt=npt: e.activation(out=self.SQF[0:npt, :], in_=ps[0:npt, :], func=AF.Square),
                     writes=[("PS", bank), "SQF"])
                P.op("dve", lambda e, npt=npt: e.tensor_reduce(out=self.SS4[0:npt, :], in_=self.SQF[0:npt, :].rearrange("p (h d) -> p h d", d=64),
                                                                axis=AX.X, op=ALU.add),
                     reads=["SQF"], writes=["SS4"])
                P.op("act", lambda e, npt=npt: e.activation(out=self.RS4[0:npt, :], in_=self.SS4[0:npt, :], func=AF.Sqrt,
                                                            scale=1.0 / 64, bias=self.EPSB[0:npt, :]),
                     reads=["SS4", "EPSB"], writes=["RS4"])
                P.op("dve", lambda e, npt=npt: e.reciprocal(out=self.RS4[0:npt, :], in_=self.RS4[0:npt, :]),
                     writes=["RS4"])
                P.op("dve", lambda e, ps3=ps3, npt=npt: e.tensor_tensor(
                    out=self.XS[0:npt], in0=ps3, in1=self.RS4[0:npt, :].unsqueeze(2).to_broadcast([npt, 8, 64]), op=ALU.mult),
                    reads=["RS4"], writes=[("PS", bank), "XS"])
                ob = self.nxt("OUTF", 3)
                OF = self.OUTF[ob]
                P.op("dve", lambda e, OF=OF, RA=RA, npt=npt, t=t: e.tensor_tensor(
                    out=OF[0:npt], in0=self.XS[0:npt], in1=RA[0:npt, t, :].unsqueeze(1).to_broadcast([npt, 8, 64]), op=ALU.mult),
                    reads=["XS", ("ROPE", qi, 0), ("ROPE", qi, 1)], writes=[("OUTF", ob)])
                P.op("dve", lambda e, RB=RB, npt=npt, t=t: e.tensor_tensor(
                    out=self.T2[0:npt, :, 0:32], in0=self.XS[0:npt, :, 32:64],
                    in1=RB[0:npt, t, 0:32].unsqueeze(1).to_broadcast([npt, 8, 32]), op=ALU.mult),
                    reads=["XS", ("ROPE", qi, 2)], writes=[("T2", 0)])
                P.op("dve", lambda e, RB=RB, npt=npt, t=t: e.tensor_tensor(
                    out=self.T2[0:npt, :, 32:64], in0=self.XS[0:npt, :, 0:32],
                    in1=RB[0:npt, t, 32:64].unsqueeze(1).to_broadcast([npt, 8, 32]), op=ALU.mult),
                    reads=["XS", ("ROPE", qi, 3)], writes=[("T2", 1)])
                P.op("dve", lambda e, OF=OF, npt=npt: e.tensor_tensor(out=OF[0:npt], in0=OF[0:npt], in1=self.T2[0:npt], op=ALU.add),
                     reads=[("T2", 0), ("T2", 1)], writes=[("OUTF", ob)])
                if isk:
                    P.op("sp", lambda e, OF=OF, r0=r0, npt=npt, cloc=cloc: e.dma_start(
                        out=self.newk[r0:r0 + npt, cloc:cloc + 512], in_=OF[0:npt].rearrange("p h d -> p (h d)")),
                        reads=[("OUTF", ob)], dsem=("ko", self.nxt("ko", 4)))
                xb = self.nxt("XB", 3)
                XB = self.XB16[xb]
                P.op("act", lambda e, XB=XB, OF=OF, npt=npt: e.activation(out=XB[0:npt, :], in_=OF[0:npt].rearrange("p h d -> p (h d)"),
                                                                          func=AF.Copy),
                     reads=[("OUTF", ob)], writes=[("XB", xb)])

                def stage2(XB=XB, xb=xb, npt=npt, r0=r0, t=t, cbp=cbp, isk=isk, dstT=dstT):
                    half = self.nxt("T", 2)
                    pt = self.PT_bf[half]
                    for i in range(4):
                        P.op("pe", lambda e, pt=pt, XB=XB, i=i, npt=npt: e.transpose(
                            out=pt[:, i * 128:i * 128 + npt], in_=XB[0:npt, i * 128:(i + 1) * 128], identity=self.IDB[0:npt, 0:npt]),
                            reads=[("XB", xb), "IDB"], writes=[("PS", 6 + half)])
                    src = pt.rearrange("p (a b) -> p a b", b=128)[:, :, 0:npt]
                    c0 = 4 * (cbp % 2)
                    dst = dstT[:, c0:c0 + 4, r0:r0 + npt]
                    kks = [("KT" if isk else "QT", c0 // 2, t), ("KT" if isk else "QT", c0 // 2 + 1, t)]
                    if self.nxt("evq", 2) == 0:
                        P.op("act", lambda e, src=src, dst=dst: e.activation(out=dst, in_=src, func=AF.Copy),
                             writes=[("PS", 6 + half)] + kks)
                    else:
                        P.op("dve", lambda e, src=src, dst=dst: e.tensor_copy(out=dst, in_=src),
                             writes=[("PS", 6 + half)] + kks)
                self.pend_tr.append(stage2)
                if len(self.pend_tr) > 2:
                    self.pend_tr.pop(0)()
            self.w_release(2)
        while self.pend_tr:
            self.pend_tr.pop(0)()

    def proj_v(self):
        P = self.P
        for cbp in range(2):
            cloc = cbp * 512
            rhs_aps, wks = self.w_get_pair()
            for t, (r0, npt) in enumerate(TILES):
                bank = self.nxt("PQ", 6)
                ps = self.PS[bank]
                for kc in range(16):
                    P.op("pe", lambda e, ps=ps, rhs=rhs_aps[kc], kc=kc, r0=r0, npt=npt: e.matmul(
                        ps[0:npt, :].rearrange("p (a b) -> p a b", b=256), lhsT=self.CAT[:, kc, r0:r0 + npt], rhs=rhs,
                        start=(kc == 0), stop=(kc == 15)),
                        reads=wks + [("CAT", kc // 4, t)], writes=[("PS", bank)])
                vb = self.nxt("VF", 2)
                VF = self.VF[vb]
                P.op("act", lambda e, VF=VF, ps=ps, npt=npt: e.activation(out=VF[0:npt, :], in_=ps[0:npt, :], func=AF.Copy),
                     writes=[("PS", bank), ("VF", vb)])
                P.op("sp", lambda e, VF=VF, r0=r0, npt=npt, cloc=cloc: e.dma_start(out=self.newv[r0:r0 + npt, cloc:cloc + 512], in_=VF[0:npt, :]),
                     reads=[("VF", vb)], dsem=("vo", self.nxt("vo", 4)))
                if t < 8:
                    v2 = self.nxt("VB", 2)
                    VB = self.VB[v2]
                    P.op("dve", lambda e, VB=VB, VF=VF: e.tensor_copy(out=VB[:, :], in_=VF[:, :]),
                         reads=[("VF", vb)], writes=[("VB", v2)])
                    P.op("sp", lambda e, VB=VB, r0=r0, cloc=cloc: e.dma_start(out=self.xin_v[r0:r0 + 128, cloc:cloc + 512], in_=VB[:, :]),
                         reads=[("VB", v2)], writes=["xin_v"], dsem=("xv", self.nxt("xv", 4)))
                else:
                    P.op("dve", lambda e, VF=VF, cloc=cloc: e.tensor_copy(out=self.VS[0:32, cloc:cloc + 512], in_=VF[0:32, :]),
                         reads=[("VF", vb)], writes=[("VS", cbp)])
            self.w_release(2)

    def proj_glu(self):
        P = self.P
        for i in range(4):
            sa, ka = self.w_get()
            sb_, kb = self.w_get()
            va = sa[:, :].rearrange("p (a b) -> p a b", b=256)
            vb = sb_[:, :].rearrange("p (a b) -> p a b", b=256)
            for s in range(2):
                c = 2 * i + s
                for gi, (n0, N, tiles) in enumerate(GROUPS):
                    pb = self.nxt("GU", 2)
                    psa, psb = self.PS[2 * pb], self.PS[2 * pb + 1]
                    for (ps, v, kw, bank) in ((psa, va, ka, 2 * pb), (psb, vb, kb, 2 * pb + 1)):
                        for kc in range(16):
                            P.op("pe", lambda e, ps=ps, v=v, kc=kc, s=s, n0=n0, N=N: e.matmul(
                                ps[:, 0:N], lhsT=v[:, kc, s * 128:(s + 1) * 128], rhs=self.CAT[:, kc, n0:n0 + N],
                                start=(kc == 0), stop=(kc == 15)),
                                reads=[kw] + [("CAT", kc // 4, t) for t in tiles], writes=[("PS", bank)])
                    sg = self.nxt("SIG", 2)
                    SIG, TMP = self.SIG[sg], self.TMP2[sg]
                    P.op("act", lambda e, SIG=SIG, psb=psb, N=N: e.activation(out=SIG[:, 0:N], in_=psb[:, 0:N], func=AF.Tanh, scale=0.5),
                         writes=[("PS", 2 * pb + 1), ("SIG", sg)])
                    P.op("dve", lambda e, SIG=SIG, TMP=TMP, psa=psa, N=N: e.scalar_tensor_tensor(
                        out=TMP[:, 0:N], in0=SIG[:, 0:N], scalar=1.0, in1=psa[:, 0:N], op0=ALU.add, op1=ALU.mult),
                        reads=[("SIG", sg)], writes=[("PS", 2 * pb), ("TMP2", sg)])
                    if gi < 2:
                        dst = self.UALL[:, c, 30 + n0:30 + n0 + N]
                        src = TMP[:, 0:N]
                    else:
                        dst = self.UALL[:, c, 1054:1206].rearrange("p (b j) -> p b j", j=38)[:, :, 30:38]
                        src = TMP[:, 0:32].rearrange("p (b j) -> p b j", j=8)
                    P.op("act", lambda e, dst=dst, src=src: e.activation(out=dst, in_=src, func=AF.Copy, scale=0.5),
                         reads=[("TMP2", sg)], writes=[("UALL", c, gi)])
                    if gi == 1:
                        P.op("dve", lambda e, TMP=TMP, c=c: e.tensor_scalar(out=self.UTP[:, c, :], in0=TMP[:, 482:512], scalar1=0.5, scalar2=None,
                                                                             op0=ALU.mult),
                             reads=[("TMP2", sg)], writes=[("UTP", c)])
                    if gi == 2:
                        P.op("dve", lambda e, src=src, c=c: e.tensor_scalar(out=self.UTS[:, c, :, 30:38], in0=src, scalar1=0.5, scalar2=None,
                                                                             op0=ALU.mult),
                             reads=[("TMP2", sg)], writes=[("UTS", c)])
            self.w_release(2)

    def conv_state_outputs(self):
        P = self.P
        jobs = [(self.conv_p[:, :], [self.UTP[:, c, :] for c in range(8)], [("UTP", c) for c in range(8)])]
        for b in range(4):
            jobs.append((self.conv_s[b * 30:(b + 1) * 30, :], [self.UTS[:, c, b, 8:38] for c in range(8)], [("UTS", c) for c in range(8)]))
        for (dst, srcs, keys) in jobs:
            for hf in range(2):
                bank = 4 + hf
                for i in range(4):
                    c = hf * 4 + i
                    P.op("pe", lambda e, bank=bank, i=i, src=srcs[c]: e.transpose(
                        out=self.PS[bank][0:30, i * 128:(i + 1) * 128], in_=src, identity=self.IDF[:, :]),
                        reads=["IDF", keys[c]], writes=[("PS", bank)])
                if hf == 0:
                    P.op("act", lambda e, bank=bank: e.activation(out=self.OST[0:30, 0:512], in_=self.PS[bank][0:30, :], func=AF.Copy),
                         writes=[("PS", bank), ("OST", 0)])
                else:
                    P.op("dve", lambda e, bank=bank: e.tensor_copy(out=self.OST[0:30, 512:1024], in_=self.PS[bank][0:30, :]),
                         writes=[("PS", bank), ("OST", 1)])
            P.op("sp", lambda e, dst=dst: e.dma_start(out=dst, in_=self.OST[0:30, :]),
                 reads=[("OST", 0), ("OST", 1)], dsem=("co", self.nxt("co", 2)))

    def _allgather(self, nm, src, dst):
        groups = [[2 * i, 2 * i + 1] for i in range(self.ncores // 2)]
        self.P.op("pool", lambda e: e.collective_compute("AllGather", ALU.bypass, replica_groups=groups, ins=[src], outs=[dst]),
                  reads=["xin_" + nm], writes=["xout_" + nm], dsem="cc_" + nm, dinc=self.cc_inc, nofence=True)

    def exchange_k(self):
        P = self.P
        for c in range(8):
            P.op("sp", lambda e, c=c: e.dma_start(out=self.xin_k[c * 128:(c + 1) * 128, :], in_=self.KT[:, c, 0:1024]),
                 reads=[("KT", c // 2, t) for t in range(9)], writes=["xin_k"], dsem=("xk", c % 4))
        self._allgather("k", self.xin_k, self.xout_k)

    def exchange_v(self):
        self._allgather("v", self.xin_v, self.xout_v)

    def exchange_t(self):
        P = self.P
        tail = self.xin_t.rearrange("r (q j) -> (r q) j", j=32).rearrange("(c p) j -> p c j", p=128)
        P.op("sp", lambda e: e.dma_start(out=tail[:, :, :], in_=self.UALL[:, :, 1024:1056]),
             reads=[("UALL", c, 1) for c in range(8)] + [("UALL", c, 2) for c in range(8)], writes=["xin_t"], dsem="xt")
        self._allgather("t", self.xin_t, self.xout_t)

    def conv_ln(self):
        P = self.P
        hsrc = self.xout_t[0:32, :].rearrange("r (q j) -> (r q) j", j=32).rearrange("(c p) j -> p c j", p=128)
        P.op("sp", lambda e: e.dma_start(out=self.HALO[:, :, :], in_=hsrc), reads=["xout_t"], writes=["HALO"], dsem="hl")
        P.op("dve", lambda e: e.tensor_scalar(out=self.UALL[:, :, 0:30], in0=self.HALO[:, :, 0:30], scalar1=self.FLAG[:, 0:1], scalar2=None,
                                              op0=ALU.mult),
             reads=["HALO", "FLAG"], writes=[("UALL", c, 3) for c in range(8)])
        pieces = [(0, 512), (512, 512), (1054, 122)]
        ycol = [0, 512, 1024]
        for c in range(8):
            DG = self.DG[c % 2]
            P.op("dve", lambda e, DG=DG, c=c: e.tensor_tensor(
                out=DG[:, :, :], in0=self.IDB[:, :].unsqueeze(1).to_broadcast([128, 31, 128]),
                in1=self.DWT[:, c, :].unsqueeze(2).to_broadcast([128, 31, 128]), op=ALU.mult),
                reads=["IDB", ("DWT", c)], writes=[("DG", c % 2)])
            for pi, (a0, N) in enumerate(pieces):
                bank = self.nxt("CV", 4)
                ps = self.PS[bank]
                for k in range(31):
                    P.op("pe", lambda e, ps=ps, DG=DG, k=k, c=c, a0=a0, N=N: e.matmul(
                        ps[:, 0:N], lhsT=DG[:, k, :], rhs=self.UALL[:, c, a0 + k:a0 + k + N], start=(k == 0), stop=(k == 30)),
                        reads=[("DG", c % 2)] + [("UALL", c, g) for g in range(4)], writes=[("PS", bank)])
                P.op("act", lambda e, ps=ps, c=c, y0=ycol[pi], N=N: e.activation(
                    out=self.Y[:, c, y0:y0 + N], in_=ps[:, 0:N], func=AF.Identity, bias=self.CPT[:, c:c + 1]),
                    reads=["CPT"], writes=[("PS", bank), ("Y", c, pi)])
        lnp = [(0, 256, 0), (256, 256, 0), (512, 256, 1), (768, 256, 1), (1024, 122, 2)]

        def stats(li):
            y0, N, pi = lnp[li]
            MU, VAR, RSTD = self.MUS[li % 2], self.VARS[li % 2], self.RSTDS[li % 2]
            kx = li % 2
            for c in range(8):
                yb = self.nxt("YSQ", 2)
                YSQ = self.YSQ[yb]
                P.op("act", lambda e, YSQ=YSQ, c=c, y0=y0, N=N: e.activation(out=YSQ[:, 0:N], in_=self.Y[:, c, y0:y0 + N], func=AF.Square),
                     reads=[("Y", c, pi)], writes=[("YSQ", yb)])
                P.op("pe", lambda e, c=c, y0=y0, N=N: e.matmul(self.PS[4][:, 0:N], lhsT=self.ONF[:, :], rhs=self.Y[:, c, y0:y0 + N],
                                                               start=(c == 0), stop=(c == 7)),
                     reads=["ONF", ("Y", c, pi)], writes=[("PS", 4)])
                P.op("pe", lambda e, YSQ=YSQ, c=c, N=N: e.matmul(self.PS[5][:, 0:N], lhsT=self.ONF[:, :], rhs=YSQ[:, 0:N],
                                                                 start=(c == 0), stop=(c == 7)),
                     reads=["ONF", ("YSQ", yb)], writes=[("PS", 5)])
            P.op("dve", lambda e, N=N: e.tensor_scalar(out=MU[:, 0:N], in0=self.PS[4][:, 0:N], scalar1=1.0 / 1024, scalar2=None, op0=ALU.mult),
                 writes=[("PS", 4), ("MU", kx)])
            P.op("dve", lambda e, N=N: e.tensor_tensor(out=VAR[:, 0:N], in0=MU[:, 0:N], in1=MU[:, 0:N], op=ALU.mult),
                 reads=[("MU", kx)], writes=[("VAR", kx)])
            P.op("dve", lambda e, N=N: e.scalar_tensor_tensor(out=VAR[:, 0:N], in0=self.PS[5][:, 0:N], scalar=1.0 / 1024, in1=VAR[:, 0:N],
                                                              op0=ALU.mult, op1=ALU.subtract),
                 writes=[("PS", 5), ("VAR", kx)])
            P.op("act", lambda e, N=N: e.activation(out=RSTD[:, 0:N], in_=VAR[:, 0:N], func=AF.Sqrt, bias=self.EPSB[:, :]),
                 reads=[("VAR", kx), "EPSB"], writes=[("RSTD", kx)])
            P.op("dve", lambda e, N=N: e.reciprocal(out=RSTD[:, 0:N], in_=RSTD[:, 0:N]), writes=[("RSTD", kx)])

        def normalize(li):
            y0, N, pi = lnp[li]
            MU, RSTD = self.MUS[li % 2], self.RSTDS[li % 2]
            kx = li % 2
            for c in range(8):
                tb = self.nxt("TN", 2)
                TN = self.TN[tb]
                P.op("dve", lambda e, TN=TN, c=c, y0=y0, N=N: e.tensor_tensor(out=TN[:, 0:N], in0=self.Y[:, c, y0:y0 + N], in1=MU[:, 0:N],
                                                                             op=ALU.subtract),
                     reads=[("Y", c, pi), ("MU", kx)], writes=[("TN", tb)])
                P.op("dve", lambda e, TN=TN, N=N: e.tensor_tensor(out=TN[:, 0:N], in0=TN[:, 0:N], in1=RSTD[:, 0:N], op=ALU.mult),
                     reads=[("RSTD", kx)], writes=[("TN", tb)])
                if pi < 2:
                    dst = self.CAT[:, 8 + c, y0:y0 + N]
                    src = TN[:, 0:N]
                    tiles = (y0 // 128, y0 // 128 + 1)
                else:
                    dst = self.CAT[:, 8 + c, 1024:1056].rearrange("p (b j) -> p b j", j=8)
                    src = TN[:, 0:152].rearrange("p (b j) -> p b j", j=38)[:, :, 0:8]
                    tiles = (8,)
                P.op("act", lambda e, dst=dst, src=src, c=c: e.activation(out=dst, in_=src, func=AF.Silu,
                                                                          scale=self.CPT[:, 8 + c:9 + c], bias=self.CPT[:, 16 + c:17 + c]),
                     reads=[("TN", tb), "CPT"], writes=[("CAT", (8 + c) // 4, t) for t in tiles])

        stats(0)
        for li in range(len(lnp)):
            if li + 1 < len(lnp):
                stats(li + 1)
            normalize(li)

    def attention(self):
        P = self.P
        ld = lambda dst, src, key, ds: P.op("sp", lambda e: e.dma_start(out=dst, in_=src), writes=[key], dsem=ds)
        ld(self.MASKG[:].rearrange("p a b -> p (a b)"), self.maskg_d[:, :], "MASKG", "m0")
        ld(self.MASKC[:].rearrange("p a b -> p (a b)"), self.maskc_d[:, :], "MASKC", "m1")
        ld(self.MASKS[:].rearrange("p a b -> p (a b)"), self.masks_d[:, :], "MASKS", "m2")
        ld(self.MASKN[0:32, :], self.maskn_d[:, :], "MASKN", "m3")
        for kb in range(2):
            P.op("pool", lambda e, kb=kb: e.memset(self.VSTA[kb][:, :, 64:128], 1.0), writes=[("VONE", 0, kb)])
            P.op("pool", lambda e, kb=kb: e.memset(self.VSTB[kb][:, :, 0:64], 1.0), writes=[("VONE", 1, kb)])

        def loads(c):
            kb = c % 2
            KC = self.KCTX[kb]
            P.op("sp", lambda e, KC=KC, c=c: e.dma_start(out=KC[:, :], in_=self.xout_k[c * 128:(c + 1) * 128, :]),
                 reads=["xout_k"], writes=[("KCTX", kb)], dsem=("kc", kb))
            for e_, VT in ((0, self.VSTA[kb]), (1, self.VSTB[kb])):
                f0 = c * 128 + e_ * 64
                P.op("sp", lambda e, VT=VT, f0=f0, e_=e_: e.dma_start(
                    out=VT[:, 0:8, e_ * 64:(e_ + 1) * 64], in_=self.xout_v[0:1024, f0:f0 + 64].rearrange("(t p) f -> p t f", p=128)),
                    reads=["xout_v"], writes=[("VST", e_, kb, 0)], dsem=("vc", e_, kb))
                P.op("sp", lambda e, VT=VT, f0=f0, e_=e_: e.dma_start(
                    out=VT[:, 8:16, e_ * 64:(e_ + 1) * 64], in_=self.xin_v[0:1024, f0:f0 + 64].rearrange("(t p) f -> p t f", p=128)),
                    reads=["xin_v"], writes=[("VST", e_, kb, 1)], dsem=("vo2", e_, kb))

        steps = []
        for c in range(8):
            for e_ in range(2):
                for qg in range(2):
                    blocks = [(0, j) for j in range(8)] + [(1, j) for j in range(4 * qg + 4)]
                    nd = self.nxt("ND", 2)
                    npair = len(blocks) // 2
                    for bp in range(npair):
                        steps.append(dict(c=c, e_=e_, qg=qg, bp=bp, npair=npair, blk=blocks[2 * bp:2 * bp + 2], nd=nd,
                                          pp=self.nxt("SPP", 2), ex=self.nxt("EX", 2),
                                          pb=[self.nxt("PTB", 2)],
                                          rd=self.nxt("RDEN", 2) if bp == npair - 1 else None,
                                          newc=(e_ == 0 and qg == 0 and bp == 0)))

        def stageA(s):
            c, e_, qg = s["c"], s["e_"], s["qg"]
            kb = c % 2
            KC = self.KCTX[kb]
            r0, r1 = e_ * 64, (e_ + 1) * 64
            q0 = qg * 512
            PPt = self.PP[s["pp"]]
            for h2 in range(2):
                own, j = s["blk"][h2]
                if own:
                    lhs = self.KT[r0:r1, c, j * 128:(j + 1) * 128]
                    rk = [("KT", c // 2, j)]
                else:
                    lhs = KC[r0:r1, j * 128:(j + 1) * 128]
                    rk = [("KCTX", kb)]
                P.op("pe", lambda e, PPt=PPt, h2=h2, lhs=lhs, c=c, r0=r0, r1=r1, q0=q0: e.matmul(
                    PPt[:, h2 * 512:(h2 + 1) * 512], lhsT=lhs, rhs=self.QT[r0:r1, c, q0:q0 + 512], start=True, stop=True),
                    reads=rk + [("QT", c // 2, t) for t in range(4 * qg, 4 * qg + 4)], writes=[("PS", 2 * s["pp"] + h2)])

        def stageB(s):
            PPt = self.PP[s["pp"]]
            EX = self.EX[s["ex"]]
            P.op("act", lambda e, EX=EX, PPt=PPt: e.activation(out=EX[:, :], in_=PPt[:, :], func=AF.Exp, scale=0.125),
                 writes=[("PS", 2 * s["pp"]), ("PS", 2 * s["pp"] + 1), ("EX", s["ex"])])

        def stageC(s):
            c, e_, qg, bp, npair, nd = s["c"], s["e_"], s["qg"], s["bp"], s["npair"], s["nd"]
            kb = c % 2
            VT = (self.VSTA if e_ == 0 else self.VSTB)[kb]
            r0, r1 = e_ * 64, (e_ + 1) * 64
            o0, o1 = (1 - e_) * 64, (2 - e_) * 64
            q0 = qg * 512
            bn = 4 + nd
            EX = self.EX[s["ex"]]
            own, j = s["blk"][0]
            assert s["blk"][1] == (own, j + 1)
            mt, mkey = (self.MASKG, "MASKG") if own else (self.MASKC, "MASKC")
            d0 = (4 * qg - j) if own else (8 + 4 * qg - j)
            mk = bass.AP(mt, (d0 + 3) * 128, [[19 * 128, 128], [-128, 2], [1, 512]])
            pb = s["pb"][0]
            PT = self.PTB2[pb % 2]
            P.op("dve", lambda e, PT=PT, EX=EX, mk=mk: e.tensor_tensor(
                out=PT[:, :, :], in0=EX[:, :].rearrange("p (a b) -> p a b", b=512), in1=mk, op=ALU.mult),
                reads=[("EX", s["ex"]), mkey], writes=[("PTB", pb % 2)])
            for h2 in range(2):
                own, j = s["blk"][h2]
                vt = VT[:, (8 + j) if own else j, :]
                vk = ("VST", e_, kb, 1 if own else 0)
                first = (bp == 0 and h2 == 0)
                last = (bp == npair - 1 and h2 == 1)
                P.op("pe", lambda e, PT=PT, vt=vt, bn=bn, first=first, last=last, h2=h2: e.matmul(
                    self.PS[bn][:, :], lhsT=vt, rhs=PT[:, h2, :], start=first, stop=last),
                    reads=[("PTB", pb % 2), vk, ("VONE", e_, kb)], writes=[("PS", bn)])
            if bp == npair - 1:
                rd = s["rd"]
                RD = self.RDEN[rd]

                def norm(RD=RD, rd=rd, bn=bn, r0=r0, r1=r1, o0=o0, o1=o1, c=c, q0=q0, qg=qg):
                    P.op("dve", lambda e: e.reciprocal(out=RD[r0:r1, :], in_=self.PS[bn][o0:o1, :]),
                         writes=[("PS", bn), ("RDEN", rd)])
                    P.op("dve", lambda e: e.tensor_tensor(
                        out=self.CAT[r0:r1, c, q0:q0 + 512], in0=self.PS[bn][r0:r1, :], in1=RD[r0:r1, :], op=ALU.mult),
                        reads=[("RDEN", rd)], writes=[("PS", bn)] + [("CAT", c // 4, t) for t in range(4 * qg, 4 * qg + 4)])
                pend_norm.append([2, norm])
            for pn in pend_norm[:]:
                if pn[0] == 0:
                    pn[1]()
                    pend_norm.remove(pn)
                else:
                    pn[0] -= 1

        pend_norm = []
        loads(0)
        LOOK = 2
        n = len(steps)
        for i in range(min(LOOK, n)):
            stageA(steps[i])
        for i in range(n):
            s_ = steps[i]
            if s_["newc"] and s_["c"] + 1 < 8:
                loads(s_["c"] + 1)
            stageB(s_)
            if i + LOOK < n:
                stageA(steps[i + LOOK])
            stageC(s_)
        for pn in pend_norm:
            pn[1]()

    def sample_attention(self):
        P = self.P
        NUM, DEN = 2, 3
        P.op("dve", lambda e: e.memset(self.QBD[:], 0.0), writes=["QBD"])
        for e_ in range(2):
            P.op("dve", lambda e, e_=e_: e.tensor_copy(out=self.QBD[e_ * 64:(e_ + 1) * 64, :, e_ * 32:(e_ + 1) * 32],
                                                      in_=self.QT[e_ * 64:(e_ + 1) * 64, :, 1024:1056]),
                 reads=[("QT", cp, 8) for cp in range(4)], writes=["QBD"])
        steps = []
        for (b, r) in [(b, r) for b in range(4) for r in range(12)] + [(-1, -1)]:
            steps.append(dict(b=b, r=r, new=(b < 0), sb=self.nxt("SS", 2), cb=self.nxt("CK", 3) if b >= 0 else None,
                              ex=self.nxt("EXS", 2)))
        nst = len(steps)

        def stageA(s):
            b, r, new, sb_ = s["b"], s["r"], s["new"], s["sb"]
            ps = self.PS[sb_]
            if not new:
                cb = s["cb"]
                CKB, CVB, KTS = self.CKB[cb], self.CVB[cb], self.KTS[cb]
                if r < 8:
                    ksrc = self.cache_k[b].rearrange("(m r) f -> r m f", r=16)[r]
                    vsrc = self.cache_v[b].rearrange("(m r) f -> r m f", r=16)[r]
                else:
                    ksrc = self.cache_k[b, (r + 4) * 128:(r + 5) * 128, :]
                    vsrc = self.cache_v[b, (r + 4) * 128:(r + 5) * 128, :]
                P.op("pool", lambda e, CKB=CKB, ksrc=ksrc: e.dma_start(out=CKB[:, :], in_=ksrc),
                     writes=[("CKB", cb)], dsem=("ck", cb))
                P.op("pool", lambda e, CVB=CVB, vsrc=vsrc: e.dma_start(out=CVB[:, :], in_=vsrc),
                     writes=[("CVB", cb)], dsem=("cv", cb))
                for hq in range(2):
                    half = self.nxt("T", 2)
                    pt = self.PT_bf[half]
                    for i in range(4):
                        c = hq * 4 + i
                        P.op("pe", lambda e, pt=pt, CKB=CKB, c=c, i=i: e.transpose(
                            out=pt[:, i * 128:(i + 1) * 128], in_=CKB[:, c * 128:(c + 1) * 128], identity=self.IDB[:, :]),
                            reads=[("CKB", cb), "IDB"], writes=[("PS", 6 + half)])
                    src = pt.rearrange("p (a b) -> p a b", b=128)
                    dst = KTS[:, hq * 4:hq * 4 + 4, :]
                    if hq == 0:
                        P.op("act", lambda e, src=src, dst=dst: e.activation(out=dst, in_=src, func=AF.Copy),
                             writes=[("PS", 6 + half), ("KTS", cb, hq)])
                    else:
                        P.op("dve", lambda e, src=src, dst=dst: e.tensor_copy(out=dst, in_=src),
                             writes=[("PS", 6 + half), ("KTS", cb, hq)])
                npos = 128
            else:
                npos = 32
            for c in range(8):
                if new:
                    lhs = self.KT[:, c, 1024:1056]
                    rk = [("KT", c // 2, 8)]
                else:
                    lhs = KTS[:, c, :]
                    rk = [("KTS", s["cb"], c // 4)]
                P.op("pe", lambda e, ps=ps, lhs=lhs, c=c, npos=npos: e.matmul(
                    ps[0:npos, c * 64:(c + 1) * 64], lhsT=lhs, rhs=self.QBD[:, c, :], start=True, stop=True),
                    reads=rk + ["QBD"], writes=[("PS", sb_)])

        def stageB(s):
            b, r, new, sb_, ex = s["b"], s["r"], s["new"], s["sb"], s["ex"]
            ps = self.PS[sb_]
            npos = 32 if new else 128
            EXS, PTS = self.EXS[ex], self.PTS[ex]
            P.op("act", lambda e, EXS=EXS, ps=ps, npos=npos: e.activation(out=EXS[0:npos, :], in_=ps[0:npos, :], func=AF.Exp, scale=0.125),
                 writes=[("PS", sb_), ("EXS", ex)])
            if new:
                mk = self.MASKN[0:32, :].unsqueeze(1).to_broadcast([32, 16, 32])
                mkey = "MASKN"
            else:
                mk = self.MASKS[:, b * 12 + r, :].unsqueeze(1).to_broadcast([128, 16, 32])
                mkey = "MASKS"
            P.op("dve", lambda e, PTS=PTS, EXS=EXS, mk=mk, npos=npos: e.tensor_tensor(
                out=PTS[0:npos], in0=EXS[0:npos, :].rearrange("p (h q) -> p h q", q=32), in1=mk, op=ALU.mult),
                reads=[("EXS", ex), mkey], writes=[("PTS", ex)])

        def stageC(s, si):
            new, ex = s["new"], s["ex"]
            npos = 32 if new else 128
            PTS = self.PTS[ex]
            first, last = si == 0, si == nst - 1
            for c in range(8):
                if new:
                    lhs = self.VS[0:32, c * 128:(c + 1) * 128]
                    rk = [("VS", c // 4)]
                else:
                    lhs = self.CVB[s["cb"]][:, c * 128:(c + 1) * 128]
                    rk = [("CVB", s["cb"])]
                P.op("pe", lambda e, lhs=lhs, PTS=PTS, c=c, npos=npos, st=(first and c == 0), sp=(last and c == 7): e.matmul(
                    self.PS[NUM][:, c * 64:(c + 1) * 64], lhsT=lhs, rhs=PTS[0:npos, 2 * c:2 * c + 2, :], start=st, stop=sp,
                    skip_group_check=True),
                    reads=rk + [("PTS", ex)], writes=[("PS", NUM)])
            P.op("pe", lambda e, PTS=PTS, npos=npos, first=first, last=last: e.matmul(
                self.PS[DEN][:, :], lhsT=self.ONB[0:npos, :], rhs=PTS[0:npos].rearrange("p h q -> p (h q)"), start=first, stop=last),
                reads=["ONB", ("PTS", ex)], writes=[("PS", DEN)])

        stageA(steps[0])
        if nst > 1:
            stageA(steps[1])
        for si in range(nst):
            stageB(steps[si])
            stageC(steps[si], si)
            if si + 2 < nst:
                stageA(steps[si + 2])
        P.op("dve", lambda e: e.reciprocal(out=self.RDS[:, :], in_=self.PS[DEN][:, :]), writes=[("PS", DEN), "RDS"])
        for e_ in range(2):
            r0, r1 = e_ * 64, (e_ + 1) * 64
            num = self.PS[NUM][r0:r1, :].rearrange("p (c e q) -> p c e q", e=2, q=32)[:, :, e_, :]
            rds = self.RDS[r0:r1, :].rearrange("p (c e q) -> p c e q", e=2, q=32)[:, :, e_, :]
            P.op("dve", lambda e, num=num, rds=rds, r0=r0, r1=r1: e.tensor_tensor(out=self.CAT[r0:r1, 0:8, 1024:1056], in0=num, in1=rds, op=ALU.mult),
                 reads=["RDS"], writes=[("PS", NUM), ("CAT", 0, 8), ("CAT", 1, 8)])

    def out_proj(self):
        P = self.P
        for cbp in range(4):
            rhs_aps, wks = self.w_get_pair()
            for t, (r0, npt) in enumerate(TILES):
                bank = self.nxt("PQ", 6)
                ps = self.PS[bank]
                for kc in range(16):
                    P.op("pe", lambda e, ps=ps, rhs=rhs_aps[kc], kc=kc, r0=r0, npt=npt: e.matmul(
                        ps[0:npt, :].rearrange("p (a b) -> p a b", b=256), lhsT=self.CAT[:, kc, r0:r0 + npt], rhs=rhs,
                        start=(kc == 0), stop=(kc == 15)),
                        reads=wks + [("CAT", kc // 4, t)], writes=[("PS", bank)])
                H = self.H[t]
                P.op("dve", lambda e, ps=ps, H=H, cbp=cbp, npt=npt: e.tensor_tensor(
                    out=H[0:npt, cbp * 512:(cbp + 1) * 512], in0=ps[0:npt, :], in1=H[0:npt, cbp * 512:(cbp + 1) * 512], op=ALU.add),
                    writes=[("PS", bank), ("H", t)])
            self.w_release(2)

    def mixer(self):
        upto = getattr(self, "upto", None)
        steps = [("norm", lambda: (self.rmsnorm_to_cat("ln_mix"), self.spill_h(), self.fence())),
                 ("setup", self.mixer_setup), ("qk", lambda: (self.proj_qk(), self.exchange_k())),
                 ("v", lambda: (self.proj_v(), self.exchange_v())), ("glu", lambda: (self.proj_glu(), self.exchange_t())),
                 ("cso", self.conv_state_outputs), ("xchg", self.fence),
                 ("conv", lambda: (self.conv_ln(), self.fence())), ("attn", self.attention),
                 ("sattn", lambda: (self.sample_attention(), self.fence())),
                 ("out", lambda: (self.reload_h(), self.out_proj()))]
        self.alloc_mixer()
        for name, fn in steps:
            fn()
            if upto == name:
                self.w_list = self.w_list[:self.w_issued]
                return

    def build(self):
        if self.stage != "mix":
            self.ffn_plan("ffn1")
        if self.stage == "norm1":
            self.w_list = []
            for kk in ("ffn1_g", "ffn1_u", "ffn1_d"):
                pass
            self.rmsnorm_to_cat("ln_ffn1", load_x=True)
            dbg = self.dram_out("dbg", [128, 16 * NTOK], BF16)
            self.P.op("sp", lambda e: e.dma_start(out=dbg[:, :], in_=self.CAT[:, :, :].rearrange("p a b -> p (a b)")),
                      reads=[("CAT", kq, t) for kq in range(4) for t in range(NT)], dsem="dbg")
        if self.stage == "ffn1":
            self.rmsnorm_to_cat("ln_ffn1", load_x=True)
            self.ffn("ffn1", out_dram=self.y_tok)
        if self.stage == "full":
            self.mixer_plan()
            self.ffn_plan("ffn2")
            self.rmsnorm_to_cat("ln_ffn1", load_x=True)
            self.ffn("ffn1")
            self.mixer()
            self.rmsnorm_to_cat("ln_ffn2")
            self.ffn("ffn2", out_dram=self.y_tok)
        if self.stage == "mix":
            self.w_list = []
            self.mixer_plan()
            for t, (r0, npt) in enumerate(TILES):
                H = self.H[t]
                self.P.op("sp", lambda e, H=H, r0=r0, npt=npt: e.dma_start(out=H[0:npt, :], in_=self.x_tok[r0:r0 + npt, :]),
                          writes=[("H", t)], dsem=("xl", t % 4))
            self.mixer()
            for t, (r0, npt) in enumerate(TILES):
                H = self.H[t]
                self.P.op("sp", lambda e, H=H, r0=r0, npt=npt: e.dma_start(out=self.y_tok[r0:r0 + npt, :], in_=H[0:npt, :]),
                          reads=[("H", t)], dsem=("yo", t % 4))
        self.P.finalize()
        self.P.emit()
        return self.nc


def _ident_bf():
    return np.eye(128, dtype=np.float32).astype(ml_dtypes.bfloat16)


def _mult(delta):
    d = np.asarray(delta)
    c = ((d >= 0) & (d <= 128)).astype(np.float32)
    c += ((d >= 0) & (d <= 512) & (d % 4 == 0))
    c += ((d >= 0) & (d <= 2048) & (d % 16 == 0))
    return c


def _const_tables(half):
    bf = ml_dtypes.bfloat16
    t = {}
    t["ident_bf"] = _ident_bf()
    t["ident_f"] = np.eye(128, dtype=np.float32)
    pos = np.zeros((128, NT), np.float32)
    for tt in range(8):
        pos[:, tt] = half * 1024 + tt * 128 + np.arange(128)
    pos[:32, 8] = 16384 + (np.arange(32) % 8)
    inv = (10000.0 ** (-np.arange(32, dtype=np.float32) / 32)).astype(np.float32)
    ang = (pos[:, :, None] * inv[None, None, :]).astype(np.float32)
    t["cos_t"] = np.cos(ang.astype(np.float64)).astype(np.float32).reshape(128, NT * 32)
    t["sin_t"] = np.sin(ang.astype(np.float64)).astype(np.float32).reshape(128, NT * 32)
    k = np.arange(128)[:, None, None]
    q = np.arange(128)[None, None, :]
    d = (np.arange(19) - 3)[None, :, None]
    mg = _mult(d * 128 + q - k)
    t["maskg"] = mg.astype(bf).reshape(128, 19 * 128)
    t["maskc"] = (mg * float(half)).astype(bf).reshape(128, 19 * 128)
    p = np.arange(128)[:, None, None, None, None]
    b = np.arange(4)[None, :, None, None, None]
    sidx = np.arange(12)[None, None, :, None, None]
    b2 = np.arange(4)[None, None, None, :, None]
    tq = np.arange(8)[None, None, None, None, :]
    m_p3 = ((tq == sidx) & (sidx < 8)).astype(np.float32) * np.ones_like(p, dtype=np.float32)
    dlt = 2048 + tq - (128 * (sidx + 4) + p)
    m_rc = (((dlt >= 0) & (dlt <= 128)).astype(np.float32) + ((dlt >= 0) & (dlt <= 512) & (dlt % 4 == 0))) * (sidx >= 8)
    ms = (m_p3 + m_rc) * (b2 == b)
    t["masks"] = ms.astype(bf).reshape(128, 48 * 32)
    kb = (np.arange(32) // 8)[:, None]
    kt = (np.arange(32) % 8)[:, None]
    qb = (np.arange(32) // 8)[None, :]
    qt = (np.arange(32) % 8)[None, :]
    t["maskn"] = (_mult(qt - kt) * (kb == qb)).astype(bf)
    t["flag"] = np.full((128, 1), float(half), np.float32)
    return t


def make_in_maps(inp, ncores=8, stage="full"):
    f32 = lambda a: np.ascontiguousarray(np.asarray(a, dtype=np.float32))
    maps = []
    shared = {}
    if stage == "full":
        for f in ("ffn1", "ffn2"):
            for n in ("gate", "up", "down"):
                shared["%s_w_%s" % (f, n)] = f32(inp["%s_w_%s" % (f, n)][0])
        for k in ("ln_ffn1", "ln_mix", "ln_ffn2"):
            shared[k] = f32(inp[k][0]).reshape(1, D)
    else:
        shared["ln_mix"] = f32(inp["ln_mix"][0]).reshape(1, D)
    shared["w_in"] = f32(inp["w_in"][0])
    shared["w_out"] = f32(inp["w_out"][0])
    shared["q_norm"] = f32(inp["q_norm"][0]).reshape(1, 64)
    shared["k_norm"] = f32(inp["k_norm"][0]).reshape(1, 64)
    shared["conv_dw_w"] = f32(inp["conv_dw_w"][0])
    shared["cpar"] = np.concatenate([f32(inp["conv_dw_b"][0]).reshape(8, 128), f32(inp["conv_ln_g"][0]).reshape(8, 128),
                                     f32(inp["conv_ln_b"][0]).reshape(8, 128)], axis=0)
    tabs = [_const_tables(0), _const_tables(1)]
    for c in range(ncores):
        b, half = c // 2, c % 2
        m = dict(shared)
        m.update(tabs[half])
        xs = f32(inp["x_sample"][4 * c:4 * c + 4]).reshape(32, D)
        m["x_tok"] = np.concatenate([f32(inp["x_prompt"][b, half * 1024:(half + 1) * 1024]), xs], axis=0)
        m["cache_k"] = f32(inp["cache_k"][0, 4 * c:4 * c + 4]).reshape(4, 2048, 1024)
        m["cache_v"] = f32(inp["cache_v"][0, 4 * c:4 * c + 4]).reshape(4, 2048, 1024)
        m["state_conv"] = f32(inp["state_conv"][0, 4 * c:4 * c + 4]).reshape(120, 1024)
        maps.append(m)
    return maps


def assemble(results, ncores=8):
    nb = ncores // 2
    y_p = np.zeros((nb, 2048, D), np.float32)
    y_s = np.zeros((4 * ncores, 8, D), np.float32)
    k_p = np.zeros((1, nb, 2048, 16, 64), np.float32)
    v_p = np.zeros((1, nb, 2048, 16, 64), np.float32)
    c_p = np.zeros((1, nb, 30, 1024), np.float32)
    k_s = np.zeros((1, 4 * ncores, 8, 16, 64), np.float32)
    v_s = np.zeros((1, 4 * ncores, 8, 16, 64), np.float32)
    c_s = np.zeros((1, 4 * ncores, 30, 1024), np.float32)
    for c in range(ncores):
        r = results[c]
        b, half = c // 2, c % 2
        sl = slice(half * 1024, (half + 1) * 1024)
        y = np.asarray(r["y_tok"])
        y_p[b, sl] = y[:1024]
        y_s[4 * c:4 * c + 4] = y[1024:].reshape(4, 8, D)
        nk = np.asarray(r["newk"])
        nv = np.asarray(r["newv"])
        k_p[0, b, sl] = nk[:1024].reshape(1024, 16, 64)
        v_p[0, b, sl] = nv[:1024].reshape(1024, 16, 64)
        k_s[0, 4 * c:4 * c + 4] = nk[1024:].reshape(4, 8, 16, 64)
        v_s[0, 4 * c:4 * c + 4] = nv[1024:].reshape(4, 8, 16, 64)
        if half == 1:
            c_p[0, b] = np.asarray(r["conv_p"])
        c_s[0, 4 * c:4 * c + 4] = np.asarray(r["conv_s"]).reshape(4, 30, 1024)
    return (y_p, y_s, k_p, v_p, c_p, k_s, v_s, c_s)


def kernel(**inputs):
    nc = Builder(stage="full", ncores=8).build()
    maps = make_in_maps(inputs, 8, "full")
    res = run_bass_kernel_spmd(nc, maps, core_ids=list(range(8)))
    return assemble(res.results, 8)
```

```python
import contextlib
import numpy as np
import ml_dtypes
import concourse.bass as bass
import concourse.mybir as mybir
from concourse.bass_utils import run_bass_kernel_spmd

F32 = mybir.dt.float32
BF16 = mybir.dt.bfloat16
ALU = mybir.AluOpType
AF = mybir.ActivationFunctionType
AX = mybir.AxisListType

ENGS = ("pe", "act", "dve", "pool", "sp")

D = 2048
DFF = 5632
NT = 9
NTOK = 1056
EPS = 1e-6
TILES = [(t * 128, 128) for t in range(8)] + [(1024, 32)]
GROUPS = [(0, 512, (0, 1, 2, 3)), (512, 512, (4, 5, 6, 7)), (1024, 32, (8,))]
NSLOT = 4
SLOT_BYTES = 8192


class Op:
    __slots__ = ("eng", "fn", "reads", "writes", "dsem", "pos", "deps", "sig", "cnt", "waits",
                 "dval", "dinc", "name")


class Prog:
    def __init__(self, nc):
        self.nc = nc
        self.ops = []
        self.last_w = {}
        self.readers = {}
        self.dsem_last = {}
        self.dsem_val = {}
        self.last_eng = {}
        self.fence_op = None

    def op(self, eng, fn, reads=(), writes=(), dsem=None, dinc=16, nofence=False, name=""):
        o = Op()
        o.eng, o.fn, o.reads, o.writes, o.dsem, o.name = eng, fn, tuple(reads), tuple(writes), dsem, name
        o.sig, o.cnt, o.waits, o.dval, o.dinc = False, 0, [], 0, dinc
        deps = set()
        for k in o.reads:
            w = self.last_w.get(k)
            if w is not None:
                deps.add(w)
        for k in o.writes:
            w = self.last_w.get(k)
            if w is not None:
                deps.add(w)
            for r in self.readers.get(k, ()):
                deps.add(r)
        if self.fence_op is not None and not nofence:
            deps.add(self.fence_op)
        if dsem is not None:
            p = self.dsem_last.get(dsem)
            if p is not None:
                deps.add(p)
            self.dsem_last[dsem] = o
            self.dsem_val[dsem] = self.dsem_val.get(dsem, 0) + dinc
            o.dval = self.dsem_val[dsem]
        deps.discard(o)
        o.deps = [d for d in deps if not (eng == "pe" and d.eng == "pe" and d.dsem is None)]
        for k in o.writes:
            self.last_w[k] = o
            self.readers[k] = []
        for k in o.reads:
            if k not in o.writes:
                self.readers.setdefault(k, []).append(o)
        self.ops.append(o)
        if dsem is None:
            self.last_eng[eng] = o
        return o

    def fence(self, fn):
        o = self.op("dve", fn, name="fence")
        deps = set(o.deps)
        for e, last in self.last_eng.items():
            if last is not o:
                deps.add(last)
        for k, last in self.dsem_last.items():
            if not (isinstance(k, str) and k.startswith("cc_")):
                deps.add(last)
        deps.discard(o)
        o.deps = list(deps)
        self.fence_op = o
        return o

    def finalize(self):
        per = {e: [] for e in ENGS}
        for o in self.ops:
            o.pos = len(per[o.eng])
            per[o.eng].append(o)
        for e in ENGS:
            known = {x: -1 for x in ENGS}
            kd = {}
            for o in per[e]:
                need = {}
                needd = {}
                for d in o.deps:
                    if d.dsem is not None:
                        if kd.get(d.dsem, 0) < d.dval:
                            needd[d.dsem] = max(needd.get(d.dsem, 0), d.dval)
                    else:
                        if known[d.eng] < d.pos:
                            if d.eng not in need or need[d.eng].pos < d.pos:
                                need[d.eng] = d
                o.waits = []
                for x, d in need.items():
                    d.sig = True
                    known[x] = d.pos
                    o.waits.append(d)
                for s, v in needd.items():
                    kd[s] = v
                    o.waits.append((s, v))
        for e in ENGS:
            c = 0
            for o in per[e]:
                if o.dsem is None and o.sig:
                    c += 1
                    o.cnt = c
        self.per = per

    def emit(self):
        nc = self.nc
        per = self.per
        with contextlib.ExitStack() as st:
            esem = {e: st.enter_context(nc.semaphore("s_" + e)) for e in ENGS}
            dsems = {}
            for k in self.dsem_val:
                dsems[k] = st.enter_context(nc.semaphore("d%d" % len(dsems)))
            block = st.enter_context(nc.Block())

            def run(e, eng):
                for o in per[e]:
                    for w in o.waits:
                        if isinstance(w, tuple):
                            eng.wait_ge(dsems[w[0]], w[1])
                        else:
                            eng.wait_ge(esem[w.eng], w.cnt)
                    ins = o.fn(eng)
                    if o.dsem is not None:
                        ins.then_inc(dsems[o.dsem], o.dinc)
                    elif o.sig:
                        ins.then_inc(esem[e], 1)
                for k, last in self.dsem_last.items():
                    if last.eng == e:
                        eng.wait_ge(dsems[k], last.dval)

            block.tensor(lambda eng: run("pe", eng))
            block.scalar(lambda eng: run("act", eng))
            block.vector(lambda eng: run("dve", eng))
            block.gpsimd(lambda eng: run("pool", eng))
            block.sync(lambda eng: run("sp", eng))


class Builder:
    def __init__(self, stage="full", ncores=8, cc_inc=1):
        self.stage = stage
        self.ncores = ncores
        self.cc_inc = cc_inc
        nc = bass.Bass("TRN2", target_bir_lowering=False)
        self.nc = nc
        self.P = Prog(nc)
        self.sb_off = 16384
        self.cnt = {}
        self.declare_io()
        self.alloc_common()

    def dram_in(self, name, shape, dt=F32):
        return self.nc.dram_tensor(name, list(shape), dt, kind="ExternalInput").ap()

    def dram_out(self, name, shape, dt=F32):
        return self.nc.dram_tensor(name, list(shape), dt, kind="ExternalOutput").ap()

    def sb_at(self, name, shape, dt, off):
        return self.nc.alloc_sbuf_tensor_at(name, list(shape), dt, offset=off)

    def sb(self, name, shape, dt):
        nbytes = int(np.prod(shape[1:])) * (4 if dt == F32 else 2)
        t = self.sb_at(name, shape, dt, self.sb_off)
        self.sb_off += (nbytes + 63) // 64 * 64
        assert self.sb_off <= 224 * 1024, (name, self.sb_off)
        return t

    def nxt(self, key, mod):
        v = self.cnt.get(key, 0)
        self.cnt[key] = v + 1
        return v % mod

    def declare_io(self):
        di = self.dram_in
        st = self.stage
        self.x_tok = di("x_tok", [NTOK, D])
        self.wts = {}
        ffns = {"norm1": ("ffn1",), "ffn1": ("ffn1",), "full": ("ffn1", "ffn2"), "mix": ()}[st]
        lns = {"norm1": ("ln_ffn1",), "ffn1": ("ln_ffn1",), "full": ("ln_ffn1", "ln_mix", "ln_ffn2"), "mix": ("ln_mix",)}[st]
        for f in ffns:
            self.wts[f + "_g"] = di(f + "_w_gate", [D, DFF])
            self.wts[f + "_u"] = di(f + "_w_up", [D, DFF])
            self.wts[f + "_d"] = di(f + "_w_down", [DFF, D])
        self.ln = {k: di(k, [1, D]) for k in lns}
        self.ident_bf_d = di("ident_bf", [128, 128], BF16)
        self.y_tok = self.dram_out("y_tok", [NTOK, D])
        if st in ("norm1", "ffn1"):
            return
        self.w_in = di("w_in", [D, 5120])
        self.w_out = di("w_out", [D, D])
        self.q_norm = di("q_norm", [1, 64])
        self.k_norm = di("k_norm", [1, 64])
        self.conv_dw_w = di("conv_dw_w", [31, 1024])
        self.cpar = di("cpar", [24, 128])
        self.cache_k = di("cache_k", [4, 2048, 1024])
        self.cache_v = di("cache_v", [4, 2048, 1024])
        self.state_conv = di("state_conv", [120, 1024])
        self.ident_f_d = di("ident_f", [128, 128])
        self.cos_d = di("cos_t", [128, NT * 32])
        self.sin_d = di("sin_t", [128, NT * 32])
        self.maskg_d = di("maskg", [128, 19 * 128], BF16)
        self.maskc_d = di("maskc", [128, 19 * 128], BF16)
        self.masks_d = di("masks", [128, 48 * 32], BF16)
        self.maskn_d = di("maskn", [32, 32], BF16)
        self.flag_d = di("flag", [128, 1])
        self.newk = self.dram_out("newk", [NTOK, 1024])
        self.newv = self.dram_out("newv", [NTOK, 1024])
        self.conv_p = self.dram_out("conv_p", [30, 1024])
        self.conv_s = self.dram_out("conv_s", [120, 1024])
        nc = self.nc
        self.hsp = nc.dram_tensor("hsp", [NTOK, D], F32, kind="Internal").ap()
        self.xin_v = nc.dram_tensor("xin_v", [1024, 1024], BF16, kind="Internal").ap()
        self.xin_k = nc.dram_tensor("xin_k", [1024, 1024], BF16, kind="Internal").ap()
        self.xin_t = nc.dram_tensor("xin_t", [32, 1024], BF16, kind="Internal").ap()
        self.xout_v = nc.dram_tensor("xout_v", [2048, 1024], BF16, kind="Internal").ap()
        self.xout_k = nc.dram_tensor("xout_k", [2048, 1024], BF16, kind="Internal").ap()
        self.xout_t = nc.dram_tensor("xout_t", [64, 1024], BF16, kind="Internal").ap()

    def alloc_common(self):
        sb = self.sb
        self.CAT = sb("CAT", [128, 16, NTOK], BF16)
        self.RINGALL = sb("RINGALL", [128, NSLOT * (SLOT_BYTES // 2)], BF16)
        self.RING = [self.RINGALL[:, i * (SLOT_BYTES // 2):(i + 1) * (SLOT_BYTES // 2)] for i in range(NSLOT)]
        self.N0 = self.sb_off
        self.GB = sb("GB", [128, D], F32)
        self.XN = [sb("XN%d" % i, [128, D], BF16) for i in range(2)]
        self.SQ = sb("SQ", [128, D], BF16)
        self.IDB = sb("IDB", [128, 128], BF16)
        self.ONB = sb("ONB", [128, 128], BF16)
        self.EPSB = sb("EPSB", [128, 1], F32)
        self.SS = sb("SS", [128, 16], F32)
        self.RS = sb("RS", [128, 16], F32)
        self.FDUM = sb("FDUM", [128, 2], F32)
        self.R0 = self.sb_off
        off = self.R0
        self.H = []
        for t in range(NT):
            self.H.append(self.sb_at("H%d" % t, [128, D], F32, off))
            off += D * 4
        self.AT = self.sb_at("AT", [128, 8, NTOK], BF16, off)
        off += 8 * NTOK * 2
        self.SIL = []
        for i in range(2):
            self.SIL.append(self.sb_at("SIL%d" % i, [128, 512], F32, off))
            off += 2048
        assert off <= 224 * 1024, off
        self.R_end_ffn = off
        self.PP = [self.nc.alloc_psum_tensor("PP%d" % i, [128, 1024], F32) for i in range(4)]
        self.PS = []
        for i in range(4):
            self.PS.append(self.PP[i][:, 0:512])
            self.PS.append(self.PP[i][:, 512:1024])
        ppb = self.PP[3].bitcast(BF16)
        self.PT_bf = [ppb[:, 0:512], ppb[:, 1024:1536]]

        P = self.P
        P.op("sp", lambda e: e.dma_start(out=self.IDB[:], in_=self.ident_bf_d[:, :]), writes=["IDB"], dsem="c0")
        P.op("dve", lambda e: e.memset(self.ONB[:], 1.0), writes=["ONB"])
        P.op("dve", lambda e: e.memset(self.EPSB[:], EPS), writes=["EPSB"])
        self.w_list = []
        self.w_issued = 0
        self.w_next = 0
        self.w_done = 0

    def w_plan(self, aps):
        self.w_list.extend(aps)

    def w_pump(self):
        while self.w_issued < min(len(self.w_list), self.w_done + NSLOT):
            j = self.w_issued
            ap, shape = self.w_list[j]
            slot = self.RING[j % NSLOT]
            n = int(np.prod(shape[1:]))
            if len(shape) == 3:
                dst = slot[:, 0:n].rearrange("p (a b) -> p a b", b=shape[2])
            else:
                dst = slot[:, 0:n]
            self.P.op("pool", lambda e, dst=dst, ap=ap: e.dma_start(out=dst, in_=ap),
                      writes=[("RING", j % NSLOT)], dsem=("ring", j % NSLOT), nofence=True)
            self.w_issued += 1

    def w_get(self):
        i = self.w_next
        self.w_next += 1
        self.w_pump()
        assert i < self.w_issued, (i, self.w_issued, self.w_done)
        return self.RING[i % NSLOT], ("RING", i % NSLOT)

    def w_get_pair(self):
        i = self.w_next
        assert i % 2 == 0
        _, k0 = self.w_get()
        _, k1 = self.w_get()
        base = (i % NSLOT) * (SLOT_BYTES // 2)
        aps = [bass.AP(self.RINGALL, base + kc * 256, [[NSLOT * (SLOT_BYTES // 2), 128], [SLOT_BYTES // 2, 2], [1, 256]])
               for kc in range(16)]
        return aps, [k0, k1]

    def w_release(self, n=1):
        self.w_done += n
        self.w_pump()

    def rmsnorm_to_cat(self, gname, load_x=False, spill=False):
        P = self.P
        P.op("sp", lambda e: e.dma_start(out=self.GB[:], in_=self.ln[gname].partition_broadcast(128)),
             writes=["GB"], dsem="gb")
        for t, (r0, npt) in enumerate(TILES):
            H = self.H[t]
            if load_x:
                P.op("sp", lambda e, H=H, r0=r0, npt=npt: e.dma_start(out=H[0:npt, :], in_=self.x_tok[r0:r0 + npt, :]),
                     writes=[("H", t)], dsem=("xl", t % 4))
            P.op("act", lambda e, H=H, npt=npt, t=t: e.activation(out=self.SQ[0:npt, :], in_=H[0:npt, :], func=AF.Square,
                                                                accum_out=self.SS[0:npt, t:t + 1]),
                 reads=[("H", t)], writes=["SQ", ("SS", t)])
            P.op("act", lambda e, npt=npt, t=t: e.activation(out=self.RS[0:npt, t:t + 1], in_=self.SS[0:npt, t:t + 1], func=AF.Sqrt,
                                                             scale=1.0 / D, bias=self.EPSB[0:npt, :]),
                 reads=[("SS", t), "EPSB"], writes=[("RS", t)])
            P.op("dve", lambda e, npt=npt, t=t: e.reciprocal(out=self.RS[0:npt, t:t + 1], in_=self.RS[0:npt, t:t + 1]),
                 reads=[("RS", t)], writes=[("RS", t)])
            xn = self.XN[t % 2]
            P.op("dve", lambda e, H=H, xn=xn, npt=npt, t=t: e.scalar_tensor_tensor(
                out=xn[0:npt, :], in0=H[0:npt, :], scalar=self.RS[0:npt, t:t + 1], in1=self.GB[0:npt, :],
                op0=ALU.mult, op1=ALU.mult),
                reads=[("H", t), ("RS", t), "GB"], writes=[("XN", t % 2)])
            if spill:
                P.op("sp", lambda e, H=H, r0=r0, npt=npt: e.dma_start(out=self.hsp[r0:r0 + npt, :], in_=H[0:npt, :]),
                     reads=[("H", t)], writes=["hsp"], dsem=("hs", t % 4))
            for kq in range(4):
                half = self.nxt("T", 2)
                pt = self.PT_bf[half]
                for i in range(4):
                    kc = kq * 4 + i
                    P.op("pe", lambda e, pt=pt, xn=xn, kc=kc, i=i, npt=npt: e.transpose(
                        out=pt[:, i * 128:i * 128 + npt], in_=xn[0:npt, kc * 128:(kc + 1) * 128],
                        identity=self.IDB[0:npt, 0:npt]),
                        reads=[("XN", t % 2), "IDB"], writes=[("PS", 6 + half)])
                src = pt.rearrange("p (a b) -> p a b", b=128)[:, :, 0:npt]
                dst = self.CAT[:, kq * 4:kq * 4 + 4, r0:r0 + npt]
                if kq % 2 == 0:
                    P.op("act", lambda e, src=src, dst=dst: e.activation(out=dst, in_=src, func=AF.Copy),
                         writes=[("PS", 6 + half), ("CAT", kq, t)])
                else:
                    P.op("dve", lambda e, src=src, dst=dst: e.tensor_copy(out=dst, in_=src),
                         writes=[("PS", 6 + half), ("CAT", kq, t)])

    def ffn_plan(self, f):
        wg = self.wts[f + "_g"].rearrange("(kc p) c -> p kc c", p=128)
        wu = self.wts[f + "_u"].rearrange("(kc p) c -> p kc c", p=128)
        wd = self.wts[f + "_d"].rearrange("(j p) c -> p j c", p=128)
        lst = []
        j0 = 0
        for g in range(6):
            J = 8 if g < 5 else 4
            for jp in range(J // 2):
                c0 = (j0 + 2 * jp) * 128
                lst.append((wg[:, :, c0:c0 + 256], [128, 16, 256]))
                lst.append((wu[:, :, c0:c0 + 256], [128, 16, 256]))
            for c in range(4):
                lst.append((wd[:, j0:j0 + J, c * 512:(c + 1) * 512], [128, J, 512]))
            j0 += J
        self.w_plan(lst)

    def ffn(self, f, out_dram=None):
        P = self.P
        for g in range(6):
            J = 8 if g < 5 else 4
            for jp in range(J // 2):
                sg, kg = self.w_get()
                su, ku = self.w_get()
                vg = sg[:, :].rearrange("p (a b) -> p a b", b=256)
                vu = su[:, :].rearrange("p (a b) -> p a b", b=256)
                for s in range(2):
                    j = 2 * jp + s
                    for (n0, N, tiles) in GROUPS:
                        pb = self.nxt("GU", 2)
                        psg, psu = self.PS[2 * pb], self.PS[2 * pb + 1]
                        for (ps, v, kw, bank) in ((psg, vg, kg, 2 * pb), (psu, vu, ku, 2 * pb + 1)):
                            for kc in range(16):
                                P.op("pe", lambda e, ps=ps, v=v, kc=kc, s=s, n0=n0, N=N: e.matmul(
                                    ps[:, 0:N], lhsT=v[:, kc, s * 128:(s + 1) * 128], rhs=self.CAT[:, kc, n0:n0 + N],
                                    start=(kc == 0), stop=(kc == 15)),
                                    reads=[kw] + [("CAT", kc // 4, t) for t in tiles], writes=[("PS", bank)])
                        sb_ = self.nxt("SIL", 2)
                        sil = self.SIL[sb_]
                        P.op("act", lambda e, sil=sil, psg=psg, N=N: e.activation(out=sil[:, 0:N], in_=psg[:, 0:N], func=AF.Silu),
                             writes=[("PS", 2 * pb), ("SIL", sb_)])
                        P.op("dve", lambda e, sil=sil, psu=psu, j=j, n0=n0, N=N: e.tensor_tensor(
                            out=self.AT[:, j, n0:n0 + N], in0=sil[:, 0:N], in1=psu[:, 0:N], op=ALU.mult),
                            reads=[("SIL", sb_)], writes=[("PS", 2 * pb + 1)] + [("AT", j, t) for t in tiles])
                self.w_release(2)
            for c in range(4):
                sd, kd = self.w_get()
                vd = sd[:, 0:J * 512].rearrange("p (a b) -> p a b", b=512)
                for t, (r0, npt) in enumerate(TILES):
                    db = 4 + self.nxt("D", 2)
                    psd = self.PS[db]
                    for j in range(J):
                        P.op("pe", lambda e, psd=psd, vd=vd, j=j, r0=r0, npt=npt, st=(j == 0), sp=(j == J - 1): e.matmul(
                            psd[0:npt, :], lhsT=self.AT[:, j, r0:r0 + npt], rhs=vd[:, j, :],
                            start=st, stop=sp),
                            reads=[kd, ("AT", j, t)], writes=[("PS", db)])
                    H = self.H[t]
                    P.op("dve", lambda e, psd=psd, H=H, c=c, npt=npt: e.scalar_tensor_tensor(
                        out=H[0:npt, c * 512:(c + 1) * 512], in0=psd[0:npt, :], scalar=0.5,
                        in1=H[0:npt, c * 512:(c + 1) * 512], op0=ALU.mult, op1=ALU.add),
                        writes=[("PS", db), ("H", t)])
                    if out_dram is not None and g == 5 and c == 3:
                        P.op("sp", lambda e, H=H, r0=r0, npt=npt: e.dma_start(out=out_dram[r0:r0 + npt, :], in_=H[0:npt, :]),
                             reads=[("H", t)], dsem=("yo", t % 4))
                self.w_release(1)

    def mixer_plan(self):
        w_in = self.w_in.rearrange("(kc p) c -> p kc c", p=128)
        w_out = self.w_out.rearrange("(kc p) c -> p kc c", p=128)
        lst = []
        for cb in range(12):
            lst.append((w_in[:, :, cb * 256:(cb + 1) * 256], [128, 16, 256]))
        for i in range(4):
            lst.append((w_in[:, :, 3072 + i * 256:3072 + (i + 1) * 256], [128, 16, 256]))
            lst.append((w_in[:, :, 4096 + i * 256:4096 + (i + 1) * 256], [128, 16, 256]))
        for cb in range(8):
            lst.append((w_out[:, :, cb * 256:(cb + 1) * 256], [128, 16, 256]))
        self.w_plan(lst)

    def alloc_mixer(self):
        A = [self.R0]

        def al(name, shape, dt, ptr=A):
            nbytes = int(np.prod(shape[1:])) * (4 if dt == F32 else 2)
            t = self.sb_at(name, shape, dt, ptr[0])
            ptr[0] += (nbytes + 63) // 64 * 64
            assert ptr[0] <= 224 * 1024, (name, ptr[0])
            return t
        self.QT = al("QT", [128, 8, NTOK], BF16)
        self.KT = al("KT", [128, 8, NTOK], BF16)
        self.VS = al("VS", [128, 1024], BF16)
        self.IDF = al("IDF", [128, 128], F32)
        self.ONF = al("ONF", [128, 128], F32)
        self.DWT = al("DWT", [128, 8, 31], F32)
        self.CPT = al("CPT", [128, 24], F32)
        self.FLAG = al("FLAG", [128, 1], F32)
        self.UALL_off = A[0]
        self.UALL = al("UALL", [128, 8, 1208], BF16)
        self.UTP = al("UTP", [128, 8, 30], F32)
        self.UTS = al("UTS", [128, 8, 4, 38], F32)
        x0 = A[0]
        M = [x0]
        m = lambda n, sh, dt: al(n, sh, dt, M)
        self.COS = m("COS", [128, NT, 32], F32)
        self.SIN = m("SIN", [128, NT, 32], F32)
        self.GQK = m("GQK", [128, 2, 64], F32)
        self.ROPE = [[m("RA%d" % i, [128, NT, 64], F32), m("RB%d" % i, [128, NT, 64], F32)] for i in range(2)]
        self.SQF = m("SQF", [128, 512], F32)
        self.SS4 = m("SS4", [128, 8], F32)
        self.RS4 = m("RS4", [128, 8], F32)
        self.XS = m("XS", [128, 8, 64], F32)
        self.T2 = m("T2", [128, 8, 64], F32)
        self.OUTF = [m("OUTF%d" % i, [128, 8, 64], F32) for i in range(3)]
        self.XB16 = [m("XB16%d" % i, [128, 512], BF16) for i in range(3)]
        self.VF = [m("VF%d" % i, [128, 512], F32) for i in range(2)]
        self.VB = [m("VB%d" % i, [128, 512], BF16) for i in range(2)]
        self.CPL = m("CPL", [128, 128], F32)
        N = [self.N0]
        n = lambda nm, sh, dt: al(nm, sh, dt, N)
        self.SIG = [n("SIG%d" % i, [128, 512], F32) for i in range(2)]
        self.TMP2 = [n("TMP2%d" % i, [128, 512], F32) for i in range(2)]
        self.SCT = n("SCT", [128, 1024], F32)
        self.DWL = n("DWL", [128, 1024], F32)
        self.OST = n("OST", [128, 1024], F32)
        assert N[0] <= self.N0 + 20480, N[0]
        C = [x0]
        c = lambda nm, sh, dt: al(nm, sh, dt, C)
        self.Y = c("Y", [128, 8, 1176], F32)
        self.HALO = c("HALO", [128, 8, 32], BF16)
        self.YSQ = [c("YSQ%d" % i, [128, 256], F32) for i in range(2)]
        self.MUS = [c("MU%d" % i, [128, 256], F32) for i in range(2)]
        self.VARS = [c("VAR%d" % i, [128, 256], F32) for i in range(2)]
        self.RSTDS = [c("RSTD%d" % i, [128, 256], F32) for i in range(2)]
        self.TN = [c("TN%d" % i, [128, 256], F32) for i in range(2)]
        N2 = [self.N0]
        self.DG = [al("DG%d" % i, [128, 31, 128], BF16, N2) for i in range(2)]
        assert N2[0] <= self.N0 + 20480, N2[0]
        T = [self.UALL_off]
        a = lambda nm, sh, dt: al(nm, sh, dt, T)
        self.KCTX = [a("KCTX%d" % i, [128, 1024], BF16) for i in range(2)]
        self.VSTA = [a("VSTA%d" % i, [128, 16, 128], BF16) for i in range(2)]
        self.VSTB = [a("VSTB%d" % i, [128, 16, 128], BF16) for i in range(2)]
        self.MASKG = a("MASKG", [128, 19, 128], BF16)
        self.MASKC = a("MASKC", [128, 19, 128], BF16)
        self.EX = [a("EX%d" % i, [128, 1024], BF16) for i in range(2)]
        self.PTB2 = [a("PTB%d" % i, [128, 2, 512], BF16) for i in range(2)]
        self.RDEN = [a("RDEN%d" % i, [128, 512], F32) for i in range(2)]
        self.CKB = [a("CKB%d" % i, [128, 1024], BF16) for i in range(3)]
        self.CVB = [a("CVB%d" % i, [128, 1024], BF16) for i in range(3)]
        self.KTS = [a("KTS%d" % i, [128, 8, 128], BF16) for i in range(3)]
        self.EXS = [a("EXS%d" % i, [128, 512], BF16) for i in range(2)]
        self.PTS = [a("PTS%d" % i, [128, 16, 32], BF16) for i in range(2)]
        self.MASKS = a("MASKS", [128, 48, 32], BF16)
        self.MASKN = a("MASKN", [128, 32], BF16)
        self.RDS = a("RDS", [128, 512], F32)
        self.QBD = a("QBD", [128, 8, 64], BF16)

    def fence(self):
        self.P.fence(lambda e: e.memset(self.FDUM[:], 0.0))

    def spill_h(self):
        P = self.P
        for t, (r0, npt) in enumerate(TILES):
            H = self.H[t]
            P.op("sp", lambda e, H=H, r0=r0, npt=npt: e.dma_start(out=self.hsp[r0:r0 + npt, :], in_=H[0:npt, :]),
                 reads=[("H", t)], writes=["hsp"], dsem=("hs", t % 4))

    def reload_h(self):
        P = self.P
        for t, (r0, npt) in enumerate(TILES):
            H = self.H[t]
            P.op("sp", lambda e, H=H, r0=r0, npt=npt: e.dma_start(out=H[0:npt, :], in_=self.hsp[r0:r0 + npt, :]),
                 reads=["hsp"], writes=[("H", t)], dsem=("hs", t % 4))

    def pe_transpose_f32(self, bank, src_ap, rows, cols, key):
        ps = self.PS[bank]
        self.P.op("pe", lambda e, ps=ps, src_ap=src_ap, rows=rows, cols=cols: e.transpose(
            out=ps[0:cols, 0:rows], in_=src_ap, identity=self.IDF[0:rows, 0:rows]),
            reads=["IDF", key], writes=[("PS", bank)])

    def mixer_setup(self):
        P = self.P
        ld = lambda dst, src, key, ds: P.op("sp", lambda e: e.dma_start(out=dst, in_=src), writes=[key], dsem=ds)
        ld(self.IDF[:], self.ident_f_d[:, :], "IDF", "m0")
        ld(self.COS[:].rearrange("p a b -> p (a b)"), self.cos_d[:, :], "COS", "m1")
        ld(self.SIN[:].rearrange("p a b -> p (a b)"), self.sin_d[:, :], "SIN", "m2")
        ld(self.GQK[:, 0, :], self.q_norm.partition_broadcast(128), "GQ", "m3")
        ld(self.GQK[:, 1, :], self.k_norm.partition_broadcast(128), "GK", "m0")
        ld(self.FLAG[:], self.flag_d[:, :], "FLAG", "m1")
        ld(self.SCT[0:120, :], self.state_conv[:, :], "SCT", "m2")
        ld(self.DWL[0:31, :], self.conv_dw_w[:, :], "DWL", "m3")
        ld(self.CPL[0:24, :], self.cpar[:, :], "CPL", "m0")
        P.op("dve", lambda e: e.memset(self.ONF[:], 1.0), writes=["ONF"])
        for i, gk in enumerate(("GQ", "GK")):
            RA, RB = self.ROPE[i]
            g1 = self.GQK[:, i, 0:32].unsqueeze(1).to_broadcast([128, NT, 32])
            g2 = self.GQK[:, i, 32:64].unsqueeze(1).to_broadcast([128, NT, 32])
            P.op("dve", lambda e, RA=RA, g1=g1: e.tensor_tensor(out=RA[:, :, 0:32], in0=self.COS[:], in1=g1, op=ALU.mult),
                 reads=["COS", gk], writes=[("ROPE", i, 0)])
            P.op("dve", lambda e, RA=RA, g2=g2: e.tensor_tensor(out=RA[:, :, 32:64], in0=self.COS[:], in1=g2, op=ALU.mult),
                 reads=["COS", gk], writes=[("ROPE", i, 1)])
            P.op("dve", lambda e, RB=RB, g2=g2: e.scalar_tensor_tensor(out=RB[:, :, 0:32], in0=self.SIN[:], scalar=-1.0, in1=g2,
                                                                       op0=ALU.mult, op1=ALU.mult),
                 reads=["SIN", gk], writes=[("ROPE", i, 2)])
            P.op("dve", lambda e, RB=RB, g1=g1: e.tensor_tensor(out=RB[:, :, 32:64], in0=self.SIN[:], in1=g1, op=ALU.mult),
                 reads=["SIN", gk], writes=[("ROPE", i, 3)])

    def mixer_setup_pe(self):
        P = self.P
        bank = 4
        self.pe_transpose_f32(bank, self.CPL[0:24, :], 24, 128, "CPL")
        P.op("dve", lambda e: e.tensor_copy(out=self.CPT[:, :], in_=self.PS[4][:, 0:24]),
             writes=[("PS", 4), "CPT"])
        for c in range(8):
            bank = 4 + (c % 2)
            P.op("pe", lambda e, c=c, bank=bank: e.transpose(out=self.PS[bank][:, 0:31], in_=self.DWL[0:31, c * 128:(c + 1) * 128],
                                                             identity=self.IDF[0:31, 0:31]),
                 reads=["IDF", "DWL"], writes=[("PS", bank)])
            P.op("dve", lambda e, c=c, bank=bank: e.tensor_copy(out=self.DWT[:, c, :], in_=self.PS[bank][:, 0:31]),
                 writes=[("PS", bank), ("DWT", c)])
        for c in range(8):
            bank = 4 + (c % 2)
            P.op("pe", lambda e, c=c, bank=bank: e.transpose(out=self.PS[bank][:, 0:120], in_=self.SCT[0:120, c * 128:(c + 1) * 128],
                                                             identity=self.IDF[0:120, 0:120]),
                 reads=["IDF", "SCT"], writes=[("PS", bank)])
            src = self.PS[bank][:, 0:120].rearrange("p (b j) -> p b j", j=30)
            dst1 = self.UALL[:, c, 1054:1206].rearrange("p (b j) -> p b j", j=38)[:, :, 0:30]
            P.op("act", lambda e, src=src, dst1=dst1: e.activation(out=dst1, in_=src, func=AF.Copy),
                 writes=[("PS", bank), ("UALL", c, 2)])
            P.op("dve", lambda e, src=src, c=c: e.tensor_copy(out=self.UTS[:, c, :, 0:30], in_=src),
                 writes=[("PS", bank), ("UTS", c)])

    def proj_qk(self):
        P = self.P
        self.pend_tr = []
        for cbp in range(4):
            isk = cbp >= 2
            qi = 1 if isk else 0
            cloc = (cbp % 2) * 512
            dstT = self.KT if isk else self.QT
            rhs_aps, wks = self.w_get_pair()
            RA, RB = self.ROPE[qi]
            for t, (r0, npt) in enumerate(TILES):
                bank = self.nxt("PQ", 6)
                ps = self.PS[bank]
                for kc in range(16):
                    P.op("pe", lambda e, ps=ps, rhs=rhs_aps[kc], kc=kc, r0=r0, npt=npt: e.matmul(
                        ps[0:npt, :].rearrange("p (a b) -> p a b", b=256), lhsT=self.CAT[:, kc, r0:r0 + npt], rhs=rhs,
                        start=(kc == 0), stop=(kc == 15)),
                        reads=wks + [("CAT", kc // 4, t)], writes=[("PS", bank)])
                ps3 = ps[0:npt, :].rearrange("p (h d) -> p h d", d=64)
                P.op("act", lambda e, ps=ps, npt=npt: e.activation(out=self.SQF[0:npt, :], in_=ps[0:npt, :], func=AF.Square),
                     writes=[("PS", bank), "SQF"])
                P.op("dve", lambda e, npt=npt: e.tensor_reduce(out=self.SS4[0:npt, :], in_=self.SQF[0:npt, :].rearrange("p (h d) -> p h d", d=64),
                                                                axis=AX.X, op=ALU.add),
                     reads=["SQF"], writes=["SS4"])
                P.op("act", lambda e, npt=npt: e.activation(out=self.RS4[0:npt, :], in_=self.SS4[0:npt, :], func=AF.Sqrt,
                                                            scale=1.0 / 64, bias=self.EPSB[0:npt, :]),
                     reads=["SS4", "EPSB"], writes=["RS4"])
                P.op("dve", lambda e, npt=npt: e.reciprocal(out=self.RS4[0:npt, :], in_=self.RS4[0:npt, :]),
                     writes=["RS4"])
                P.op("dve", lambda e, ps3=ps3, npt=npt: e.tensor_tensor(
                    out=self.XS[0:npt], in0=ps3, in1=self.RS4[0:npt, :].unsqueeze(2).to_broadcast([npt, 8, 64]), op=ALU.mult),
                    reads=["RS4"], writes=[("PS", bank), "XS"])
                ob = self.nxt("OUTF", 3)
                OF = self.OUTF[ob]
                P.op("dve", lambda e, OF=OF, RA=RA, npt=npt, t=t: e.tensor_tensor(
                    out=OF[0:npt], in0=self.XS[0:npt], in1=RA[0:npt, t, :].unsqueeze(1).to_broadcast([npt, 8, 64]), op=ALU.mult),
                    reads=["XS", ("ROPE", qi, 0), ("ROPE", qi, 1)], writes=[("OUTF", ob)])
                P.op("dve", lambda e, RB=RB, npt=npt, t=t: e.tensor_tensor(
                    out=self.T2[0:npt, :, 0:32], in0=self.XS[0:npt, :, 32:64],
                    in1=RB[0:npt, t, 0:32].unsqueeze(1).to_broadcast([npt, 8, 32]), op=ALU.mult),
                    reads=["XS", ("ROPE", qi, 2)], writes=[("T2", 0)])
                P.op("dve", lambda e, RB=RB, npt=npt, t=t: e.tensor_tensor(
                    out=self.T2[0:npt, :, 32:64], in0=self.XS[0:npt, :, 0:32],
                    in1=RB[0:npt, t, 32:64].unsqueeze(1).to_broadcast([npt, 8, 32]), op=ALU.mult),
                    reads=["XS", ("ROPE", qi, 3)], writes=[("T2", 1)])
                P.op("dve", lambda e, OF=OF, npt=npt: e.tensor_tensor(out=OF[0:npt], in0=OF[0:npt], in1=self.T2[0:npt], op=ALU.add),
                     reads=[("T2", 0), ("T2", 1)], writes=[("OUTF", ob)])
                if isk:
                    P.op("sp", lambda e, OF=OF, r0=r0, npt=npt, cloc=cloc: e.dma_start(
                        out=self.newk[r0:r0 + npt, cloc:cloc + 512], in_=OF[0:npt].rearrange("p h d -> p (h d)")),
                        reads=[("OUTF", ob)], dsem=("ko", self.nxt("ko", 4)))
                xb = self.nxt("XB", 3)
                XB = self.XB16[xb]
                P.op("act", lambda e, XB=XB, OF=OF, npt=npt: e.activation(out=XB[0:npt, :], in_=OF[0:npt].rearrange("p h d -> p (h d)"),
                                                                          func=AF.Copy),
                     reads=[("OUTF", ob)], writes=[("XB", xb)])

                def stage2(XB=XB, xb=xb, npt=npt, r0=r0, t=t, cbp=cbp, isk=isk, dstT=dstT):
                    half = self.nxt("T", 2)
                    pt = self.PT_bf[half]
                    for i in range(4):
                        P.op("pe", lambda e, pt=pt, XB=XB, i=i, npt=npt: e.transpose(
                            out=pt[:, i * 128:i * 128 + npt], in_=XB[0:npt, i * 128:(i + 1) * 128], identity=self.IDB[0:npt, 0:npt]),
                            reads=[("XB", xb), "IDB"], writes=[("PS", 6 + half)])
                    src = pt.rearrange("p (a b) -> p a b", b=128)[:, :, 0:npt]
                    c0 = 4 * (cbp % 2)
                    dst = dstT[:, c0:c0 + 4, r0:r0 + npt]
                    kks = [("KT" if isk else "QT", c0 // 2, t), ("KT" if isk else "QT", c0 // 2 + 1, t)]
                    if self.nxt("evq", 2) == 0:
                        P.op("act", lambda e, src=src, dst=dst: e.activation(out=dst, in_=src, func=AF.Copy),
                             writes=[("PS", 6 + half)] + kks)
                    else:
                        P.op("dve", lambda e, src=src, dst=dst: e.tensor_copy(out=dst, in_=src),
                             writes=[("PS", 6 + half)] + kks)
                self.pend_tr.append(stage2)
                if len(self.pend_tr) > 2:
                    self.pend_tr.pop(0)()
            self.w_release(2)
        while self.pend_tr:
            self.pend_tr.pop(0)()

    def proj_v(self):
        P = self.P
        for cbp in range(2):
            cloc = cbp * 512
            rhs_aps, wks = self.w_get_pair()
            for t, (r0, npt) in enumerate(TILES):
                bank = self.nxt("PQ", 6)
                ps = self.PS[bank]
                for kc in range(16):
                    P.op("pe", lambda e, ps=ps, rhs=rhs_aps[kc], kc=kc, r0=r0, npt=npt: e.matmul(
                        ps[0:npt, :].rearrange("p (a b) -> p a b", b=256), lhsT=self.CAT[:, kc, r0:r0 + npt], rhs=rhs,
                        start=(kc == 0), stop=(kc == 15)),
                        reads=wks + [("CAT", kc // 4, t)], writes=[("PS", bank)])
                vb = self.nxt("VF", 2)
                VF = self.VF[vb]
                P.op("act", lambda e, VF=VF, ps=ps, npt=npt: e.activation(out=VF[0:npt, :], in_=ps[0:npt, :], func=AF.Copy),
                     writes=[("PS", bank), ("VF", vb)])
                P.op("sp", lambda e, VF=VF, r0=r0, npt=npt, cloc=cloc: e.dma_start(out=self.newv[r0:r0 + npt, cloc:cloc + 512], in_=VF[0:npt, :]),
                     reads=[("VF", vb)], dsem=("vo", self.nxt("vo", 4)))
                if t < 8:
                    v2 = self.nxt("VB", 2)
                    VB = self.VB[v2]
                    P.op("dve", lambda e, VB=VB, VF=VF: e.tensor_copy(out=VB[:, :], in_=VF[:, :]),
                         reads=[("VF", vb)], writes=[("VB", v2)])
                    P.op("sp", lambda e, VB=VB, r0=r0, cloc=cloc: e.dma_start(out=self.xin_v[r0:r0 + 128, cloc:cloc + 512], in_=VB[:, :]),
                         reads=[("VB", v2)], writes=["xin_v"], dsem=("xv", self.nxt("xv", 4)))
                else:
                    P.op("dve", lambda e, VF=VF, cloc=cloc: e.tensor_copy(out=self.VS[0:32, cloc:cloc + 512], in_=VF[0:32, :]),
                         reads=[("VF", vb)], writes=[("VS", cbp)])
            self.w_release(2)

    def proj_glu(self):
        P = self.P
        for i in range(4):
            sa, ka = self.w_get()
            sb_, kb = self.w_get()
            va = sa[:, :].rearrange("p (a b) -> p a b", b=256)
            vb = sb_[:, :].rearrange("p (a b) -> p a b", b=256)
            for s in range(2):
                c = 2 * i + s
                for gi, (n0, N, tiles) in enumerate(GROUPS):
                    pb = self.nxt("GU", 2)
                    psa, psb = self.PS[2 * pb], self.PS[2 * pb + 1]
                    for (ps, v, kw, bank) in ((psa, va, ka, 2 * pb), (psb, vb, kb, 2 * pb + 1)):
                        for kc in range(16):
                            P.op("pe", lambda e, ps=ps, v=v, kc=kc, s=s, n0=n0, N=N: e.matmul(
                                ps[:, 0:N], lhsT=v[:, kc, s * 128:(s + 1) * 128], rhs=self.CAT[:, kc, n0:n0 + N],
                                start=(kc == 0), stop=(kc == 15)),
                                reads=[kw] + [("CAT", kc // 4, t) for t in tiles], writes=[("PS", bank)])
                    sg = self.nxt("SIG", 2)
                    SIG, TMP = self.SIG[sg], self.TMP2[sg]
                    P.op("act", lambda e, SIG=SIG, psb=psb, N=N: e.activation(out=SIG[:, 0:N], in_=psb[:, 0:N], func=AF.Tanh, scale=0.5),
                         writes=[("PS", 2 * pb + 1), ("SIG", sg)])
                    P.op("dve", lambda e, SIG=SIG, TMP=TMP, psa=psa, N=N: e.scalar_tensor_tensor(
                        out=TMP[:, 0:N], in0=SIG[:, 0:N], scalar=1.0, in1=psa[:, 0:N], op0=ALU.add, op1=ALU.mult),
                        reads=[("SIG", sg)], writes=[("PS", 2 * pb), ("TMP2", sg)])
                    if gi < 2:
                        dst = self.UALL[:, c, 30 + n0:30 + n0 + N]
                        src = TMP[:, 0:N]
                    else:
                        dst = self.UALL[:, c, 1054:1206].rearrange("p (b j) -> p b j", j=38)[:, :, 30:38]
                        src = TMP[:, 0:32].rearrange("p (b j) -> p b j", j=8)
                    P.op("act", lambda e, dst=dst, src=src: e.activation(out=dst, in_=src, func=AF.Copy, scale=0.5),
                         reads=[("TMP2", sg)], writes=[("UALL", c, gi)])
                    if gi == 1:
                        P.op("dve", lambda e, TMP=TMP, c=c: e.tensor_scalar(out=self.UTP[:, c, :], in0=TMP[:, 482:512], scalar1=0.5, scalar2=None,
                                                                             op0=ALU.mult),
                             reads=[("TMP2", sg)], writes=[("UTP", c)])
                    if gi == 2:
                        P.op("dve", lambda e, src=src, c=c: e.tensor_scalar(out=self.UTS[:, c, :, 30:38], in0=src, scalar1=0.5, scalar2=None,
                                                                             op0=ALU.mult),
                             reads=[("TMP2", sg)], writes=[("UTS", c)])
            self.w_release(2)

    def conv_state_outputs(self):
        P = self.P
        jobs = [(self.conv_p[:, :], [self.UTP[:, c, :] for c in range(8)], [("UTP", c) for c in range(8)])]
        for b in range(4):
            jobs.append((self.conv_s[b * 30:(b + 1) * 30, :], [self.UTS[:, c, b, 8:38] for c in range(8)], [("UTS", c) for c in range(8)]))
        for (dst, srcs, keys) in jobs:
            for hf in range(2):
                bank = 4 + hf
                for i in range(4):
                    c = hf * 4 + i
                    P.op("pe", lambda e, bank=bank, i=i, src=srcs[c]: e.transpose(
                        out=self.PS[bank][0:30, i * 128:(i + 1) * 128], in_=src, identity=self.IDF[:, :]),
                        reads=["IDF", keys[c]], writes=[("PS", bank)])
                if hf == 0:
                    P.op("act", lambda e, bank=bank: e.activation(out=self.OST[0:30, 0:512], in_=self.PS[bank][0:30, :], func=AF.Copy),
                         writes=[("PS", bank), ("OST", 0)])
                else:
                    P.op("dve", lambda e, bank=bank: e.tensor_copy(out=self.OST[0:30, 512:1024], in_=self.PS[bank][0:30, :]),
                         writes=[("PS", bank), ("OST", 1)])
            P.op("sp", lambda e, dst=dst: e.dma_start(out=dst, in_=self.OST[0:30, :]),
                 reads=[("OST", 0), ("OST", 1)], dsem=("co", self.nxt("co", 2)))

    def _allgather(self, nm, src, dst):
        groups = [[2 * i, 2 * i + 1] for i in range(self.ncores // 2)]
        self.P.op("pool", lambda e: e.collective_compute("AllGather", ALU.bypass, replica_groups=groups, ins=[src], outs=[dst]),
                  reads=["xin_" + nm], writes=["xout_" + nm], dsem="cc_" + nm, dinc=self.cc_inc, nofence=True)

    def exchange_k(self):
        P = self.P
        for c in range(8):
            P.op("sp", lambda e, c=c: e.dma_start(out=self.xin_k[c * 128:(c + 1) * 128, :], in_=self.KT[:, c, 0:1024]),
                 reads=[("KT", c // 2, t) for t in range(9)], writes=["xin_k"], dsem=("xk", c % 4))
        self._allgather("k", self.xin_k, self.xout_k)

    def exchange_v(self):
        self._allgather("v", self.xin_v, self.xout_v)

    def exchange_t(self):
        P = self.P
        tail = self.xin_t.rearrange("r (q j) -> (r q) j", j=32).rearrange("(c p) j -> p c j", p=128)
        P.op("sp", lambda e: e.dma_start(out=tail[:, :, :], in_=self.UALL[:, :, 1024:1056]),
             reads=[("UALL", c, 1) for c in range(8)] + [("UALL", c, 2) for c in range(8)], writes=["xin_t"], dsem="xt")
        self._allgather("t", self.xin_t, self.xout_t)

    def conv_ln(self):
        P = self.P
        hsrc = self.xout_t[0:32, :].rearrange("r (q j) -> (r q) j", j=32).rearrange("(c p) j -> p c j", p=128)
        P.op("sp", lambda e: e.dma_start(out=self.HALO[:, :, :], in_=hsrc), reads=["xout_t"], writes=["HALO"], dsem="hl")
        P.op("dve", lambda e: e.tensor_scalar(out=self.UALL[:, :, 0:30], in0=self.HALO[:, :, 0:30], scalar1=self.FLAG[:, 0:1], scalar2=None,
                                              op0=ALU.mult),
             reads=["HALO", "FLAG"], writes=[("UALL", c, 3) for c in range(8)])
        pieces = [(0, 512), (512, 512), (1054, 122)]
        ycol = [0, 512, 1024]
        for c in range(8):
            DG = self.DG[c % 2]
            P.op("dve", lambda e, DG=DG, c=c: e.tensor_tensor(
                out=DG[:, :, :], in0=self.IDB[:, :].unsqueeze(1).to_broadcast([128, 31, 128]),
                in1=self.DWT[:, c, :].unsqueeze(2).to_broadcast([128, 31, 128]), op=ALU.mult),
                reads=["IDB", ("DWT", c)], writes=[("DG", c % 2)])
            for pi, (a0, N) in enumerate(pieces):
                bank = self.nxt("CV", 4)
                ps = self.PS[bank]
                for k in range(31):
                    P.op("pe", lambda e, ps=ps, DG=DG, k=k, c=c, a0=a0, N=N: e.matmul(
                        ps[:, 0:N], lhsT=DG[:, k, :], rhs=self.UALL[:, c, a0 + k:a0 + k + N], start=(k == 0), stop=(k == 30)),
                        reads=[("DG", c % 2)] + [("UALL", c, g) for g in range(4)], writes=[("PS", bank)])
                P.op("act", lambda e, ps=ps, c=c, y0=ycol[pi], N=N: e.activation(
                    out=self.Y[:, c, y0:y0 + N], in_=ps[:, 0:N], func=AF.Identity, bias=self.CPT[:, c:c + 1]),
                    reads=["CPT"], writes=[("PS", bank), ("Y", c, pi)])
        lnp = [(0, 256, 0), (256, 256, 0), (512, 256, 1), (768, 256, 1), (1024, 122, 2)]

        def stats(li):
            y0, N, pi = lnp[li]
            MU, VAR, RSTD = self.MUS[li % 2], self.VARS[li % 2], self.RSTDS[li % 2]
            kx = li % 2
            for c in range(8):
                yb = self.nxt("YSQ", 2)
                YSQ = self.YSQ[yb]
                P.op("act", lambda e, YSQ=YSQ, c=c, y0=y0, N=N: e.activation(out=YSQ[:, 0:N], in_=self.Y[:, c, y0:y0 + N], func=AF.Square),
                     reads=[("Y", c, pi)], writes=[("YSQ", yb)])
                P.op("pe", lambda e, c=c, y0=y0, N=N: e.matmul(self.PS[4][:, 0:N], lhsT=self.ONF[:, :], rhs=self.Y[:, c, y0:y0 + N],
                                                               start=(c == 0), stop=(c == 7)),
                     reads=["ONF", ("Y", c, pi)], writes=[("PS", 4)])
                P.op("pe", lambda e, YSQ=YSQ, c=c, N=N: e.matmul(self.PS[5][:, 0:N], lhsT=self.ONF[:, :], rhs=YSQ[:, 0:N],
                                                                 start=(c == 0), stop=(c == 7)),
                     reads=["ONF", ("YSQ", yb)], writes=[("PS", 5)])
            P.op("dve", lambda e, N=N: e.tensor_scalar(out=MU[:, 0:N], in0=self.PS[4][:, 0:N], scalar1=1.0 / 1024, scalar2=None, op0=ALU.mult),
                 writes=[("PS", 4), ("MU", kx)])
            P.op("dve", lambda e, N=N: e.tensor_tensor(out=VAR[:, 0:N], in0=MU[:, 0:N], in1=MU[:, 0:N], op=ALU.mult),
                 reads=[("MU", kx)], writes=[("VAR", kx)])
            P.op("dve", lambda e, N=N: e.scalar_tensor_tensor(out=VAR[:, 0:N], in0=self.PS[5][:, 0:N], scalar=1.0 / 1024, in1=VAR[:, 0:N],
                                                              op0=ALU.mult, op1=ALU.subtract),
                 writes=[("PS", 5), ("VAR", kx)])
            P.op("act", lambda e, N=N: e.activation(out=RSTD[:, 0:N], in_=VAR[:, 0:N], func=AF.Sqrt, bias=self.EPSB[:, :]),
                 reads=[("VAR", kx), "EPSB"], writes=[("RSTD", kx)])
            P.op("dve", lambda e, N=N: e.reciprocal(out=RSTD[:, 0:N], in_=RSTD[:, 0:N]), writes=[("RSTD", kx)])

        def normalize(li):
            y0, N, pi = lnp[li]
            MU, RSTD = self.MUS[li % 2], self.RSTDS[li % 2]
            kx = li % 2
            for c in range(8):
                tb = self.nxt("TN", 2)
                TN = self.TN[tb]
                P.op("dve", lambda e, TN=TN, c=c, y0=y0, N=N: e.tensor_tensor(out=TN[:, 0:N], in0=self.Y[:, c, y0:y0 + N], in1=MU[:, 0:N],
                                                                             op=ALU.subtract),
                     reads=[("Y", c, pi), ("MU", kx)], writes=[("TN", tb)])
                P.op("dve", lambda e, TN=TN, N=N: e.tensor_tensor(out=TN[:, 0:N], in0=TN[:, 0:N], in1=RSTD[:, 0:N], op=ALU.mult),
                     reads=[("RSTD", kx)], writes=[("TN", tb)])
                if pi < 2:
                    dst = self.CAT[:, 8 + c, y0:y0 + N]
                    src = TN[:, 0:N]
                    tiles = (y0 // 128, y0 // 128 + 1)
                else:
                    dst = self.CAT[:, 8 + c, 1024:1056].rearrange("p (b j) -> p b j", j=8)
                    src = TN[:, 0:152].rearrange("p (b j) -> p b j", j=38)[:, :, 0:8]
                    tiles = (8,)
                P.op("act", lambda e, dst=dst, src=src, c=c: e.activation(out=dst, in_=src, func=AF.Silu,
                                                                          scale=self.CPT[:, 8 + c:9 + c], bias=self.CPT[:, 16 + c:17 + c]),
                     reads=[("TN", tb), "CPT"], writes=[("CAT", (8 + c) // 4, t) for t in tiles])

        stats(0)
        for li in range(len(lnp)):
            if li + 1 < len(lnp):
                stats(li + 1)
            normalize(li)

    def attention(self):
        P = self.P
        ld = lambda dst, src, key, ds: P.op("sp", lambda e: e.dma_start(out=dst, in_=src), writes=[key], dsem=ds)
        ld(self.MASKG[:].rearrange("p a b -> p (a b)"), self.maskg_d[:, :], "MASKG", "m0")
        ld(self.MASKC[:].rearrange("p a b -> p (a b)"), self.maskc_d[:, :], "MASKC", "m1")
        ld(self.MASKS[:].rearrange("p a b -> p (a b)"), self.masks_d[:, :], "MASKS", "m2")
        ld(self.MASKN[0:32, :], self.maskn_d[:, :], "MASKN", "m3")
        for kb in range(2):
            P.op("pool", lambda e, kb=kb: e.memset(self.VSTA[kb][:, :, 64:128], 1.0), writes=[("VONE", 0, kb)])
            P.op("pool", lambda e, kb=kb: e.memset(self.VSTB[kb][:, :, 0:64], 1.0), writes=[("VONE", 1, kb)])

        def loads(c):
            kb = c % 2
            KC = self.KCTX[kb]
            P.op("sp", lambda e, KC=KC, c=c: e.dma_start(out=KC[:, :], in_=self.xout_k[c * 128:(c + 1) * 128, :]),
                 reads=["xout_k"], writes=[("KCTX", kb)], dsem=("kc", kb))
            for e_, VT in ((0, self.VSTA[kb]), (1, self.VSTB[kb])):
                f0 = c * 128 + e_ * 64
                P.op("sp", lambda e, VT=VT, f0=f0, e_=e_: e.dma_start(
                    out=VT[:, 0:8, e_ * 64:(e_ + 1) * 64], in_=self.xout_v[0:1024, f0:f0 + 64].rearrange("(t p) f -> p t f", p=128)),
                    reads=["xout_v"], writes=[("VST", e_, kb, 0)], dsem=("vc", e_, kb))
                P.op("sp", lambda e, VT=VT, f0=f0, e_=e_: e.dma_start(
                    out=VT[:, 8:16, e_ * 64:(e_ + 1) * 64], in_=self.xin_v[0:1024, f0:f0 + 64].rearrange("(t p) f -> p t f", p=128)),
                    reads=["xin_v"], writes=[("VST", e_, kb, 1)], dsem=("vo2", e_, kb))

        steps = []
        for c in range(8):
            for e_ in range(2):
                for qg in range(2):
                    blocks = [(0, j) for j in range(8)] + [(1, j) for j in range(4 * qg + 4)]
                    nd = self.nxt("ND", 2)
                    npair = len(blocks) // 2
                    for bp in range(npair):
                        steps.append(dict(c=c, e_=e_, qg=qg, bp=bp, npair=npair, blk=blocks[2 * bp:2 * bp + 2], nd=nd,
                                          pp=self.nxt("SPP", 2), ex=self.nxt("EX", 2),
                                          pb=[self.nxt("PTB", 2)],
                                          rd=self.nxt("RDEN", 2) if bp == npair - 1 else None,
                                          newc=(e_ == 0 and qg == 0 and bp == 0)))

        def stageA(s):
            c, e_, qg = s["c"], s["e_"], s["qg"]
            kb = c % 2
            KC = self.KCTX[kb]
            r0, r1 = e_ * 64, (e_ + 1) * 64
            q0 = qg * 512
            PPt = self.PP[s["pp"]]
            for h2 in range(2):
                own, j = s["blk"][h2]
                if own:
                    lhs = self.KT[r0:r1, c, j * 128:(j + 1) * 128]
                    rk = [("KT", c // 2, j)]
                else:
                    lhs = KC[r0:r1, j * 128:(j + 1) * 128]
                    rk = [("KCTX", kb)]
                P.op("pe", lambda e, PPt=PPt, h2=h2, lhs=lhs, c=c, r0=r0, r1=r1, q0=q0: e.matmul(
                    PPt[:, h2 * 512:(h2 + 1) * 512], lhsT=lhs, rhs=self.QT[r0:r1, c, q0:q0 + 512], start=True, stop=True),
                    reads=rk + [("QT", c // 2, t) for t in range(4 * qg, 4 * qg + 4)], writes=[("PS", 2 * s["pp"] + h2)])

        def stageB(s):
            PPt = self.PP[s["pp"]]
            EX = self.EX[s["ex"]]
            P.op("act", lambda e, EX=EX, PPt=PPt: e.activation(out=EX[:, :], in_=PPt[:, :], func=AF.Exp, scale=0.125),
                 writes=[("PS", 2 * s["pp"]), ("PS", 2 * s["pp"] + 1), ("EX", s["ex"])])

        def stageC(s):
            c, e_, qg, bp, npair, nd = s["c"], s["e_"], s["qg"], s["bp"], s["npair"], s["nd"]
            kb = c % 2
            VT = (self.VSTA if e_ == 0 else self.VSTB)[kb]
            r0, r1 = e_ * 64, (e_ + 1) * 64
            o0, o1 = (1 - e_) * 64, (2 - e_) * 64
            q0 = qg * 512
            bn = 4 + nd
            EX = self.EX[s["ex"]]
            own, j = s["blk"][0]
            assert s["blk"][1] == (own, j + 1)
            mt, mkey = (self.MASKG, "MASKG") if own else (self.MASKC, "MASKC")
            d0 = (4 * qg - j) if own else (8 + 4 * qg - j)
            mk = bass.AP(mt, (d0 + 3) * 128, [[19 * 128, 128], [-128, 2], [1, 512]])
            pb = s["pb"][0]
            PT = self.PTB2[pb % 2]
            P.op("dve", lambda e, PT=PT, EX=EX, mk=mk: e.tensor_tensor(
                out=PT[:, :, :], in0=EX[:, :].rearrange("p (a b) -> p a b", b=512), in1=mk, op=ALU.mult),
                reads=[("EX", s["ex"]), mkey], writes=[("PTB", pb % 2)])
            for h2 in range(2):
                own, j = s["blk"][h2]
                vt = VT[:, (8 + j) if own else j, :]
                vk = ("VST", e_, kb, 1 if own else 0)
                first = (bp == 0 and h2 == 0)
                last = (bp == npair - 1 and h2 == 1)
                P.op("pe", lambda e, PT=PT, vt=vt, bn=bn, first=first, last=last, h2=h2: e.matmul(
                    self.PS[bn][:, :], lhsT=vt, rhs=PT[:, h2, :], start=first, stop=last),
                    reads=[("PTB", pb % 2), vk, ("VONE", e_, kb)], writes=[("PS", bn)])
            if bp == npair - 1:
                rd = s["rd"]
                RD = self.RDEN[rd]

                def norm(RD=RD, rd=rd, bn=bn, r0=r0, r1=r1, o0=o0, o1=o1, c=c, q0=q0, qg=qg):
                    P.op("dve", lambda e: e.reciprocal(out=RD[r0:r1, :], in_=self.PS[bn][o0:o1, :]),
                         writes=[("PS", bn), ("RDEN", rd)])
                    P.op("dve", lambda e: e.tensor_tensor(
                        out=self.CAT[r0:r1, c, q0:q0 + 512], in0=self.PS[bn][r0:r1, :], in1=RD[r0:r1, :], op=ALU.mult),
                        reads=[("RDEN", rd)], writes=[("PS", bn)] + [("CAT", c // 4, t) for t in range(4 * qg, 4 * qg + 4)])
                pend_norm.append([2, norm])
            for pn in pend_norm[:]:
                if pn[0] == 0:
                    pn[1]()
                    pend_norm.remove(pn)
                else:
                    pn[0] -= 1

        pend_norm = []
        loads(0)
        LOOK = 2
        n = len(steps)
        for i in range(min(LOOK, n)):
            stageA(steps[i])
        for i in range(n):
            s_ = steps[i]
            if s_["newc"] and s_["c"] + 1 < 8:
                loads(s_["c"] + 1)
            stageB(s_)
            if i + LOOK < n:
                stageA(steps[i + LOOK])
            stageC(s_)
        for pn in pend_norm:
            pn[1]()

    def sample_attention(self):
        P = self.P
        NUM, DEN = 2, 3
        P.op("dve", lambda e: e.memset(self.QBD[:], 0.0), writes=["QBD"])
        for e_ in range(2):
            P.op("dve", lambda e, e_=e_: e.tensor_copy(out=self.QBD[e_ * 64:(e_ + 1) * 64, :, e_ * 32:(e_ + 1) * 32],
                                                      in_=self.QT[e_ * 64:(e_ + 1) * 64, :, 1024:1056]),
                 reads=[("QT", cp, 8) for cp in range(4)], writes=["QBD"])
        steps = []
        for (b, r) in [(b, r) for b in range(4) for r in range(12)] + [(-1, -1)]:
            steps.append(dict(b=b, r=r, new=(b < 0), sb=self.nxt("SS", 2), cb=self.nxt("CK", 3) if b >= 0 else None,
                              ex=self.nxt("EXS", 2)))
        nst = len(steps)

        def stageA(s):
            b, r, new, sb_ = s["b"], s["r"], s["new"], s["sb"]
            ps = self.PS[sb_]
            if not new:
                cb = s["cb"]
                CKB, CVB, KTS = self.CKB[cb], self.CVB[cb], self.KTS[cb]
                if r < 8:
                    ksrc = self.cache_k[b].rearrange("(m r) f -> r m f", r=16)[r]
                    vsrc = self.cache_v[b].rearrange("(m r) f -> r m f", r=16)[r]
                else:
                    ksrc = self.cache_k[b, (r + 4) * 128:(r + 5) * 128, :]
                    vsrc = self.cache_v[b, (r + 4) * 128:(r + 5) * 128, :]
                P.op("pool", lambda e, CKB=CKB, ksrc=ksrc: e.dma_start(out=CKB[:, :], in_=ksrc),
                     writes=[("CKB", cb)], dsem=("ck", cb))
                P.op("pool", lambda e, CVB=CVB, vsrc=vsrc: e.dma_start(out=CVB[:, :], in_=vsrc),
                     writes=[("CVB", cb)], dsem=("cv", cb))
                for hq in range(2):
                    half = self.nxt("T", 2)
                    pt = self.PT_bf[half]
                    for i in range(4):
                        c = hq * 4 + i
                        P.op("pe", lambda e, pt=pt, CKB=CKB, c=c, i=i: e.transpose(
                            out=pt[:, i * 128:(i + 1) * 128], in_=CKB[:, c * 128:(c + 1) * 128], identity=self.IDB[:, :]),
                            reads=[("CKB", cb), "IDB"], writes=[("PS", 6 + half)])
                    src = pt.rearrange("p (a b) -> p a b", b=128)
                    dst = KTS[:, hq * 4:hq * 4 + 4, :]
                    if hq == 0:
                        P.op("act", lambda e, src=src, dst=dst: e.activation(out=dst, in_=src, func=AF.Copy),
                             writes=[("PS", 6 + half), ("KTS", cb, hq)])
                    else:
                        P.op("dve", lambda e, src=src, dst=dst: e.tensor_copy(out=dst, in_=src),
                             writes=[("PS", 6 + half), ("KTS", cb, hq)])
                npos = 128
            else:
                npos = 32
            for c in range(8):
                if new:
                    lhs = self.KT[:, c, 1024:1056]
                    rk = [("KT", c // 2, 8)]
                else:
                    lhs = KTS[:, c, :]
                    rk = [("KTS", s["cb"], c // 4)]
                P.op("pe", lambda e, ps=ps, lhs=lhs, c=c, npos=npos: e.matmul(
                    ps[0:npos, c * 64:(c + 1) * 64], lhsT=lhs, rhs=self.QBD[:, c, :], start=True, stop=True),
                    reads=rk + ["QBD"], writes=[("PS", sb_)])

        def stageB(s):
            b, r, new, sb_, ex = s["b"], s["r"], s["new"], s["sb"], s["ex"]
            ps = self.PS[sb_]
            npos = 32 if new else 128
            EXS, PTS = self.EXS[ex], self.PTS[ex]
            P.op("act", lambda e, EXS=EXS, ps=ps, npos=npos: e.activation(out=EXS[0:npos, :], in_=ps[0:npos, :], func=AF.Exp, scale=0.125),
                 writes=[("PS", sb_), ("EXS", ex)])
            if new:
                mk = self.MASKN[0:32, :].unsqueeze(1).to_broadcast([32, 16, 32])
                mkey = "MASKN"
            else:
                mk = self.MASKS[:, b * 12 + r, :].unsqueeze(1).to_broadcast([128, 16, 32])
                mkey = "MASKS"
            P.op("dve", lambda e, PTS=PTS, EXS=EXS, mk=mk, npos=npos: e.tensor_tensor(
                out=PTS[0:npos], in0=EXS[0:npos, :].rearrange("p (h q) -> p h q", q=32), in1=mk, op=ALU.mult),
                reads=[("EXS", ex), mkey], writes=[("PTS", ex)])

        def stageC(s, si):
            new, ex = s["new"], s["ex"]
            npos = 32 if new else 128
            PTS = self.PTS[ex]
            first, last = si == 0, si == nst - 1
            for c in range(8):
                if new:
                    lhs = self.VS[0:32, c * 128:(c + 1) * 128]
                    rk = [("VS", c // 4)]
                else:
                    lhs = self.CVB[s["cb"]][:, c * 128:(c + 1) * 128]
                    rk = [("CVB", s["cb"])]
                P.op("pe", lambda e, lhs=lhs, PTS=PTS, c=c, npos=npos, st=(first and c == 0), sp=(last and c == 7): e.matmul(
                    self.PS[NUM][:, c * 64:(c + 1) * 64], lhsT=lhs, rhs=PTS[0:npos, 2 * c:2 * c + 2, :], start=st, stop=sp,
                    skip_group_check=True),
                    reads=rk + [("PTS", ex)], writes=[("PS", NUM)])
            P.op("pe", lambda e, PTS=PTS, npos=npos, first=first, last=last: e.matmul(
                self.PS[DEN][:, :], lhsT=self.ONB[0:npos, :], rhs=PTS[0:npos].rearrange("p h q -> p (h q)"), start=first, stop=last),
                reads=["ONB", ("PTS", ex)], writes=[("PS", DEN)])

        stageA(steps[0])
        if nst > 1:
            stageA(steps[1])
        for si in range(nst):
            stageB(steps[si])
            stageC(steps[si], si)
            if si + 2 < nst:
                stageA(steps[si + 2])
        P.op("dve", lambda e: e.reciprocal(out=self.RDS[:, :], in_=self.PS[DEN][:, :]), writes=[("PS", DEN), "RDS"])
        for e_ in range(2):
            r0, r1 = e_ * 64, (e_ + 1) * 64
            num = self.PS[NUM][r0:r1, :].rearrange("p (c e q) -> p c e q", e=2, q=32)[:, :, e_, :]
            rds = self.RDS[r0:r1, :].rearrange("p (c e q) -> p c e q", e=2, q=32)[:, :, e_, :]
            P.op("dve", lambda e, num=num, rds=rds, r0=r0, r1=r1: e.tensor_tensor(out=self.CAT[r0:r1, 0:8, 1024:1056], in0=num, in1=rds, op=ALU.mult),
                 reads=["RDS"], writes=[("PS", NUM), ("CAT", 0, 8), ("CAT", 1, 8)])

    def out_proj(self):
        P = self.P
        for cbp in range(4):
            rhs_aps, wks = self.w_get_pair()
            for t, (r0, npt) in enumerate(TILES):
                bank = self.nxt("PQ", 6)
                ps = self.PS[bank]
                for kc in range(16):
                    P.op("pe", lambda e, ps=ps, rhs=rhs_aps[kc], kc=kc, r0=r0, npt=npt: e.matmul(
                        ps[0:npt, :].rearrange("p (a b) -> p a b", b=256), lhsT=self.CAT[:, kc, r0:r0 + npt], rhs=rhs,
                        start=(kc == 0), stop=(kc == 15)),
                        reads=wks + [("CAT", kc // 4, t)], writes=[("PS", bank)])
                H = self.H[t]
                P.op("dve", lambda e, ps=ps, H=H, cbp=cbp, npt=npt: e.tensor_tensor(
                    out=H[0:npt, cbp * 512:(cbp + 1) * 512], in0=ps[0:npt, :], in1=H[0:npt, cbp * 512:(cbp + 1) * 512], op=ALU.add),
                    writes=[("PS", bank), ("H", t)])
            self.w_release(2)

    def mixer(self):
        upto = getattr(self, "upto", None)
        steps = [("norm", lambda: (self.rmsnorm_to_cat("ln_mix", spill=True), self.fence())),
                 ("setup", self.mixer_setup), ("qk", lambda: (self.proj_qk(), self.exchange_k())),
                 ("v", lambda: (self.proj_v(), self.exchange_v(), self.mixer_setup_pe())), ("glu", lambda: (self.proj_glu(), self.exchange_t())),
                 ("cso", self.conv_state_outputs), ("xchg", self.fence),
                 ("conv", lambda: (self.conv_ln(), self.fence())), ("attn", self.attention),
                 ("sattn", lambda: (self.sample_attention(), self.fence())),
                 ("out", lambda: (self.reload_h(), self.out_proj()))]
        self.alloc_mixer()
        for name, fn in steps:
            fn()
            if upto == name:
                self.w_list = self.w_list[:self.w_issued]
                return

    def build(self):
        if self.stage != "mix":
            self.ffn_plan("ffn1")
        if self.stage == "norm1":
            self.w_list = []
            for kk in ("ffn1_g", "ffn1_u", "ffn1_d"):
                pass
            self.rmsnorm_to_cat("ln_ffn1", load_x=True)
            dbg = self.dram_out("dbg", [128, 16 * NTOK], BF16)
            self.P.op("sp", lambda e: e.dma_start(out=dbg[:, :], in_=self.CAT[:, :, :].rearrange("p a b -> p (a b)")),
                      reads=[("CAT", kq, t) for kq in range(4) for t in range(NT)], dsem="dbg")
        if self.stage == "ffn1":
            self.rmsnorm_to_cat("ln_ffn1", load_x=True)
            self.ffn("ffn1", out_dram=self.y_tok)
        if self.stage == "full":
            self.mixer_plan()
            self.ffn_plan("ffn2")
            self.rmsnorm_to_cat("ln_ffn1", load_x=True)
            self.ffn("ffn1")
            self.mixer()
            self.rmsnorm_to_cat("ln_ffn2")
            self.ffn("ffn2", out_dram=self.y_tok)
        if self.stage == "mix":
            self.w_list = []
            self.mixer_plan()
            for t, (r0, npt) in enumerate(TILES):
                H = self.H[t]
                self.P.op("sp", lambda e, H=H, r0=r0, npt=npt: e.dma_start(out=H[0:npt, :], in_=self.x_tok[r0:r0 + npt, :]),
                          writes=[("H", t)], dsem=("xl", t % 4))
            self.mixer()
            for t, (r0, npt) in enumerate(TILES):
                H = self.H[t]
                self.P.op("sp", lambda e, H=H, r0=r0, npt=npt: e.dma_start(out=self.y_tok[r0:r0 + npt, :], in_=H[0:npt, :]),
                          reads=[("H", t)], dsem=("yo", t % 4))
        self.P.finalize()
        self.P.emit()
        return self.nc


def _ident_bf():
    return np.eye(128, dtype=np.float32).astype(ml_dtypes.bfloat16)


def _mult(delta):
    d = np.asarray(delta)
    c = ((d >= 0) & (d <= 128)).astype(np.float32)
    c += ((d >= 0) & (d <= 512) & (d % 4 == 0))
    c += ((d >= 0) & (d <= 2048) & (d % 16 == 0))
    return c


def _const_tables(half):
    bf = ml_dtypes.bfloat16
    t = {}
    t["ident_bf"] = _ident_bf()
    t["ident_f"] = np.eye(128, dtype=np.float32)
    pos = np.zeros((128, NT), np.float32)
    for tt in range(8):
        pos[:, tt] = half * 1024 + tt * 128 + np.arange(128)
    pos[:32, 8] = 16384 + (np.arange(32) % 8)
    inv = (10000.0 ** (-np.arange(32, dtype=np.float32) / 32)).astype(np.float32)
    ang = (pos[:, :, None] * inv[None, None, :]).astype(np.float32)
    t["cos_t"] = np.cos(ang.astype(np.float64)).astype(np.float32).reshape(128, NT * 32)
    t["sin_t"] = np.sin(ang.astype(np.float64)).astype(np.float32).reshape(128, NT * 32)
    k = np.arange(128)[:, None, None]
    q = np.arange(128)[None, None, :]
    d = (np.arange(19) - 3)[None, :, None]
    mg = _mult(d * 128 + q - k)
    t["maskg"] = mg.astype(bf).reshape(128, 19 * 128)
    t["maskc"] = (mg * float(half)).astype(bf).reshape(128, 19 * 128)
    p = np.arange(128)[:, None, None, None, None]
    b = np.arange(4)[None, :, None, None, None]
    sidx = np.arange(12)[None, None, :, None, None]
    b2 = np.arange(4)[None, None, None, :, None]
    tq = np.arange(8)[None, None, None, None, :]
    m_p3 = ((tq == sidx) & (sidx < 8)).astype(np.float32) * np.ones_like(p, dtype=np.float32)
    dlt = 2048 + tq - (128 * (sidx + 4) + p)
    m_rc = (((dlt >= 0) & (dlt <= 128)).astype(np.float32) + ((dlt >= 0) & (dlt <= 512) & (dlt % 4 == 0))) * (sidx >= 8)
    ms = (m_p3 + m_rc) * (b2 == b)
    t["masks"] = ms.astype(bf).reshape(128, 48 * 32)
    kb = (np.arange(32) // 8)[:, None]
    kt = (np.arange(32) % 8)[:, None]
    qb = (np.arange(32) // 8)[None, :]
    qt = (np.arange(32) % 8)[None, :]
    t["maskn"] = (_mult(qt - kt) * (kb == qb)).astype(bf)
    t["flag"] = np.full((128, 1), float(half), np.float32)
    return t


def make_in_maps(inp, ncores=8, stage="full"):
    f32 = lambda a: np.ascontiguousarray(np.asarray(a, dtype=np.float32))
    maps = []
    shared = {}
    if stage == "full":
        for f in ("ffn1", "ffn2"):
            for n in ("gate", "up", "down"):
                shared["%s_w_%s" % (f, n)] = f32(inp["%s_w_%s" % (f, n)][0])
        for k in ("ln_ffn1", "ln_mix", "ln_ffn2"):
            shared[k] = f32(inp[k][0]).reshape(1, D)
    else:
        shared["ln_mix"] = f32(inp["ln_mix"][0]).reshape(1, D)
    shared["w_in"] = f32(inp["w_in"][0])
    shared["w_out"] = f32(inp["w_out"][0])
    shared["q_norm"] = f32(inp["q_norm"][0]).reshape(1, 64)
    shared["k_norm"] = f32(inp["k_norm"][0]).reshape(1, 64)
    shared["conv_dw_w"] = f32(inp["conv_dw_w"][0])
    shared["cpar"] = np.concatenate([f32(inp["conv_dw_b"][0]).reshape(8, 128), f32(inp["conv_ln_g"][0]).reshape(8, 128),
                                     f32(inp["conv_ln_b"][0]).reshape(8, 128)], axis=0)
    tabs = [_const_tables(0), _const_tables(1)]
    for c in range(ncores):
        b, half = c // 2, c % 2
        m = dict(shared)
        m.update(tabs[half])
        xs = f32(inp["x_sample"][4 * c:4 * c + 4]).reshape(32, D)
        m["x_tok"] = np.concatenate([f32(inp["x_prompt"][b, half * 1024:(half + 1) * 1024]), xs], axis=0)
        m["cache_k"] = f32(inp["cache_k"][0, 4 * c:4 * c + 4]).reshape(4, 2048, 1024)
        m["cache_v"] = f32(inp["cache_v"][0, 4 * c:4 * c + 4]).reshape(4, 2048, 1024)
        m["state_conv"] = f32(inp["state_conv"][0, 4 * c:4 * c + 4]).reshape(120, 1024)
        maps.append(m)
    return maps


def assemble(results, ncores=8):
    nb = ncores // 2
    y_p = np.zeros((nb, 2048, D), np.float32)
    y_s = np.zeros((4 * ncores, 8, D), np.float32)
    k_p = np.zeros((1, nb, 2048, 16, 64), np.float32)
    v_p = np.zeros((1, nb, 2048, 16, 64), np.float32)
    c_p = np.zeros((1, nb, 30, 1024), np.float32)
    k_s = np.zeros((1, 4 * ncores, 8, 16, 64), np.float32)
    v_s = np.zeros((1, 4 * ncores, 8, 16, 64), np.float32)
    c_s = np.zeros((1, 4 * ncores, 30, 1024), np.float32)
    for c in range(ncores):
        r = results[c]
        b, half = c // 2, c % 2
        sl = slice(half * 1024, (half + 1) * 1024)
        y = np.asarray(r["y_tok"])
        y_p[b, sl] = y[:1024]
        y_s[4 * c:4 * c + 4] = y[1024:].reshape(4, 8, D)
        nk = np.asarray(r["newk"])
        nv = np.asarray(r["newv"])
        k_p[0, b, sl] = nk[:1024].reshape(1024, 16, 64)
        v_p[0, b, sl] = nv[:1024].reshape(1024, 16, 64)
        k_s[0, 4 * c:4 * c + 4] = nk[1024:].reshape(4, 8, 16, 64)
        v_s[0, 4 * c:4 * c + 4] = nv[1024:].reshape(4, 8, 16, 64)
        if half == 1:
            c_p[0, b] = np.asarray(r["conv_p"])
        c_s[0, 4 * c:4 * c + 4] = np.asarray(r["conv_s"]).reshape(4, 30, 1024)
    return (y_p, y_s, k_p, v_p, c_p, k_s, v_s, c_s)


def kernel(**inputs):
    nc = Builder(stage="full", ncores=8).build()
    maps = make_in_maps(inputs, 8, "full")
    res = run_bass_kernel_spmd(nc, maps, core_ids=list(range(8)))
    return assemble(res.results, 8)
```

```python
import contextlib
import numpy as np
import ml_dtypes
import concourse.bass as bass
import concourse.mybir as mybir
from concourse.bass_utils import run_bass_kernel_spmd

F32 = mybir.dt.float32
BF16 = mybir.dt.bfloat16
ALU = mybir.AluOpType
AF = mybir.ActivationFunctionType
AX = mybir.AxisListType

ENGS = ("pe", "act", "dve", "pool", "sp")

D = 2048
DFF = 5632
NT = 9
NTOK = 1056
EPS = 1e-6
TILES = [(t * 128, 128) for t in range(8)] + [(1024, 32)]
GROUPS = [(0, 512, (0, 1, 2, 3)), (512, 512, (4, 5, 6, 7)), (1024, 32, (8,))]
NSLOT = 4
SLOT_BYTES = 8192


class Op:
    __slots__ = ("eng", "fn", "reads", "writes", "dsem", "pos", "deps", "sig", "cnt", "waits",
                 "dval", "dinc", "name")


class Prog:
    def __init__(self, nc):
        self.nc = nc
        self.ops = []
        self.last_w = {}
        self.readers = {}
        self.dsem_last = {}
        self.dsem_val = {}
        self.last_eng = {}
        self.fence_op = None

    def op(self, eng, fn, reads=(), writes=(), dsem=None, dinc=16, nofence=False, name=""):
        o = Op()
        o.eng, o.fn, o.reads, o.writes, o.dsem, o.name = eng, fn, tuple(reads), tuple(writes), dsem, name
        o.sig, o.cnt, o.waits, o.dval, o.dinc = False, 0, [], 0, dinc
        deps = set()
        for k in o.reads:
            w = self.last_w.get(k)
            if w is not None:
                deps.add(w)
        for k in o.writes:
            w = self.last_w.get(k)
            if w is not None:
                deps.add(w)
            for r in self.readers.get(k, ()):
                deps.add(r)
        if self.fence_op is not None and not nofence:
            deps.add(self.fence_op)
        if dsem is not None:
            p = self.dsem_last.get(dsem)
            if p is not None:
                deps.add(p)
            self.dsem_last[dsem] = o
            self.dsem_val[dsem] = self.dsem_val.get(dsem, 0) + dinc
            o.dval = self.dsem_val[dsem]
        deps.discard(o)
        o.deps = [d for d in deps if not (eng == "pe" and d.eng == "pe" and d.dsem is None)]
        for k in o.writes:
            self.last_w[k] = o
            self.readers[k] = []
        for k in o.reads:
            if k not in o.writes:
                self.readers.setdefault(k, []).append(o)
        self.ops.append(o)
        if dsem is None:
            self.last_eng[eng] = o
        return o

    def fence(self, fn):
        o = self.op("dve", fn, name="fence")
        deps = set(o.deps)
        for e, last in self.last_eng.items():
            if last is not o:
                deps.add(last)
        for k, last in self.dsem_last.items():
            if not (isinstance(k, str) and k.startswith("cc_")):
                deps.add(last)
        deps.discard(o)
        o.deps = list(deps)
        self.fence_op = o
        return o

    def finalize(self):
        per = {e: [] for e in ENGS}
        for o in self.ops:
            o.pos = len(per[o.eng])
            per[o.eng].append(o)
        for e in ENGS:
            known = {x: -1 for x in ENGS}
            kd = {}
            for o in per[e]:
                need = {}
                needd = {}
                for d in o.deps:
                    if d.dsem is not None:
                        if kd.get(d.dsem, 0) < d.dval:
                            needd[d.dsem] = max(needd.get(d.dsem, 0), d.dval)
                    else:
                        if known[d.eng] < d.pos:
                            if d.eng not in need or need[d.eng].pos < d.pos:
                                need[d.eng] = d
                o.waits = []
                for x, d in need.items():
                    d.sig = True
                    known[x] = d.pos
                    o.waits.append(d)
                for s, v in needd.items():
                    kd[s] = v
                    o.waits.append((s, v))
        for e in ENGS:
            c = 0
            for o in per[e]:
                if o.dsem is None and o.sig:
                    c += 1
                    o.cnt = c
        self.per = per

    def emit(self):
        nc = self.nc
        per = self.per
        with contextlib.ExitStack() as st:
            esem = {e: st.enter_context(nc.semaphore("s_" + e)) for e in ENGS}
            dsems = {}
            for k in self.dsem_val:
                dsems[k] = st.enter_context(nc.semaphore("d%d" % len(dsems)))
            block = st.enter_context(nc.Block())

            def run(e, eng):
                for o in per[e]:
                    for w in o.waits:
                        if isinstance(w, tuple):
                            eng.wait_ge(dsems[w[0]], w[1])
                        else:
                            eng.wait_ge(esem[w.eng], w.cnt)
                    ins = o.fn(eng)
                    if o.dsem is not None:
                        ins.then_inc(dsems[o.dsem], o.dinc)
                    elif o.sig:
                        ins.then_inc(esem[e], 1)
                for k, last in self.dsem_last.items():
                    if last.eng == e:
                        eng.wait_ge(dsems[k], last.dval)

            block.tensor(lambda eng: run("pe", eng))
            block.scalar(lambda eng: run("act", eng))
            block.vector(lambda eng: run("dve", eng))
            block.gpsimd(lambda eng: run("pool", eng))
            block.sync(lambda eng: run("sp", eng))


class Builder:
    def __init__(self, stage="full", ncores=8, cc_inc=1):
        self.stage = stage
        self.ncores = ncores
        self.cc_inc = cc_inc
        nc = bass.Bass("TRN2", target_bir_lowering=False)
        self.nc = nc
        self.P = Prog(nc)
        self.sb_off = 16384
        self.cnt = {}
        self.declare_io()
        self.alloc_common()

    def dram_in(self, name, shape, dt=F32):
        return self.nc.dram_tensor(name, list(shape), dt, kind="ExternalInput").ap()

    def dram_out(self, name, shape, dt=F32):
        return self.nc.dram_tensor(name, list(shape), dt, kind="ExternalOutput").ap()

    def sb_at(self, name, shape, dt, off):
        return self.nc.alloc_sbuf_tensor_at(name, list(shape), dt, offset=off)

    def sb(self, name, shape, dt):
        nbytes = int(np.prod(shape[1:])) * (4 if dt == F32 else 2)
        t = self.sb_at(name, shape, dt, self.sb_off)
        self.sb_off += (nbytes + 63) // 64 * 64
        assert self.sb_off <= 224 * 1024, (name, self.sb_off)
        return t

    def nxt(self, key, mod):
        v = self.cnt.get(key, 0)
        self.cnt[key] = v + 1
        return v % mod

    def declare_io(self):
        di = self.dram_in
        st = self.stage
        self.x_tok = di("x_tok", [NTOK, D])
        self.wts = {}
        ffns = {"norm1": ("ffn1",), "ffn1": ("ffn1",), "full": ("ffn1", "ffn2"), "mix": ()}[st]
        lns = {"norm1": ("ln_ffn1",), "ffn1": ("ln_ffn1",), "full": ("ln_ffn1", "ln_mix", "ln_ffn2"), "mix": ("ln_mix",)}[st]
        for f in ffns:
            self.wts[f + "_g"] = di(f + "_w_gate", [D, DFF])
            self.wts[f + "_u"] = di(f + "_w_up", [D, DFF])
            self.wts[f + "_d"] = di(f + "_w_down", [DFF, D])
        self.ln = {k: di(k, [1, D]) for k in lns}
        self.ident_bf_d = di("ident_bf", [128, 128], BF16)
        self.y_tok = self.dram_out("y_tok", [NTOK, D])
        if st in ("norm1", "ffn1"):
            return
        self.w_in = di("w_in", [D, 5120])
        self.w_out = di("w_out", [D, D])
        self.q_norm = di("q_norm", [1, 64])
        self.k_norm = di("k_norm", [1, 64])
        self.conv_dw_w = di("conv_dw_w", [31, 1024])
        self.cpar = di("cpar", [24, 128])
        self.cache_k = di("cache_k", [4, 2048, 1024])
        self.cache_v = di("cache_v", [4, 2048, 1024])
        self.state_conv = di("state_conv", [120, 1024])
        self.ident_f_d = di("ident_f", [128, 128])
        self.cos_d = di("cos_t", [128, NT * 32])
        self.sin_d = di("sin_t", [128, NT * 32])
        self.maskg_d = di("maskg", [128, 19 * 128], BF16)
        self.maskc_d = di("maskc", [128, 19 * 128], BF16)
        self.masks_d = di("masks", [128, 48 * 32], BF16)
        self.maskn_d = di("maskn", [32, 32], BF16)
        self.flag_d = di("flag", [128, 1])
        self.newk = self.dram_out("newk", [NTOK, 1024])
        self.newv = self.dram_out("newv", [NTOK, 1024])
        self.conv_p = self.dram_out("conv_p", [30, 1024])
        self.conv_s = self.dram_out("conv_s", [120, 1024])
        nc = self.nc
        self.hsp = nc.dram_tensor("hsp", [NTOK, D], F32, kind="Internal").ap()
        self.xin_v = nc.dram_tensor("xin_v", [1024, 1024], BF16, kind="Internal").ap()
        self.xin_k = nc.dram_tensor("xin_k", [1024, 1024], BF16, kind="Internal").ap()
        self.xin_t = nc.dram_tensor("xin_t", [32, 1024], BF16, kind="Internal").ap()
        self.xout_v = nc.dram_tensor("xout_v", [2048, 1024], BF16, kind="Internal").ap()
        self.xout_k = nc.dram_tensor("xout_k", [2048, 1024], BF16, kind="Internal").ap()
        self.xout_t = nc.dram_tensor("xout_t", [64, 1024], BF16, kind="Internal").ap()

    def alloc_common(self):
        sb = self.sb
        self.CAT = sb("CAT", [128, 16, NTOK], BF16)
        self.RINGALL = sb("RINGALL", [128, NSLOT * (SLOT_BYTES // 2)], BF16)
        self.RING = [self.RINGALL[:, i * (SLOT_BYTES // 2):(i + 1) * (SLOT_BYTES // 2)] for i in range(NSLOT)]
        self.N0 = self.sb_off
        self.GB = sb("GB", [128, D], F32)
        self.XN = [sb("XN%d" % i, [128, D], BF16) for i in range(2)]
        self.SQ = sb("SQ", [128, D], BF16)
        self.IDB = sb("IDB", [128, 128], BF16)
        self.ONB = sb("ONB", [128, 128], BF16)
        self.EPSB = sb("EPSB", [128, 1], F32)
        self.SS = sb("SS", [128, 16], F32)
        self.RS = sb("RS", [128, 16], F32)
        self.FDUM = sb("FDUM", [128, 2], F32)
        self.R0 = self.sb_off
        off = self.R0
        self.H = []
        for t in range(NT):
            self.H.append(self.sb_at("H%d" % t, [128, D], F32, off))
            off += D * 4
        self.AT = self.sb_at("AT", [128, 8, NTOK], BF16, off)
        off += 8 * NTOK * 2
        self.SIL = []
        for i in range(2):
            self.SIL.append(self.sb_at("SIL%d" % i, [128, 512], F32, off))
            off += 2048
        assert off <= 224 * 1024, off
        self.R_end_ffn = off
        self.PP = [self.nc.alloc_psum_tensor("PP%d" % i, [128, 1024], F32) for i in range(4)]
        self.PS = []
        for i in range(4):
            self.PS.append(self.PP[i][:, 0:512])
            self.PS.append(self.PP[i][:, 512:1024])
        ppb = self.PP[3].bitcast(BF16)
        self.PT_bf = [ppb[:, 0:512], ppb[:, 1024:1536]]

        P = self.P
        P.op("sp", lambda e: e.dma_start(out=self.IDB[:], in_=self.ident_bf_d[:, :]), writes=["IDB"], dsem="c0")
        P.op("dve", lambda e: e.memset(self.ONB[:], 1.0), writes=["ONB"])
        P.op("dve", lambda e: e.memset(self.EPSB[:], EPS), writes=["EPSB"])
        self.w_list = []
        self.w_issued = 0
        self.w_next = 0
        self.w_done = 0

    def w_plan(self, aps):
        self.w_list.extend(aps)

    def w_pump(self):
        while self.w_issued < min(len(self.w_list), self.w_done + NSLOT):
            j = self.w_issued
            ap, shape = self.w_list[j]
            slot = self.RING[j % NSLOT]
            n = int(np.prod(shape[1:]))
            if len(shape) == 3:
                dst = slot[:, 0:n].rearrange("p (a b) -> p a b", b=shape[2])
            else:
                dst = slot[:, 0:n]
            self.P.op("pool", lambda e, dst=dst, ap=ap: e.dma_start(out=dst, in_=ap),
                      writes=[("RING", j % NSLOT)], dsem=("ring", j % NSLOT), nofence=True)
            self.w_issued += 1

    def w_get(self):
        i = self.w_next
        self.w_next += 1
        self.w_pump()
        assert i < self.w_issued, (i, self.w_issued, self.w_done)
        return self.RING[i % NSLOT], ("RING", i % NSLOT)

    def w_get_pair(self):
        i = self.w_next
        assert i % 2 == 0
        _, k0 = self.w_get()
        _, k1 = self.w_get()
        base = (i % NSLOT) * (SLOT_BYTES // 2)
        aps = [bass.AP(self.RINGALL, base + kc * 256, [[NSLOT * (SLOT_BYTES // 2), 128], [SLOT_BYTES // 2, 2], [1, 256]])
               for kc in range(16)]
        return aps, [k0, k1]

    def w_release(self, n=1):
        self.w_done += n
        self.w_pump()

    def rmsnorm_to_cat(self, gname, load_x=False, spill=False):
        P = self.P
        P.op("sp", lambda e: e.dma_start(out=self.GB[:], in_=self.ln[gname].partition_broadcast(128)),
             writes=["GB"], dsem="gb")
        for t, (r0, npt) in enumerate(TILES):
            H = self.H[t]
            if load_x:
                P.op("sp", lambda e, H=H, r0=r0, npt=npt: e.dma_start(out=H[0:npt, :], in_=self.x_tok[r0:r0 + npt, :]),
                     writes=[("H", t)], dsem=("xl", t % 4))
            P.op("act", lambda e, H=H, npt=npt, t=t: e.activation(out=self.SQ[0:npt, :], in_=H[0:npt, :], func=AF.Square,
                                                                accum_out=self.SS[0:npt, t:t + 1]),
                 reads=[("H", t)], writes=["SQ", ("SS", t)])
            P.op("act", lambda e, npt=npt, t=t: e.activation(out=self.RS[0:npt, t:t + 1], in_=self.SS[0:npt, t:t + 1], func=AF.Sqrt,
                                                             scale=1.0 / D, bias=self.EPSB[0:npt, :]),
                 reads=[("SS", t), "EPSB"], writes=[("RS", t)])
            P.op("dve", lambda e, npt=npt, t=t: e.reciprocal(out=self.RS[0:npt, t:t + 1], in_=self.RS[0:npt, t:t + 1]),
                 reads=[("RS", t)], writes=[("RS", t)])
            xn = self.XN[t % 2]
            P.op("dve", lambda e, H=H, xn=xn, npt=npt, t=t: e.scalar_tensor_tensor(
                out=xn[0:npt, :], in0=H[0:npt, :], scalar=self.RS[0:npt, t:t + 1], in1=self.GB[0:npt, :],
                op0=ALU.mult, op1=ALU.mult),
                reads=[("H", t), ("RS", t), "GB"], writes=[("XN", t % 2)])
            if spill:
                P.op("sp", lambda e, H=H, r0=r0, npt=npt: e.dma_start(out=self.hsp[r0:r0 + npt, :], in_=H[0:npt, :]),
                     reads=[("H", t)], writes=["hsp"], dsem=("hs", t % 4))
            for kq in range(4):
                half = self.nxt("T", 2)
                pt = self.PT_bf[half]
                for i in range(4):
                    kc = kq * 4 + i
                    P.op("pe", lambda e, pt=pt, xn=xn, kc=kc, i=i, npt=npt: e.transpose(
                        out=pt[:, i * 128:i * 128 + npt], in_=xn[0:npt, kc * 128:(kc + 1) * 128],
                        identity=self.IDB[0:npt, 0:npt]),
                        reads=[("XN", t % 2), "IDB"], writes=[("PS", 6 + half)])
                src = pt.rearrange("p (a b) -> p a b", b=128)[:, :, 0:npt]
                dst = self.CAT[:, kq * 4:kq * 4 + 4, r0:r0 + npt]
                if kq % 2 == 0:
                    P.op("act", lambda e, src=src, dst=dst: e.activation(out=dst, in_=src, func=AF.Copy),
                         writes=[("PS", 6 + half), ("CAT", kq, t)])
                else:
                    P.op("dve", lambda e, src=src, dst=dst: e.tensor_copy(out=dst, in_=src),
                         writes=[("PS", 6 + half), ("CAT", kq, t)])

    def ffn_plan(self, f):
        wg = self.wts[f + "_g"].rearrange("(kc p) c -> p kc c", p=128)
        wu = self.wts[f + "_u"].rearrange("(kc p) c -> p kc c", p=128)
        wd = self.wts[f + "_d"].rearrange("(j p) c -> p j c", p=128)
        lst = []
        j0 = 0
        for g in range(6):
            J = 8 if g < 5 else 4
            for jp in range(J // 2):
                c0 = (j0 + 2 * jp) * 128
                lst.append((wg[:, :, c0:c0 + 256], [128, 16, 256]))
                lst.append((wu[:, :, c0:c0 + 256], [128, 16, 256]))
            for c in range(4):
                lst.append((wd[:, j0:j0 + J, c * 512:(c + 1) * 512], [128, J, 512]))
            j0 += J
        self.w_plan(lst)

    def ffn(self, f, out_dram=None):
        P = self.P
        for g in range(6):
            J = 8 if g < 5 else 4
            for jp in range(J // 2):
                sg, kg = self.w_get()
                su, ku = self.w_get()
                vg = sg[:, :].rearrange("p (a b) -> p a b", b=256)
                vu = su[:, :].rearrange("p (a b) -> p a b", b=256)
                for s in range(2):
                    j = 2 * jp + s
                    for (n0, N, tiles) in GROUPS:
                        pb = self.nxt("GU", 2)
                        psg, psu = self.PS[2 * pb], self.PS[2 * pb + 1]
                        for (ps, v, kw, bank) in ((psg, vg, kg, 2 * pb), (psu, vu, ku, 2 * pb + 1)):
                            for kc in range(16):
                                P.op("pe", lambda e, ps=ps, v=v, kc=kc, s=s, n0=n0, N=N: e.matmul(
                                    ps[:, 0:N], lhsT=v[:, kc, s * 128:(s + 1) * 128], rhs=self.CAT[:, kc, n0:n0 + N],
                                    start=(kc == 0), stop=(kc == 15)),
                                    reads=[kw] + [("CAT", kc // 4, t) for t in tiles], writes=[("PS", bank)])
                        sb_ = self.nxt("SIL", 2)
                        sil = self.SIL[sb_]
                        P.op("act", lambda e, sil=sil, psg=psg, N=N: e.activation(out=sil[:, 0:N], in_=psg[:, 0:N], func=AF.Silu),
                             writes=[("PS", 2 * pb), ("SIL", sb_)])
                        P.op("dve", lambda e, sil=sil, psu=psu, j=j, n0=n0, N=N: e.tensor_tensor(
                            out=self.AT[:, j, n0:n0 + N], in0=sil[:, 0:N], in1=psu[:, 0:N], op=ALU.mult),
                            reads=[("SIL", sb_)], writes=[("PS", 2 * pb + 1)] + [("AT", j, t) for t in tiles])
                self.w_release(2)
            for c in range(4):
                sd, kd = self.w_get()
                vd = sd[:, 0:J * 512].rearrange("p (a b) -> p a b", b=512)
                for t, (r0, npt) in enumerate(TILES):
                    db = 4 + self.nxt("D", 2)
                    psd = self.PS[db]
                    for j in range(J):
                        P.op("pe", lambda e, psd=psd, vd=vd, j=j, r0=r0, npt=npt, st=(j == 0), sp=(j == J - 1): e.matmul(
                            psd[0:npt, :], lhsT=self.AT[:, j, r0:r0 + npt], rhs=vd[:, j, :],
                            start=st, stop=sp),
                            reads=[kd, ("AT", j, t)], writes=[("PS", db)])
                    H = self.H[t]
                    P.op("dve", lambda e, psd=psd, H=H, c=c, npt=npt: e.scalar_tensor_tensor(
                        out=H[0:npt, c * 512:(c + 1) * 512], in0=psd[0:npt, :], scalar=0.5,
                        in1=H[0:npt, c * 512:(c + 1) * 512], op0=ALU.mult, op1=ALU.add),
                        writes=[("PS", db), ("H", t)])
                    if out_dram is not None and g == 5 and c == 3:
                        P.op("sp", lambda e, H=H, r0=r0, npt=npt: e.dma_start(out=out_dram[r0:r0 + npt, :], in_=H[0:npt, :]),
                             reads=[("H", t)], dsem=("yo", t % 4))
                self.w_release(1)

    def mixer_plan(self):
        w_in = self.w_in.rearrange("(kc p) c -> p kc c", p=128)
        w_out = self.w_out.rearrange("(kc p) c -> p kc c", p=128)
        lst = []
        for cb in range(12):
            lst.append((w_in[:, :, cb * 256:(cb + 1) * 256], [128, 16, 256]))
        for i in range(4):
            lst.append((w_in[:, :, 3072 + i * 256:3072 + (i + 1) * 256], [128, 16, 256]))
            lst.append((w_in[:, :, 4096 + i * 256:4096 + (i + 1) * 256], [128, 16, 256]))
        for cb in range(8):
            lst.append((w_out[:, :, cb * 256:(cb + 1) * 256], [128, 16, 256]))
        self.w_plan(lst)

    def alloc_mixer(self):
        A = [self.R0]

        def al(name, shape, dt, ptr=A):
            nbytes = int(np.prod(shape[1:])) * (4 if dt == F32 else 2)
            t = self.sb_at(name, shape, dt, ptr[0])
            ptr[0] += (nbytes + 63) // 64 * 64
            assert ptr[0] <= 224 * 1024, (name, ptr[0])
            return t
        self.QT = al("QT", [128, 8, NTOK], BF16)
        self.KT = al("KT", [128, 8, NTOK], BF16)
        self.VS = al("VS", [128, 1024], BF16)
        self.IDF = al("IDF", [128, 128], F32)
        self.ONF = al("ONF", [128, 128], F32)
        self.DWT = al("DWT", [128, 8, 31], F32)
        self.CPT = al("CPT", [128, 24], F32)
        self.FLAG = al("FLAG", [128, 1], F32)
        self.UALL_off = A[0]
        self.UALL = al("UALL", [128, 8, 1208], BF16)
        self.UTP = al("UTP", [128, 8, 30], F32)
        self.UTS = al("UTS", [128, 8, 4, 38], F32)
        x0 = A[0]
        M = [x0]
        m = lambda n, sh, dt: al(n, sh, dt, M)
        self.COS = m("COS", [128, NT, 32], F32)
        self.SIN = m("SIN", [128, NT, 32], F32)
        self.GQK = m("GQK", [128, 2, 64], F32)
        self.ROPE = [[m("RA%d" % i, [128, NT, 64], F32), m("RB%d" % i, [128, NT, 64], F32)] for i in range(2)]
        self.SQF = m("SQF", [128, 512], F32)
        self.SS4 = m("SS4", [128, 8], F32)
        self.RS4 = m("RS4", [128, 8], F32)
        self.XS = m("XS", [128, 8, 64], F32)
        self.T2 = m("T2", [128, 8, 64], F32)
        self.OUTF = [m("OUTF%d" % i, [128, 8, 64], F32) for i in range(3)]
        self.XB16 = [m("XB16%d" % i, [128, 512], BF16) for i in range(3)]
        self.VF = [m("VF%d" % i, [128, 512], F32) for i in range(2)]
        self.VB = [m("VB%d" % i, [128, 512], BF16) for i in range(2)]
        self.CPL = m("CPL", [128, 128], F32)
        N = [self.N0]
        n = lambda nm, sh, dt: al(nm, sh, dt, N)
        self.SIG = [n("SIG%d" % i, [128, 512], F32) for i in range(2)]
        self.TMP2 = [n("TMP2%d" % i, [128, 512], F32) for i in range(2)]
        self.SCT = n("SCT", [128, 1024], F32)
        self.DWL = n("DWL", [128, 1024], F32)
        self.OST = n("OST", [128, 1024], F32)
        assert N[0] <= self.N0 + 20480, N[0]
        C = [x0]
        c = lambda nm, sh, dt: al(nm, sh, dt, C)
        self.Y = c("Y", [128, 8, 1176], F32)
        self.HALO = c("HALO", [128, 8, 32], BF16)
        self.YSQ = [c("YSQ%d" % i, [128, 256], F32) for i in range(2)]
        self.MUS = [c("MU%d" % i, [128, 256], F32) for i in range(2)]
        self.VARS = [c("VAR%d" % i, [128, 256], F32) for i in range(2)]
        self.RSTDS = [c("RSTD%d" % i, [128, 256], F32) for i in range(2)]
        self.TN = [c("TN%d" % i, [128, 256], F32) for i in range(2)]
        N2 = [self.N0]
        self.DG = [al("DG%d" % i, [128, 31, 128], BF16, N2) for i in range(2)]
        assert N2[0] <= self.N0 + 20480, N2[0]
        T = [self.UALL_off]
        a = lambda nm, sh, dt: al(nm, sh, dt, T)
        self.KCTX = [a("KCTX%d" % i, [128, 1024], BF16) for i in range(2)]
        self.VSTA = [a("VSTA%d" % i, [128, 16, 128], BF16) for i in range(2)]
        self.VSTB = [a("VSTB%d" % i, [128, 16, 128], BF16) for i in range(2)]
        self.MASKG = a("MASKG", [128, 19, 128], BF16)
        self.MASKC = a("MASKC", [128, 19, 128], BF16)
        self.EX = [a("EX%d" % i, [128, 1024], BF16) for i in range(2)]
        self.PTB2 = [a("PTB%d" % i, [128, 2, 512], BF16) for i in range(2)]
        self.RDEN = [a("RDEN%d" % i, [128, 512], F32) for i in range(2)]
        self.CKB = [a("CKB%d" % i, [128, 1024], BF16) for i in range(3)]
        self.CVB = [a("CVB%d" % i, [128, 1024], BF16) for i in range(3)]
        self.KTS = [a("KTS%d" % i, [128, 8, 128], BF16) for i in range(3)]
        self.EXS = [a("EXS%d" % i, [128, 512], BF16) for i in range(2)]
        self.PTS = [a("PTS%d" % i, [128, 16, 32], BF16) for i in range(2)]
        self.MASKS = a("MASKS", [128, 48, 32], BF16)
        self.MASKN = a("MASKN", [128, 32], BF16)
        self.RDS = a("RDS", [128, 512], F32)
        self.QBD = a("QBD", [128, 8, 64], BF16)

    def fence(self):
        self.P.fence(lambda e: e.memset(self.FDUM[:], 0.0))

    def spill_h(self):
        P = self.P
        for t, (r0, npt) in enumerate(TILES):
            H = self.H[t]
            P.op("sp", lambda e, H=H, r0=r0, npt=npt: e.dma_start(out=self.hsp[r0:r0 + npt, :], in_=H[0:npt, :]),
                 reads=[("H", t)], writes=["hsp"], dsem=("hs", t % 4))

    def reload_h(self):
        P = self.P
        for t, (r0, npt) in enumerate(TILES):
            H = self.H[t]
            P.op("sp", lambda e, H=H, r0=r0, npt=npt: e.dma_start(out=H[0:npt, :], in_=self.hsp[r0:r0 + npt, :]),
                 reads=["hsp"], writes=[("H", t)], dsem=("hs", t % 4))

    def pe_transpose_f32(self, bank, src_ap, rows, cols, key):
        ps = self.PS[bank]
        self.P.op("pe", lambda e, ps=ps, src_ap=src_ap, rows=rows, cols=cols: e.transpose(
            out=ps[0:cols, 0:rows], in_=src_ap, identity=self.IDF[0:rows, 0:rows]),
            reads=["IDF", key], writes=[("PS", bank)])

    def mixer_setup(self):
        P = self.P
        ld = lambda dst, src, key, ds: P.op("sp", lambda e: e.dma_start(out=dst, in_=src), writes=[key], dsem=ds)
        ld(self.IDF[:], self.ident_f_d[:, :], "IDF", "m0")
        ld(self.COS[:].rearrange("p a b -> p (a b)"), self.cos_d[:, :], "COS", "m1")
        ld(self.SIN[:].rearrange("p a b -> p (a b)"), self.sin_d[:, :], "SIN", "m2")
        ld(self.GQK[:, 0, :], self.q_norm.partition_broadcast(128), "GQ", "m3")
        ld(self.GQK[:, 1, :], self.k_norm.partition_broadcast(128), "GK", "m0")
        ld(self.FLAG[:], self.flag_d[:, :], "FLAG", "m1")
        ld(self.SCT[0:120, :], self.state_conv[:, :], "SCT", "m2")
        ld(self.DWL[0:31, :], self.conv_dw_w[:, :], "DWL", "m3")
        ld(self.CPL[0:24, :], self.cpar[:, :], "CPL", "m0")
        P.op("dve", lambda e: e.memset(self.ONF[:], 1.0), writes=["ONF"])
        for i, gk in enumerate(("GQ", "GK")):
            RA, RB = self.ROPE[i]
            g1 = self.GQK[:, i, 0:32].unsqueeze(1).to_broadcast([128, NT, 32])
            g2 = self.GQK[:, i, 32:64].unsqueeze(1).to_broadcast([128, NT, 32])
            P.op("dve", lambda e, RA=RA, g1=g1: e.tensor_tensor(out=RA[:, :, 0:32], in0=self.COS[:], in1=g1, op=ALU.mult),
                 reads=["COS", gk], writes=[("ROPE", i, 0)])
            P.op("dve", lambda e, RA=RA, g2=g2: e.tensor_tensor(out=RA[:, :, 32:64], in0=self.COS[:], in1=g2, op=ALU.mult),
                 reads=["COS", gk], writes=[("ROPE", i, 1)])
            P.op("dve", lambda e, RB=RB, g2=g2: e.scalar_tensor_tensor(out=RB[:, :, 0:32], in0=self.SIN[:], scalar=-1.0, in1=g2,
                                                                       op0=ALU.mult, op1=ALU.mult),
                 reads=["SIN", gk], writes=[("ROPE", i, 2)])
            P.op("dve", lambda e, RB=RB, g1=g1: e.tensor_tensor(out=RB[:, :, 32:64], in0=self.SIN[:], in1=g1, op=ALU.mult),
                 reads=["SIN", gk], writes=[("ROPE", i, 3)])

    def mixer_setup_pe(self):
        P = self.P
        bank = 4
        self.pe_transpose_f32(bank, self.CPL[0:24, :], 24, 128, "CPL")
        P.op("dve", lambda e: e.tensor_copy(out=self.CPT[:, :], in_=self.PS[4][:, 0:24]),
             writes=[("PS", 4), "CPT"])
        for c in range(8):
            bank = 4 + (c % 2)
            P.op("pe", lambda e, c=c, bank=bank: e.transpose(out=self.PS[bank][:, 0:31], in_=self.DWL[0:31, c * 128:(c + 1) * 128],
                                                             identity=self.IDF[0:31, 0:31]),
                 reads=["IDF", "DWL"], writes=[("PS", bank)])
            P.op("dve", lambda e, c=c, bank=bank: e.tensor_copy(out=self.DWT[:, c, :], in_=self.PS[bank][:, 0:31]),
                 writes=[("PS", bank), ("DWT", c)])
        for c in range(8):
            bank = 4 + (c % 2)
            P.op("pe", lambda e, c=c, bank=bank: e.transpose(out=self.PS[bank][:, 0:120], in_=self.SCT[0:120, c * 128:(c + 1) * 128],
                                                             identity=self.IDF[0:120, 0:120]),
                 reads=["IDF", "SCT"], writes=[("PS", bank)])
            src = self.PS[bank][:, 0:120].rearrange("p (b j) -> p b j", j=30)
            dst1 = self.UALL[:, c, 1054:1206].rearrange("p (b j) -> p b j", j=38)[:, :, 0:30]
            P.op("act", lambda e, src=src, dst1=dst1: e.activation(out=dst1, in_=src, func=AF.Copy),
                 writes=[("PS", bank), ("UALL", c, 2)])
            P.op("dve", lambda e, src=src, c=c: e.tensor_copy(out=self.UTS[:, c, :, 0:30], in_=src),
                 writes=[("PS", bank), ("UTS", c)])

    def proj_qk(self):
        P = self.P
        self.pend_tr = []
        for cbp in range(4):
            isk = cbp >= 2
            qi = 1 if isk else 0
            cloc = (cbp % 2) * 512
            dstT = self.KT if isk else self.QT
            rhs_aps, wks = self.w_get_pair()
            RA, RB = self.ROPE[qi]
            for t, (r0, npt) in enumerate(TILES):
                bank = self.nxt("PQ", 6)
                ps = self.PS[bank]
                for kc in range(16):
                    P.op("pe", lambda e, ps=ps, rhs=rhs_aps[kc], kc=kc, r0=r0, npt=npt: e.matmul(
                        ps[0:npt, :].rearrange("p (a b) -> p a b", b=256), lhsT=self.CAT[:, kc, r0:r0 + npt], rhs=rhs,
                        start=(kc == 0), stop=(kc == 15)),
                        reads=wks + [("CAT", kc // 4, t)], writes=[("PS", bank)])
                ps3 = ps[0:npt, :].rearrange("p (h d) -> p h d", d=64)
                P.op("act", lambda e, ps=ps, npt=npt: e.activation(out=self.SQF[0:npt, :], in_=ps[0:npt, :], func=AF.Square),
                     writes=[("PS", bank), "SQF"])
                P.op("dve", lambda e, npt=npt: e.tensor_reduce(out=self.SS4[0:npt, :], in_=self.SQF[0:npt, :].rearrange("p (h d) -> p h d", d=64),
                                                                axis=AX.X, op=ALU.add),
                     reads=["SQF"], writes=["SS4"])
                P.op("act", lambda e, npt=npt: e.activation(out=self.RS4[0:npt, :], in_=self.SS4[0:npt, :], func=AF.Sqrt,
                                                            scale=1.0 / 64, bias=self.EPSB[0:npt, :]),
                     reads=["SS4", "EPSB"], writes=["RS4"])
                P.op("dve", lambda e, npt=npt: e.reciprocal(out=self.RS4[0:npt, :], in_=self.RS4[0:npt, :]),
                     writes=["RS4"])
                P.op("dve", lambda e, ps3=ps3, npt=npt: e.tensor_tensor(
                    out=self.XS[0:npt], in0=ps3, in1=self.RS4[0:npt, :].unsqueeze(2).to_broadcast([npt, 8, 64]), op=ALU.mult),
                    reads=["RS4"], writes=[("PS", bank), "XS"])
                ob = self.nxt("OUTF", 3)
                OF = self.OUTF[ob]
                P.op("dve", lambda e, OF=OF, RA=RA, npt=npt, t=t: e.tensor_tensor(
                    out=OF[0:npt], in0=self.XS[0:npt], in1=RA[0:npt, t, :].unsqueeze(1).to_broadcast([npt, 8, 64]), op=ALU.mult),
                    reads=["XS", ("ROPE", qi, 0), ("ROPE", qi, 1)], writes=[("OUTF", ob)])
                P.op("dve", lambda e, RB=RB, npt=npt, t=t: e.tensor_tensor(
                    out=self.T2[0:npt, :, 0:32], in0=self.XS[0:npt, :, 32:64],
                    in1=RB[0:npt, t, 0:32].unsqueeze(1).to_broadcast([npt, 8, 32]), op=ALU.mult),
                    reads=["XS", ("ROPE", qi, 2)], writes=[("T2", 0)])
                P.op("dve", lambda e, RB=RB, npt=npt, t=t: e.tensor_tensor(
                    out=self.T2[0:npt, :, 32:64], in0=self.XS[0:npt, :, 0:32],
                    in1=RB[0:npt, t, 32:64].unsqueeze(1).to_broadcast([npt, 8, 32]), op=ALU.mult),
                    reads=["XS", ("ROPE", qi, 3)], writes=[("T2", 1)])
                P.op("dve", lambda e, OF=OF, npt=npt: e.tensor_tensor(out=OF[0:npt], in0=OF[0:npt], in1=self.T2[0:npt], op=ALU.add),
                     reads=[("T2", 0), ("T2", 1)], writes=[("OUTF", ob)])
                if isk:
                    P.op("sp", lambda e, OF=OF, r0=r0, npt=npt, cloc=cloc: e.dma_start(
                        out=self.newk[r0:r0 + npt, cloc:cloc + 512], in_=OF[0:npt].rearrange("p h d -> p (h d)")),
                        reads=[("OUTF", ob)], dsem=("ko", self.nxt("ko", 4)))
                xb = self.nxt("XB", 3)
                XB = self.XB16[xb]
                P.op("act", lambda e, XB=XB, OF=OF, npt=npt: e.activation(out=XB[0:npt, :], in_=OF[0:npt].rearrange("p h d -> p (h d)"),
                                                                          func=AF.Copy),
                     reads=[("OUTF", ob)], writes=[("XB", xb)])

                def stage2(XB=XB, xb=xb, npt=npt, r0=r0, t=t, cbp=cbp, isk=isk, dstT=dstT):
                    half = self.nxt("T", 2)
                    pt = self.PT_bf[half]
                    for i in range(4):
                        P.op("pe", lambda e, pt=pt, XB=XB, i=i, npt=npt: e.transpose(
                            out=pt[:, i * 128:i * 128 + npt], in_=XB[0:npt, i * 128:(i + 1) * 128], identity=self.IDB[0:npt, 0:npt]),
                            reads=[("XB", xb), "IDB"], writes=[("PS", 6 + half)])
                    src = pt.rearrange("p (a b) -> p a b", b=128)[:, :, 0:npt]
                    c0 = 4 * (cbp % 2)
                    dst = dstT[:, c0:c0 + 4, r0:r0 + npt]
                    kks = [("KT" if isk else "QT", c0 // 2, t), ("KT" if isk else "QT", c0 // 2 + 1, t)]
                    if self.nxt("evq", 2) == 0:
                        P.op("act", lambda e, src=src, dst=dst: e.activation(out=dst, in_=src, func=AF.Copy),
                             writes=[("PS", 6 + half)] + kks)
                    else:
                        P.op("dve", lambda e, src=src, dst=dst: e.tensor_copy(out=dst, in_=src),
                             writes=[("PS", 6 + half)] + kks)
                self.pend_tr.append(stage2)
                if len(self.pend_tr) > 2:
                    self.pend_tr.pop(0)()
            self.w_release(2)
        while self.pend_tr:
            self.pend_tr.pop(0)()

    def proj_v(self):
        P = self.P
        for cbp in range(2):
            cloc = cbp * 512
            rhs_aps, wks = self.w_get_pair()
            for t, (r0, npt) in enumerate(TILES):
                bank = self.nxt("PQ", 6)
                ps = self.PS[bank]
                for kc in range(16):
                    P.op("pe", lambda e, ps=ps, rhs=rhs_aps[kc], kc=kc, r0=r0, npt=npt: e.matmul(
                        ps[0:npt, :].rearrange("p (a b) -> p a b", b=256), lhsT=self.CAT[:, kc, r0:r0 + npt], rhs=rhs,
                        start=(kc == 0), stop=(kc == 15)),
                        reads=wks + [("CAT", kc // 4, t)], writes=[("PS", bank)])
                vb = self.nxt("VF", 2)
                VF = self.VF[vb]
                P.op("act", lambda e, VF=VF, ps=ps, npt=npt: e.activation(out=VF[0:npt, :], in_=ps[0:npt, :], func=AF.Copy),
                     writes=[("PS", bank), ("VF", vb)])
                P.op("sp", lambda e, VF=VF, r0=r0, npt=npt, cloc=cloc: e.dma_start(out=self.newv[r0:r0 + npt, cloc:cloc + 512], in_=VF[0:npt, :]),
                     reads=[("VF", vb)], dsem=("vo", self.nxt("vo", 4)))
                if t < 8:
                    v2 = self.nxt("VB", 2)
                    VB = self.VB[v2]
                    P.op("dve", lambda e, VB=VB, VF=VF: e.tensor_copy(out=VB[:, :], in_=VF[:, :]),
                         reads=[("VF", vb)], writes=[("VB", v2)])
                    P.op("sp", lambda e, VB=VB, r0=r0, cloc=cloc: e.dma_start(out=self.xin_v[r0:r0 + 128, cloc:cloc + 512], in_=VB[:, :]),
                         reads=[("VB", v2)], writes=["xin_v"], dsem=("xv", self.nxt("xv", 4)))
                else:
                    P.op("dve", lambda e, VF=VF, cloc=cloc: e.tensor_copy(out=self.VS[0:32, cloc:cloc + 512], in_=VF[0:32, :]),
                         reads=[("VF", vb)], writes=[("VS", cbp)])
            self.w_release(2)

    def proj_glu(self):
        P = self.P
        for i in range(4):
            sa, ka = self.w_get()
            sb_, kb = self.w_get()
            va = sa[:, :].rearrange("p (a b) -> p a b", b=256)
            vb = sb_[:, :].rearrange("p (a b) -> p a b", b=256)
            for s in range(2):
                c = 2 * i + s
                for gi, (n0, N, tiles) in enumerate(GROUPS):
                    pb = self.nxt("GU", 2)
                    psa, psb = self.PS[2 * pb], self.PS[2 * pb + 1]
                    for (ps, v, kw, bank) in ((psa, va, ka, 2 * pb), (psb, vb, kb, 2 * pb + 1)):
                        for kc in range(16):
                            P.op("pe", lambda e, ps=ps, v=v, kc=kc, s=s, n0=n0, N=N: e.matmul(
                                ps[:, 0:N], lhsT=v[:, kc, s * 128:(s + 1) * 128], rhs=self.CAT[:, kc, n0:n0 + N],
                                start=(kc == 0), stop=(kc == 15)),
                                reads=[kw] + [("CAT", kc // 4, t) for t in tiles], writes=[("PS", bank)])
                    sg = self.nxt("SIG", 2)
                    SIG, TMP = self.SIG[sg], self.TMP2[sg]
                    P.op("act", lambda e, SIG=SIG, psb=psb, N=N: e.activation(out=SIG[:, 0:N], in_=psb[:, 0:N], func=AF.Tanh, scale=0.5),
                         writes=[("PS", 2 * pb + 1), ("SIG", sg)])
                    P.op("dve", lambda e, SIG=SIG, TMP=TMP, psa=psa, N=N: e.scalar_tensor_tensor(
                        out=TMP[:, 0:N], in0=SIG[:, 0:N], scalar=1.0, in1=psa[:, 0:N], op0=ALU.add, op1=ALU.mult),
                        reads=[("SIG", sg)], writes=[("PS", 2 * pb), ("TMP2", sg)])
                    if gi < 2:
                        dst = self.UALL[:, c, 30 + n0:30 + n0 + N]
                        src = TMP[:, 0:N]
                    else:
                        dst = self.UALL[:, c, 1054:1206].rearrange("p (b j) -> p b j", j=38)[:, :, 30:38]
                        src = TMP[:, 0:32].rearrange("p (b j) -> p b j", j=8)
                    P.op("act", lambda e, dst=dst, src=src: e.activation(out=dst, in_=src, func=AF.Copy, scale=0.5),
                         reads=[("TMP2", sg)], writes=[("UALL", c, gi)])
                    if gi == 1:
                        P.op("dve", lambda e, TMP=TMP, c=c: e.tensor_scalar(out=self.UTP[:, c, :], in0=TMP[:, 482:512], scalar1=0.5, scalar2=None,
                                                                             op0=ALU.mult),
                             reads=[("TMP2", sg)], writes=[("UTP", c)])
                    if gi == 2:
                        P.op("dve", lambda e, src=src, c=c: e.tensor_scalar(out=self.UTS[:, c, :, 30:38], in0=src, scalar1=0.5, scalar2=None,
                                                                             op0=ALU.mult),
                             reads=[("TMP2", sg)], writes=[("UTS", c)])
            self.w_release(2)

    def conv_state_outputs(self):
        P = self.P
        jobs = [(self.conv_p[:, :], [self.UTP[:, c, :] for c in range(8)], [("UTP", c) for c in range(8)])]
        for b in range(4):
            jobs.append((self.conv_s[b * 30:(b + 1) * 30, :], [self.UTS[:, c, b, 8:38] for c in range(8)], [("UTS", c) for c in range(8)]))
        for (dst, srcs, keys) in jobs:
            for hf in range(2):
                bank = 4 + hf
                for i in range(4):
                    c = hf * 4 + i
                    P.op("pe", lambda e, bank=bank, i=i, src=srcs[c]: e.transpose(
                        out=self.PS[bank][0:30, i * 128:(i + 1) * 128], in_=src, identity=self.IDF[:, :]),
                        reads=["IDF", keys[c]], writes=[("PS", bank)])
                if hf == 0:
                    P.op("act", lambda e, bank=bank: e.activation(out=self.OST[0:30, 0:512], in_=self.PS[bank][0:30, :], func=AF.Copy),
                         writes=[("PS", bank), ("OST", 0)])
                else:
                    P.op("dve", lambda e, bank=bank: e.tensor_copy(out=self.OST[0:30, 512:1024], in_=self.PS[bank][0:30, :]),
                         writes=[("PS", bank), ("OST", 1)])
            P.op("sp", lambda e, dst=dst: e.dma_start(out=dst, in_=self.OST[0:30, :]),
                 reads=[("OST", 0), ("OST", 1)], dsem=("co", self.nxt("co", 2)))

    def _allgather(self, nm, src, dst):
        groups = [[2 * i, 2 * i + 1] for i in range(self.ncores // 2)]
        self.P.op("pool", lambda e: e.collective_compute("AllGather", ALU.bypass, replica_groups=groups, ins=[src], outs=[dst]),
                  reads=["xin_" + nm], writes=["xout_" + nm], dsem="cc_" + nm, dinc=self.cc_inc, nofence=True)

    def exchange_k(self):
        P = self.P
        for c in range(8):
            P.op("sp", lambda e, c=c: e.dma_start(out=self.xin_k[c * 128:(c + 1) * 128, :], in_=self.KT[:, c, 0:1024]),
                 reads=[("KT", c // 2, t) for t in range(9)], writes=["xin_k"], dsem=("xk", c % 4))
        self._allgather("k", self.xin_k, self.xout_k)

    def exchange_v(self):
        self._allgather("v", self.xin_v, self.xout_v)

    def exchange_t(self):
        P = self.P
        tail = self.xin_t.rearrange("r (q j) -> (r q) j", j=32).rearrange("(c p) j -> p c j", p=128)
        P.op("sp", lambda e: e.dma_start(out=tail[:, :, :], in_=self.UALL[:, :, 1024:1056]),
             reads=[("UALL", c, 1) for c in range(8)] + [("UALL", c, 2) for c in range(8)], writes=["xin_t"], dsem="xt")
        self._allgather("t", self.xin_t, self.xout_t)

    def conv_ln(self):
        P = self.P
        hsrc = self.xout_t[0:32, :].rearrange("r (q j) -> (r q) j", j=32).rearrange("(c p) j -> p c j", p=128)
        P.op("sp", lambda e: e.dma_start(out=self.HALO[:, :, :], in_=hsrc), reads=["xout_t"], writes=["HALO"], dsem="hl")
        P.op("dve", lambda e: e.tensor_scalar(out=self.UALL[:, :, 0:30], in0=self.HALO[:, :, 0:30], scalar1=self.FLAG[:, 0:1], scalar2=None,
                                              op0=ALU.mult),
             reads=["HALO", "FLAG"], writes=[("UALL", c, 3) for c in range(8)])
        pieces = [(0, 512), (512, 512), (1054, 122)]
        ycol = [0, 512, 1024]
        for c in range(8):
            DG = self.DG[c % 2]
            P.op("dve", lambda e, DG=DG, c=c: e.tensor_tensor(
                out=DG[:, :, :], in0=self.IDB[:, :].unsqueeze(1).to_broadcast([128, 31, 128]),
                in1=self.DWT[:, c, :].unsqueeze(2).to_broadcast([128, 31, 128]), op=ALU.mult),
                reads=["IDB", ("DWT", c)], writes=[("DG", c % 2)])
            for pi, (a0, N) in enumerate(pieces):
                bank = self.nxt("CV", 4)
                ps = self.PS[bank]
                for k in range(31):
                    P.op("pe", lambda e, ps=ps, DG=DG, k=k, c=c, a0=a0, N=N: e.matmul(
                        ps[:, 0:N], lhsT=DG[:, k, :], rhs=self.UALL[:, c, a0 + k:a0 + k + N], start=(k == 0), stop=(k == 30)),
                        reads=[("DG", c % 2)] + [("UALL", c, g) for g in range(4)], writes=[("PS", bank)])
                P.op("act", lambda e, ps=ps, c=c, y0=ycol[pi], N=N: e.activation(
                    out=self.Y[:, c, y0:y0 + N], in_=ps[:, 0:N], func=AF.Identity, bias=self.CPT[:, c:c + 1]),
                    reads=["CPT"], writes=[("PS", bank), ("Y", c, pi)])
        lnp = [(0, 256, 0), (256, 256, 0), (512, 256, 1), (768, 256, 1), (1024, 122, 2)]

        def stats(li):
            y0, N, pi = lnp[li]
            MU, VAR, RSTD = self.MUS[li % 2], self.VARS[li % 2], self.RSTDS[li % 2]
            kx = li % 2
            for c in range(8):
                yb = self.nxt("YSQ", 2)
                YSQ = self.YSQ[yb]
                P.op("act", lambda e, YSQ=YSQ, c=c, y0=y0, N=N: e.activation(out=YSQ[:, 0:N], in_=self.Y[:, c, y0:y0 + N], func=AF.Square),
                     reads=[("Y", c, pi)], writes=[("YSQ", yb)])
                P.op("pe", lambda e, c=c, y0=y0, N=N: e.matmul(self.PS[4][:, 0:N], lhsT=self.ONF[:, :], rhs=self.Y[:, c, y0:y0 + N],
                                                               start=(c == 0), stop=(c == 7)),
                     reads=["ONF", ("Y", c, pi)], writes=[("PS", 4)])
                P.op("pe", lambda e, YSQ=YSQ, c=c, N=N: e.matmul(self.PS[5][:, 0:N], lhsT=self.ONF[:, :], rhs=YSQ[:, 0:N],
                                                                 start=(c == 0), stop=(c == 7)),
                     reads=["ONF", ("YSQ", yb)], writes=[("PS", 5)])
            P.op("dve", lambda e, N=N: e.tensor_scalar(out=MU[:, 0:N], in0=self.PS[4][:, 0:N], scalar1=1.0 / 1024, scalar2=None, op0=ALU.mult),
                 writes=[("PS", 4), ("MU", kx)])
            P.op("dve", lambda e, N=N: e.tensor_tensor(out=VAR[:, 0:N], in0=MU[:, 0:N], in1=MU[:, 0:N], op=ALU.mult),
                 reads=[("MU", kx)], writes=[("VAR", kx)])
            P.op("dve", lambda e, N=N: e.scalar_tensor_tensor(out=VAR[:, 0:N], in0=self.PS[5][:, 0:N], scalar=1.0 / 1024, in1=VAR[:, 0:N],
                                                              op0=ALU.mult, op1=ALU.subtract),
                 writes=[("PS", 5), ("VAR", kx)])
            P.op("act", lambda e, N=N: e.activation(out=RSTD[:, 0:N], in_=VAR[:, 0:N], func=AF.Sqrt, bias=self.EPSB[:, :]),
                 reads=[("VAR", kx), "EPSB"], writes=[("RSTD", kx)])
            P.op("dve", lambda e, N=N: e.reciprocal(out=RSTD[:, 0:N], in_=RSTD[:, 0:N]), writes=[("RSTD", kx)])

        def normalize(li):
            y0, N, pi = lnp[li]
            MU, RSTD = self.MUS[li % 2], self.RSTDS[li % 2]
            kx = li % 2
            for c in range(8):
                tb = self.nxt("TN", 2)
                TN = self.TN[tb]
                P.op("dve", lambda e, TN=TN, c=c, y0=y0, N=N: e.tensor_tensor(out=TN[:, 0:N], in0=self.Y[:, c, y0:y0 + N], in1=MU[:, 0:N],
                                                                             op=ALU.subtract),
                     reads=[("Y", c, pi), ("MU", kx)], writes=[("TN", tb)])
                P.op("dve", lambda e, TN=TN, N=N: e.tensor_tensor(out=TN[:, 0:N], in0=TN[:, 0:N], in1=RSTD[:, 0:N], op=ALU.mult),
                     reads=[("RSTD", kx)], writes=[("TN", tb)])
                if pi < 2:
                    dst = self.CAT[:, 8 + c, y0:y0 + N]
                    src = TN[:, 0:N]
                    tiles = (y0 // 128, y0 // 128 + 1)
                else:
                    dst = self.CAT[:, 8 + c, 1024:1056].rearrange("p (b j) -> p b j", j=8)
                    src = TN[:, 0:152].rearrange("p (b j) -> p b j", j=38)[:, :, 0:8]
                    tiles = (8,)
                P.op("act", lambda e, dst=dst, src=src, c=c: e.activation(out=dst, in_=src, func=AF.Silu,
                                                                          scale=self.CPT[:, 8 + c:9 + c], bias=self.CPT[:, 16 + c:17 + c]),
                     reads=[("TN", tb), "CPT"], writes=[("CAT", (8 + c) // 4, t) for t in tiles])

        stats(0)
        for li in range(len(lnp)):
            if li + 1 < len(lnp):
                stats(li + 1)
            normalize(li)

    def attention(self):
        P = self.P
        ld = lambda dst, src, key, ds: P.op("sp", lambda e: e.dma_start(out=dst, in_=src), writes=[key], dsem=ds)
        ld(self.MASKG[:].rearrange("p a b -> p (a b)"), self.maskg_d[:, :], "MASKG", "m0")
        ld(self.MASKC[:].rearrange("p a b -> p (a b)"), self.maskc_d[:, :], "MASKC", "m1")
        ld(self.MASKS[:].rearrange("p a b -> p (a b)"), self.masks_d[:, :], "MASKS", "m2")
        ld(self.MASKN[0:32, :], self.maskn_d[:, :], "MASKN", "m3")
        for kb in range(2):
            P.op("pool", lambda e, kb=kb: e.memset(self.VSTA[kb][:, :, 64:128], 1.0), writes=[("VONE", 0, kb)])
            P.op("pool", lambda e, kb=kb: e.memset(self.VSTB[kb][:, :, 0:64], 1.0), writes=[("VONE", 1, kb)])

        def loads(c):
            kb = c % 2
            KC = self.KCTX[kb]
            P.op("sp", lambda e, KC=KC, c=c: e.dma_start(out=KC[:, :], in_=self.xout_k[c * 128:(c + 1) * 128, :]),
                 reads=["xout_k"], writes=[("KCTX", kb)], dsem=("kc", kb))
            for e_, VT in ((0, self.VSTA[kb]), (1, self.VSTB[kb])):
                f0 = c * 128 + e_ * 64
                P.op("sp", lambda e, VT=VT, f0=f0, e_=e_: e.dma_start(
                    out=VT[:, 0:8, e_ * 64:(e_ + 1) * 64], in_=self.xout_v[0:1024, f0:f0 + 64].rearrange("(t p) f -> p t f", p=128)),
                    reads=["xout_v"], writes=[("VST", e_, kb, 0)], dsem=("vc", e_, kb))
                P.op("sp", lambda e, VT=VT, f0=f0, e_=e_: e.dma_start(
                    out=VT[:, 8:16, e_ * 64:(e_ + 1) * 64], in_=self.xin_v[0:1024, f0:f0 + 64].rearrange("(t p) f -> p t f", p=128)),
                    reads=["xin_v"], writes=[("VST", e_, kb, 1)], dsem=("vo2", e_, kb))

        steps = []
        for c in range(8):
            for e_ in range(2):
                for qg in range(2):
                    blocks = [(0, j) for j in range(8)] + [(1, j) for j in range(4 * qg + 4)]
                    nd = self.nxt("ND", 2)
                    npair = len(blocks) // 2
                    for bp in range(npair):
                        steps.append(dict(c=c, e_=e_, qg=qg, bp=bp, npair=npair, blk=blocks[2 * bp:2 * bp + 2], nd=nd,
                                          pp=self.nxt("SPP", 2), ex=self.nxt("EX", 2),
                                          pb=[self.nxt("PTB", 2)],
                                          rd=self.nxt("RDEN", 2) if bp == npair - 1 else None,
                                          newc=(e_ == 0 and qg == 0 and bp == 0)))

        def stageA(s):
            c, e_, qg = s["c"], s["e_"], s["qg"]
            kb = c % 2
            KC = self.KCTX[kb]
            r0, r1 = e_ * 64, (e_ + 1) * 64
            q0 = qg * 512
            PPt = self.PP[s["pp"]]
            for h2 in range(2):
                own, j = s["blk"][h2]
                if own:
                    lhs = self.KT[r0:r1, c, j * 128:(j + 1) * 128]
                    rk = [("KT", c // 2, j)]
                else:
                    lhs = KC[r0:r1, j * 128:(j + 1) * 128]
                    rk = [("KCTX", kb)]
                off = 128 * max(0, j - 4 * qg) if own else 0
                P.op("pe", lambda e, PPt=PPt, h2=h2, lhs=lhs, c=c, r0=r0, r1=r1, q0=q0, off=off: e.matmul(
                    PPt[:, h2 * 512 + off:(h2 + 1) * 512], lhsT=lhs, rhs=self.QT[r0:r1, c, q0 + off:q0 + 512], start=True, stop=True),
                    reads=rk + [("QT", c // 2, t) for t in range(4 * qg, 4 * qg + 4)], writes=[("PS", 2 * s["pp"] + h2)])

        def stageB(s):
            PPt = self.PP[s["pp"]]
            EX = self.EX[s["ex"]]
            P.op("act", lambda e, EX=EX, PPt=PPt: e.activation(out=EX[:, :], in_=PPt[:, :], func=AF.Exp, scale=0.125),
                 writes=[("PS", 2 * s["pp"]), ("PS", 2 * s["pp"] + 1), ("EX", s["ex"])])

        def stageC(s):
            c, e_, qg, bp, npair, nd = s["c"], s["e_"], s["qg"], s["bp"], s["npair"], s["nd"]
            kb = c % 2
            VT = (self.VSTA if e_ == 0 else self.VSTB)[kb]
            r0, r1 = e_ * 64, (e_ + 1) * 64
            o0, o1 = (1 - e_) * 64, (2 - e_) * 64
            q0 = qg * 512
            bn = 4 + nd
            EX = self.EX[s["ex"]]
            own, j = s["blk"][0]
            assert s["blk"][1] == (own, j + 1)
            mt, mkey = (self.MASKG, "MASKG") if own else (self.MASKC, "MASKC")
            d0 = (4 * qg - j) if own else (8 + 4 * qg - j)
            mk = bass.AP(mt, (d0 + 3) * 128, [[19 * 128, 128], [-128, 2], [1, 512]])
            pb = s["pb"][0]
            PT = self.PTB2[pb % 2]
            P.op("dve", lambda e, PT=PT, EX=EX, mk=mk: e.tensor_tensor(
                out=PT[:, :, :], in0=EX[:, :].rearrange("p (a b) -> p a b", b=512), in1=mk, op=ALU.mult),
                reads=[("EX", s["ex"]), mkey], writes=[("PTB", pb % 2)])
            for h2 in range(2):
                own, j = s["blk"][h2]
                vt = VT[:, (8 + j) if own else j, :]
                vk = ("VST", e_, kb, 1 if own else 0)
                first = (bp == 0 and h2 == 0)
                last = (bp == npair - 1 and h2 == 1)
                off = 128 * max(0, j - 4 * qg) if own else 0
                P.op("pe", lambda e, PT=PT, vt=vt, bn=bn, first=first, last=last, h2=h2, off=off: e.matmul(
                    self.PS[bn][:, off:512], lhsT=vt, rhs=PT[:, h2, off:512], start=first, stop=last),
                    reads=[("PTB", pb % 2), vk, ("VONE", e_, kb)], writes=[("PS", bn)])
            if bp == npair - 1:
                rd = s["rd"]
                RD = self.RDEN[rd]

                def norm(RD=RD, rd=rd, bn=bn, r0=r0, r1=r1, o0=o0, o1=o1, c=c, q0=q0, qg=qg):
                    P.op("dve", lambda e: e.reciprocal(out=RD[r0:r1, :], in_=self.PS[bn][o0:o1, :]),
                         writes=[("PS", bn), ("RDEN", rd)])
                    P.op("dve", lambda e: e.tensor_tensor(
                        out=self.CAT[r0:r1, c, q0:q0 + 512], in0=self.PS[bn][r0:r1, :], in1=RD[r0:r1, :], op=ALU.mult),
                        reads=[("RDEN", rd)], writes=[("PS", bn)] + [("CAT", c // 4, t) for t in range(4 * qg, 4 * qg + 4)])
                pend_norm.append([2, norm])
            for pn in pend_norm[:]:
                if pn[0] == 0:
                    pn[1]()
                    pend_norm.remove(pn)
                else:
                    pn[0] -= 1

        pend_norm = []
        loads(0)
        LOOK = 2
        n = len(steps)
        for i in range(min(LOOK, n)):
            stageA(steps[i])
        for i in range(n):
            s_ = steps[i]
            if s_["newc"] and s_["c"] + 1 < 8:
                loads(s_["c"] + 1)
            stageB(s_)
            if i + LOOK < n:
                stageA(steps[i + LOOK])
            stageC(s_)
        for pn in pend_norm:
            pn[1]()

    def sample_attention(self):
        P = self.P
        NUM, DEN = 2, 3
        P.op("dve", lambda e: e.memset(self.QBD[:], 0.0), writes=["QBD"])
        for e_ in range(2):
            P.op("dve", lambda e, e_=e_: e.tensor_copy(out=self.QBD[e_ * 64:(e_ + 1) * 64, :, e_ * 32:(e_ + 1) * 32],
                                                      in_=self.QT[e_ * 64:(e_ + 1) * 64, :, 1024:1056]),
                 reads=[("QT", cp, 8) for cp in range(4)], writes=["QBD"])
        steps = []
        for (b, r) in [(b, r) for b in range(4) for r in range(12)] + [(-1, -1)]:
            steps.append(dict(b=b, r=r, new=(b < 0), sb=self.nxt("SS", 2), cb=self.nxt("CK", 3) if b >= 0 else None,
                              ex=self.nxt("EXS", 2)))
        nst = len(steps)

        def stageA(s):
            b, r, new, sb_ = s["b"], s["r"], s["new"], s["sb"]
            ps = self.PS[sb_]
            if not new:
                cb = s["cb"]
                CKB, CVB, KTS = self.CKB[cb], self.CVB[cb], self.KTS[cb]
                if r < 8:
                    ksrc = self.cache_k[b].rearrange("(m r) f -> r m f", r=16)[r]
                    vsrc = self.cache_v[b].rearrange("(m r) f -> r m f", r=16)[r]
                else:
                    ksrc = self.cache_k[b, (r + 4) * 128:(r + 5) * 128, :]
                    vsrc = self.cache_v[b, (r + 4) * 128:(r + 5) * 128, :]
                P.op("pool", lambda e, CKB=CKB, ksrc=ksrc: e.dma_start(out=CKB[:, :], in_=ksrc),
                     writes=[("CKB", cb)], dsem=("ck", cb))
                P.op("pool", lambda e, CVB=CVB, vsrc=vsrc: e.dma_start(out=CVB[:, :], in_=vsrc),
                     writes=[("CVB", cb)], dsem=("cv", cb))
                for hq in range(2):
                    half = self.nxt("T", 2)
                    pt = self.PT_bf[half]
                    for i in range(4):
                        c = hq * 4 + i
                        P.op("pe", lambda e, pt=pt, CKB=CKB, c=c, i=i: e.transpose(
                            out=pt[:, i * 128:(i + 1) * 128], in_=CKB[:, c * 128:(c + 1) * 128], identity=self.IDB[:, :]),
                            reads=[("CKB", cb), "IDB"], writes=[("PS", 6 + half)])
                    src = pt.rearrange("p (a b) -> p a b", b=128)
                    dst = KTS[:, hq * 4:hq * 4 + 4, :]
                    if hq == 0:
                        P.op("act", lambda e, src=src, dst=dst: e.activation(out=dst, in_=src, func=AF.Copy),
                             writes=[("PS", 6 + half), ("KTS", cb, hq)])
                    else:
                        P.op("dve", lambda e, src=src, dst=dst: e.tensor_copy(out=dst, in_=src),
                             writes=[("PS", 6 + half), ("KTS", cb, hq)])
                npos = 128
            else:
                npos = 32
            for c in range(8):
                if new:
                    lhs = self.KT[:, c, 1024:1056]
                    rk = [("KT", c // 2, 8)]
                else:
                    lhs = KTS[:, c, :]
                    rk = [("KTS", s["cb"], c // 4)]
                P.op("pe", lambda e, ps=ps, lhs=lhs, c=c, npos=npos: e.matmul(
                    ps[0:npos, c * 64:(c + 1) * 64], lhsT=lhs, rhs=self.QBD[:, c, :], start=True, stop=True),
                    reads=rk + ["QBD"], writes=[("PS", sb_)])

        def stageB(s):
            b, r, new, sb_, ex = s["b"], s["r"], s["new"], s["sb"], s["ex"]
            ps = self.PS[sb_]
            npos = 32 if new else 128
            EXS, PTS = self.EXS[ex], self.PTS[ex]
            P.op("act", lambda e, EXS=EXS, ps=ps, npos=npos: e.activation(out=EXS[0:npos, :], in_=ps[0:npos, :], func=AF.Exp, scale=0.125),
                 writes=[("PS", sb_), ("EXS", ex)])
            if new:
                mk = self.MASKN[0:32, :].unsqueeze(1).to_broadcast([32, 16, 32])
                mkey = "MASKN"
            else:
                mk = self.MASKS[:, b * 12 + r, :].unsqueeze(1).to_broadcast([128, 16, 32])
                mkey = "MASKS"
            P.op("dve", lambda e, PTS=PTS, EXS=EXS, mk=mk, npos=npos: e.tensor_tensor(
                out=PTS[0:npos], in0=EXS[0:npos, :].rearrange("p (h q) -> p h q", q=32), in1=mk, op=ALU.mult),
                reads=[("EXS", ex), mkey], writes=[("PTS", ex)])

        def stageC(s, si):
            new, ex = s["new"], s["ex"]
            npos = 32 if new else 128
            PTS = self.PTS[ex]
            first, last = si == 0, si == nst - 1
            for c in range(8):
                if new:
                    lhs = self.VS[0:32, c * 128:(c + 1) * 128]
                    rk = [("VS", c // 4)]
                else:
                    lhs = self.CVB[s["cb"]][:, c * 128:(c + 1) * 128]
                    rk = [("CVB", s["cb"])]
                P.op("pe", lambda e, lhs=lhs, PTS=PTS, c=c, npos=npos, st=(first and c == 0), sp=(last and c == 7): e.matmul(
                    self.PS[NUM][:, c * 64:(c + 1) * 64], lhsT=lhs, rhs=PTS[0:npos, 2 * c:2 * c + 2, :], start=st, stop=sp,
                    skip_group_check=True),
                    reads=rk + [("PTS", ex)], writes=[("PS", NUM)])
            P.op("pe", lambda e, PTS=PTS, npos=npos, first=first, last=last: e.matmul(
                self.PS[DEN][:, :], lhsT=self.ONB[0:npos, :], rhs=PTS[0:npos].rearrange("p h q -> p (h q)"), start=first, stop=last),
                reads=["ONB", ("PTS", ex)], writes=[("PS", DEN)])

        stageA(steps[0])
        if nst > 1:
            stageA(steps[1])
        for si in range(nst):
            stageB(steps[si])
            stageC(steps[si], si)
            if si + 2 < nst:
                stageA(steps[si + 2])
        P.op("dve", lambda e: e.reciprocal(out=self.RDS[:, :], in_=self.PS[DEN][:, :]), writes=[("PS", DEN), "RDS"])
        for e_ in range(2):
            r0, r1 = e_ * 64, (e_ + 1) * 64
            num = self.PS[NUM][r0:r1, :].rearrange("p (c e q) -> p c e q", e=2, q=32)[:, :, e_, :]
            rds = self.RDS[r0:r1, :].rearrange("p (c e q) -> p c e q", e=2, q=32)[:, :, e_, :]
            P.op("dve", lambda e, num=num, rds=rds, r0=r0, r1=r1: e.tensor_tensor(out=self.CAT[r0:r1, 0:8, 1024:1056], in0=num, in1=rds, op=ALU.mult),
                 reads=["RDS"], writes=[("PS", NUM), ("CAT", 0, 8), ("CAT", 1, 8)])

    def out_proj(self):
        P = self.P
        for cbp in range(4):
            rhs_aps, wks = self.w_get_pair()
            for t, (r0, npt) in enumerate(TILES):
                bank = self.nxt("PQ", 6)
                ps = self.PS[bank]
                for kc in range(16):
                    P.op("pe", lambda e, ps=ps, rhs=rhs_aps[kc], kc=kc, r0=r0, npt=npt: e.matmul(
                        ps[0:npt, :].rearrange("p (a b) -> p a b", b=256), lhsT=self.CAT[:, kc, r0:r0 + npt], rhs=rhs,
                        start=(kc == 0), stop=(kc == 15)),
                        reads=wks + [("CAT", kc // 4, t)], writes=[("PS", bank)])
                H = self.H[t]
                P.op("dve", lambda e, ps=ps, H=H, cbp=cbp, npt=npt: e.tensor_tensor(
                    out=H[0:npt, cbp * 512:(cbp + 1) * 512], in0=ps[0:npt, :], in1=H[0:npt, cbp * 512:(cbp + 1) * 512], op=ALU.add),
                    writes=[("PS", bank), ("H", t)])
            self.w_release(2)

    def mixer(self):
        upto = getattr(self, "upto", None)
        steps = [("norm", lambda: (self.rmsnorm_to_cat("ln_mix", spill=True), self.fence())),
                 ("setup", self.mixer_setup), ("qk", lambda: (self.proj_qk(), self.exchange_k())),
                 ("v", lambda: (self.proj_v(), self.exchange_v(), self.mixer_setup_pe())), ("glu", lambda: (self.proj_glu(), self.exchange_t())),
                 ("cso", self.conv_state_outputs), ("xchg", self.fence),
                 ("conv", lambda: (self.conv_ln(), self.fence())), ("attn", self.attention),
                 ("sattn", lambda: (self.sample_attention(), self.fence())),
                 ("out", lambda: (self.reload_h(), self.out_proj()))]
        self.alloc_mixer()
        for name, fn in steps:
            fn()
            if upto == name:
                self.w_list = self.w_list[:self.w_issued]
                return

    def build(self):
        if self.stage != "mix":
            self.ffn_plan("ffn1")
        if self.stage == "norm1":
            self.w_list = []
            for kk in ("ffn1_g", "ffn1_u", "ffn1_d"):
                pass
            self.rmsnorm_to_cat("ln_ffn1", load_x=True)
            dbg = self.dram_out("dbg", [128, 16 * NTOK], BF16)
            self.P.op("sp", lambda e: e.dma_start(out=dbg[:, :], in_=self.CAT[:, :, :].rearrange("p a b -> p (a b)")),
                      reads=[("CAT", kq, t) for kq in range(4) for t in range(NT)], dsem="dbg")
        if self.stage == "ffn1":
            self.rmsnorm_to_cat("ln_ffn1", load_x=True)
            self.ffn("ffn1", out_dram=self.y_tok)
        if self.stage == "full":
            self.mixer_plan()
            self.ffn_plan("ffn2")
            self.rmsnorm_to_cat("ln_ffn1", load_x=True)
            self.ffn("ffn1")
            self.mixer()
            self.rmsnorm_to_cat("ln_ffn2")
            self.ffn("ffn2", out_dram=self.y_tok)
        if self.stage == "mix":
            self.w_list = []
            self.mixer_plan()
            for t, (r0, npt) in enumerate(TILES):
                H = self.H[t]
                self.P.op("sp", lambda e, H=H, r0=r0, npt=npt: e.dma_start(out=H[0:npt, :], in_=self.x_tok[r0:r0 + npt, :]),
                          writes=[("H", t)], dsem=("xl", t % 4))
            self.mixer()
            for t, (r0, npt) in enumerate(TILES):
                H = self.H[t]
                self.P.op("sp", lambda e, H=H, r0=r0, npt=npt: e.dma_start(out=self.y_tok[r0:r0 + npt, :], in_=H[0:npt, :]),
                          reads=[("H", t)], dsem=("yo", t % 4))
        self.P.finalize()
        self.P.emit()
        return self.nc


def _ident_bf():
    return np.eye(128, dtype=np.float32).astype(ml_dtypes.bfloat16)


def _mult(delta):
    d = np.asarray(delta)
    c = ((d >= 0) & (d <= 128)).astype(np.float32)
    c += ((d >= 0) & (d <= 512) & (d % 4 == 0))
    c += ((d >= 0) & (d <= 2048) & (d % 16 == 0))
    return c


def _const_tables(half):
    bf = ml_dtypes.bfloat16
    t = {}
    t["ident_bf"] = _ident_bf()
    t["ident_f"] = np.eye(128, dtype=np.float32)
    pos = np.zeros((128, NT), np.float32)
    for tt in range(8):
        pos[:, tt] = half * 1024 + tt * 128 + np.arange(128)
    pos[:32, 8] = 16384 + (np.arange(32) % 8)
    inv = (10000.0 ** (-np.arange(32, dtype=np.float32) / 32)).astype(np.float32)
    ang = (pos[:, :, None] * inv[None, None, :]).astype(np.float32)
    t["cos_t"] = np.cos(ang.astype(np.float64)).astype(np.float32).reshape(128, NT * 32)
    t["sin_t"] = np.sin(ang.astype(np.float64)).astype(np.float32).reshape(128, NT * 32)
    k = np.arange(128)[:, None, None]
    q = np.arange(128)[None, None, :]
    d = (np.arange(19) - 3)[None, :, None]
    mg = _mult(d * 128 + q - k)
    t["maskg"] = mg.astype(bf).reshape(128, 19 * 128)
    t["maskc"] = (mg * float(half)).astype(bf).reshape(128, 19 * 128)
    p = np.arange(128)[:, None, None, None, None]
    b = np.arange(4)[None, :, None, None, None]
    sidx = np.arange(12)[None, None, :, None, None]
    b2 = np.arange(4)[None, None, None, :, None]
    tq = np.arange(8)[None, None, None, None, :]
    m_p3 = ((tq == sidx) & (sidx < 8)).astype(np.float32) * np.ones_like(p, dtype=np.float32)
    dlt = 2048 + tq - (128 * (sidx + 4) + p)
    m_rc = (((dlt >= 0) & (dlt <= 128)).astype(np.float32) + ((dlt >= 0) & (dlt <= 512) & (dlt % 4 == 0))) * (sidx >= 8)
    ms = (m_p3 + m_rc) * (b2 == b)
    t["masks"] = ms.astype(bf).reshape(128, 48 * 32)
    kb = (np.arange(32) // 8)[:, None]
    kt = (np.arange(32) % 8)[:, None]
    qb = (np.arange(32) // 8)[None, :]
    qt = (np.arange(32) % 8)[None, :]
    t["maskn"] = (_mult(qt - kt) * (kb == qb)).astype(bf)
    t["flag"] = np.full((128, 1), float(half), np.float32)
    return t


def make_in_maps(inp, ncores=8, stage="full"):
    f32 = lambda a: np.ascontiguousarray(np.asarray(a, dtype=np.float32))
    maps = []
    shared = {}
    if stage == "full":
        for f in ("ffn1", "ffn2"):
            for n in ("gate", "up", "down"):
                shared["%s_w_%s" % (f, n)] = f32(inp["%s_w_%s" % (f, n)][0])
        for k in ("ln_ffn1", "ln_mix", "ln_ffn2"):
            shared[k] = f32(inp[k][0]).reshape(1, D)
    else:
        shared["ln_mix"] = f32(inp["ln_mix"][0]).reshape(1, D)
    shared["w_in"] = f32(inp["w_in"][0])
    shared["w_out"] = f32(inp["w_out"][0])
    shared["q_norm"] = f32(inp["q_norm"][0]).reshape(1, 64)
    shared["k_norm"] = f32(inp["k_norm"][0]).reshape(1, 64)
    shared["conv_dw_w"] = f32(inp["conv_dw_w"][0])
    shared["cpar"] = np.concatenate([f32(inp["conv_dw_b"][0]).reshape(8, 128), f32(inp["conv_ln_g"][0]).reshape(8, 128),
                                     f32(inp["conv_ln_b"][0]).reshape(8, 128)], axis=0)
    tabs = [_const_tables(0), _const_tables(1)]
    for c in range(ncores):
        b, half = c // 2, c % 2
        m = dict(shared)
        m.update(tabs[half])
        xs = f32(inp["x_sample"][4 * c:4 * c + 4]).reshape(32, D)
        m["x_tok"] = np.concatenate([f32(inp["x_prompt"][b, half * 1024:(half + 1) * 1024]), xs], axis=0)
        m["cache_k"] = f32(inp["cache_k"][0, 4 * c:4 * c + 4]).reshape(4, 2048, 1024)
        m["cache_v"] = f32(inp["cache_v"][0, 4 * c:4 * c + 4]).reshape(4, 2048, 1024)
        m["state_conv"] = f32(inp["state_conv"][0, 4 * c:4 * c + 4]).reshape(120, 1024)
        maps.append(m)
    return maps


def assemble(results, ncores=8):
    nb = ncores // 2
    y_p = np.zeros((nb, 2048, D), np.float32)
    y_s = np.zeros((4 * ncores, 8, D), np.float32)
    k_p = np.zeros((1, nb, 2048, 16, 64), np.float32)
    v_p = np.zeros((1, nb, 2048, 16, 64), np.float32)
    c_p = np.zeros((1, nb, 30, 1024), np.float32)
    k_s = np.zeros((1, 4 * ncores, 8, 16, 64), np.float32)
    v_s = np.zeros((1, 4 * ncores, 8, 16, 64), np.float32)
    c_s = np.zeros((1, 4 * ncores, 30, 1024), np.float32)
    for c in range(ncores):
        r = results[c]
        b, half = c // 2, c % 2
        sl = slice(half * 1024, (half + 1) * 1024)
        y = np.asarray(r["y_tok"])
        y_p[b, sl] = y[:1024]
        y_s[4 * c:4 * c + 4] = y[1024:].reshape(4, 8, D)
        nk = np.asarray(r["newk"])
        nv = np.asarray(r["newv"])
        k_p[0, b, sl] = nk[:1024].reshape(1024, 16, 64)
        v_p[0, b, sl] = nv[:1024].reshape(1024, 16, 64)
        k_s[0, 4 * c:4 * c + 4] = nk[1024:].reshape(4, 8, 16, 64)
        v_s[0, 4 * c:4 * c + 4] = nv[1024:].reshape(4, 8, 16, 64)
        if half == 1:
            c_p[0, b] = np.asarray(r["conv_p"])
        c_s[0, 4 * c:4 * c + 4] = np.asarray(r["conv_s"]).reshape(4, 30, 1024)
    return (y_p, y_s, k_p, v_p, c_p, k_s, v_s, c_s)


def kernel(**inputs):
    nc = Builder(stage="full", ncores=8).build()
    maps = make_in_maps(inputs, 8, "full")
    res = run_bass_kernel_spmd(nc, maps, core_ids=list(range(8)))
    return assemble(res.results, 8)
```
